# Optimizing a Trainium2 kernel written in Bass

```python
import jax, jax.numpy as jnp
from jax import lax
import numpy as np

D_MODEL = 1024
BATCH = 4
SEQ = 8192
DEPTH = 1

SSM_WIDTH = D_MODEL // 2
SSM_GROUP = 16
SSM_GROUPS = SSM_WIDTH // SSM_GROUP
SSM_STATE = 64
NSA_HEADS = 8
NSA_KV_GROUPS = 2
HEADS_PER_GROUP = NSA_HEADS // NSA_KV_GROUPS
HEAD_DIM = 64
NSA_WIDTH = NSA_HEADS * HEAD_DIM
CMP_STRIDE = 16
CMP_BLOCK = 2 * CMP_STRIDE
CMP_HIDDEN = 2 * HEAD_DIM
SEL_BLOCK = 64
SEL_TOP_K = 16
WINDOW = 512
Q_BLOCK = 128
D_FF = 4 * D_MODEL
N_MIXERS = 2
KV_WIDTH = 3 * 2 * NSA_KV_GROUPS * HEAD_DIM
IN_WIDTH = SSM_WIDTH + NSA_WIDTH + KV_WIDTH + 3 * NSA_HEADS + N_MIXERS * D_MODEL
EPS = 1e-6
NEG_INF = -1e30
FORCE_BONUS = 1e4

kernel_name = "hybrid_s5_nsa_gated_block"


def rmsnorm(x, g):
    xf = x.astype(jnp.float32)
    y = xf * lax.rsqrt(jnp.mean(xf * xf, axis=-1, keepdims=True) + EPS)
    return (y * g.astype(jnp.float32)).astype(x.dtype)


def masked_softmax(s, mask):
    s = jnp.where(mask, s.astype(jnp.float32), NEG_INF)
    p = jax.nn.softmax(s, axis=-1)
    return jnp.where(mask, p, 0.0)


def alibi_slopes():
    h = jnp.arange(1, NSA_HEADS + 1, dtype=jnp.float32)
    return jnp.exp2(-8.0 * h / NSA_HEADS).reshape(NSA_KV_GROUPS, HEADS_PER_GROUP, 1, 1)


def s5_mixer(u, a_re, a_im, log_dt, b_re, b_im, c_re, c_im, d_skip, w_glu, b_glu):
    bsz, s, _ = u.shape
    f32 = jnp.float32
    ug = u.astype(f32).reshape(bsz, s, SSM_GROUPS, SSM_GROUP)
    ar = a_re.astype(f32)
    ai = a_im.astype(f32)
    dt = jnp.exp(log_dt.astype(f32))[:, None]
    mag = jnp.exp(ar * dt)
    abar_re = mag * jnp.cos(ai * dt)
    abar_im = mag * jnp.sin(ai * dt)
    den = ar * ar + ai * ai
    nr = abar_re - 1.0
    fr = (nr * ar + abar_im * ai) / den
    fi = (abar_im * ar - nr * ai) / den
    br = b_re.astype(f32)
    bi = b_im.astype(f32)
    bbar_re = fr[..., None] * br - fi[..., None] * bi
    bbar_im = fr[..., None] * bi + fi[..., None] * br
    bu_re = jnp.einsum('bsgc,gpc->bsgp', ug, bbar_re)
    bu_im = jnp.einsum('bsgc,gpc->bsgp', ug, bbar_im)
    a_seq_re = jnp.broadcast_to(abar_re, (1, s, SSM_GROUPS, SSM_STATE))
    a_seq_im = jnp.broadcast_to(abar_im, (1, s, SSM_GROUPS, SSM_STATE))

    def combine(left, right):
        a1r, a1i, b1r, b1i = left
        a2r, a2i, b2r, b2i = right
        return (a2r * a1r - a2i * a1i, a2r * a1i + a2i * a1r,
                a2r * b1r - a2i * b1i + b2r, a2r * b1i + a2i * b1r + b2i)

    _, _, xr, xi = lax.associative_scan(combine, (a_seq_re, a_seq_im, bu_re, bu_im), axis=1)
    y = (jnp.einsum('bsgp,gcp->bsgc', xr, c_re.astype(f32))
         - jnp.einsum('bsgp,gcp->bsgc', xi, c_im.astype(f32))
         + d_skip.astype(f32).reshape(SSM_GROUPS, SSM_GROUP) * ug).reshape(bsz, s, SSM_WIDTH)
    yg = jax.nn.gelu(y)
    out = yg * jax.nn.sigmoid(yg @ w_glu.astype(f32) + b_glu.astype(f32))
    return out.astype(u.dtype)


def compress_blocks(tok, pe, w1, w2):
    bsz, g, s, d = tok.shape
    chunks = tok.reshape(bsz, g, s // CMP_STRIDE, CMP_STRIDE, d)
    blocks = jnp.concatenate([chunks[:, :, :-1], chunks[:, :, 1:]], axis=3) + pe
    flat = blocks.reshape(bsz, g, s // CMP_STRIDE - 1, CMP_BLOCK * d)
    return jax.nn.gelu(flat @ w1) @ w2


def nsa_mixer(q, kv, gate_logits, cmp_pe_k, cmp_pe_v, cmp_wk1, cmp_wk2, cmp_wv1, cmp_wv2,
              q_norm_g, k_norm_g):
    bsz, s, _ = q.shape
    G, HG, d = NSA_KV_GROUPS, HEADS_PER_GROUP, HEAD_DIM
    f32 = jnp.float32
    q = rmsnorm(q.reshape(bsz, s, G, HG, d), q_norm_g).transpose(0, 2, 3, 1, 4)
    kv = kv.reshape(bsz, s, 3, 2, G, d).transpose(2, 3, 0, 4, 1, 5)
    kc = rmsnorm(compress_blocks(kv[0, 0], cmp_pe_k, cmp_wk1, cmp_wk2), k_norm_g[0])
    vc = compress_blocks(kv[0, 1], cmp_pe_v, cmp_wv1, cmp_wv2)
    n_cmp = kc.shape[2]
    n_sel = s // SEL_BLOCK
    k_top = min(SEL_TOP_K, n_sel)
    ks = rmsnorm(kv[1, 0], k_norm_g[1]).reshape(bsz, G, n_sel, SEL_BLOCK, d)
    vs = kv[1, 1].reshape(bsz, G, n_sel, SEL_BLOCK, d)
    pad = ((0, 0), (0, 0), (WINDOW, 0), (0, 0))
    kw = jnp.pad(rmsnorm(kv[2, 0], k_norm_g[2]), pad)
    vw = jnp.pad(kv[2, 1], pad)
    gates = jax.nn.sigmoid(gate_logits.reshape(bsz, s, 3, G, HG).astype(f32))
    gates = gates.astype(q.dtype).transpose(0, 3, 4, 1, 2)
    slopes = alibi_slopes()
    scale = HEAD_DIM ** -0.5
    c_end = (jnp.arange(n_cmp) * CMP_STRIDE + CMP_BLOCK - 1).astype(f32)
    blk_ids = jnp.arange(n_sel)
    bi = jnp.arange(bsz)[:, None, None, None]
    gi = jnp.arange(G)[None, :, None, None]
    sel_off = jnp.arange(SEL_BLOCK)
    win_off = jnp.arange(Q_BLOCK + WINDOW)

    def query_block(i):
        qs = i * Q_BLOCK
        qb = lax.dynamic_slice_in_dim(q, qs, Q_BLOCK, axis=3)
        gb = lax.dynamic_slice_in_dim(gates, qs, Q_BLOCK, axis=3)
        t = qs + jnp.arange(Q_BLOCK)
        tf = t.astype(f32)
        dist_c = tf[:, None] - c_end[None, :]
        sc = jnp.einsum('bghqd,bgcd->bghqc', qb, kc).astype(f32) * scale - slopes * dist_c
        pc = masked_softmax(sc, dist_c >= 0)
        o_cmp = jnp.einsum('bghqc,bgcd->bghqd', pc.astype(vc.dtype), vc)
        imp = jnp.pad(pc.sum(axis=2), ((0, 0), (0, 0), (0, 0), (0, 1)))
        chunks = imp.reshape(bsz, G, Q_BLOCK, n_sel, SEL_BLOCK // CMP_STRIDE)
        prev = jnp.pad(chunks[..., :-1, -1], ((0, 0), (0, 0), (0, 0), (1, 0)))
        imp_sel = chunks.sum(axis=-1) + prev
        cur = t // SEL_BLOCK
        valid = blk_ids[None, :] <= cur[:, None]
        forced = ((blk_ids[None, :] == 0) | (blk_ids[None, :] == cur[:, None])
                  | (blk_ids[None, :] == cur[:, None] - 1))
        imp_sel = jnp.where(valid, imp_sel + jnp.where(forced, FORCE_BONUS, 0.0), NEG_INF)
        _, idx = lax.top_k(imp_sel, k_top)
        k_g = ks[bi, gi, idx].reshape(bsz, G, Q_BLOCK, k_top * SEL_BLOCK, d)
        v_g = vs[bi, gi, idx].reshape(bsz, G, Q_BLOCK, k_top * SEL_BLOCK, d)
        pos = (idx[..., None] * SEL_BLOCK + sel_off).reshape(bsz, G, Q_BLOCK, k_top * SEL_BLOCK)
        dist_s = (t[None, None, :, None] - pos).astype(f32)[:, :, None]
        ss = jnp.einsum('bghqd,bgqkd->bghqk', qb, k_g).astype(f32) * scale - slopes * dist_s
        ps = masked_softmax(ss, dist_s >= 0)
        o_slc = jnp.einsum('bghqk,bgqkd->bghqd', ps.astype(v_g.dtype), v_g)
        kwb = lax.dynamic_slice_in_dim(kw, qs, Q_BLOCK + WINDOW, axis=2)
        vwb = lax.dynamic_slice_in_dim(vw, qs, Q_BLOCK + WINDOW, axis=2)
        s_pos = qs - WINDOW + win_off
        dist_w = t[:, None] - s_pos[None, :]
        mask_w = (dist_w >= 0) & (dist_w < WINDOW) & (s_pos[None, :] >= 0)
        sw = jnp.einsum('bghqd,bgkd->bghqk', qb, kwb).astype(f32) * scale - slopes * dist_w.astype(f32)
        pw = masked_softmax(sw, mask_w)
        o_win = jnp.einsum('bghqk,bgkd->bghqd', pw.astype(vwb.dtype), vwb)
        return gb[..., 0:1] * o_cmp + gb[..., 1:2] * o_slc + gb[..., 2:3] * o_win

    out = lax.map(query_block, jnp.arange(s // Q_BLOCK))
    return out.transpose(1, 0, 4, 2, 3, 5).reshape(bsz, s, NSA_WIDTH)


def setup_inputs(seed: int = 0) -> dict:
    key = jax.random.key(seed)
    ks = jax.random.split(key, 32)
    L = DEPTH
    f32 = jnp.float32

    def nrm(k, shape, scale):
        return jax.random.normal(k, shape, f32) * scale

    return {
        'x': nrm(ks[0], (BATCH, SEQ, D_MODEL), 1.0),
        'norm1_g': 1.0 + nrm(ks[1], (L, D_MODEL), 0.02),
        'w_in': nrm(ks[2], (L, D_MODEL, IN_WIDTH), D_MODEL ** -0.5),
        'ssm_a_re': -0.5 + nrm(ks[3], (L, SSM_GROUPS, SSM_STATE), 0.01),
        'ssm_a_im': np.pi * jnp.arange(SSM_STATE, dtype=f32) + nrm(ks[4], (L, SSM_GROUPS, SSM_STATE), 0.01),
        'ssm_log_dt': jax.random.uniform(ks[5], (L, SSM_GROUPS), f32, np.log(1e-3), np.log(1e-1)),
        'ssm_b_re': nrm(ks[6], (L, SSM_GROUPS, SSM_STATE, SSM_GROUP), (2 * SSM_GROUP) ** -0.5),
        'ssm_b_im': nrm(ks[7], (L, SSM_GROUPS, SSM_STATE, SSM_GROUP), (2 * SSM_GROUP) ** -0.5),
        'ssm_c_re': nrm(ks[8], (L, SSM_GROUPS, SSM_GROUP, SSM_STATE), SSM_STATE ** -0.5),
        'ssm_c_im': nrm(ks[9], (L, SSM_GROUPS, SSM_GROUP, SSM_STATE), SSM_STATE ** -0.5),
        'ssm_d': nrm(ks[10], (L, SSM_WIDTH), 1.0),
        'ssm_w_glu': nrm(ks[11], (L, SSM_WIDTH, SSM_WIDTH), SSM_WIDTH ** -0.5),
        'ssm_b_glu': nrm(ks[12], (L, SSM_WIDTH), 0.01),
        'cmp_pe_k': nrm(ks[13], (L, CMP_BLOCK, HEAD_DIM), 0.1),
        'cmp_pe_v': nrm(ks[14], (L, CMP_BLOCK, HEAD_DIM), 0.1),
        'cmp_wk1': nrm(ks[15], (L, CMP_BLOCK * HEAD_DIM, CMP_HIDDEN), (CMP_BLOCK * HEAD_DIM) ** -0.5),
        'cmp_wk2': nrm(ks[16], (L, CMP_HIDDEN, HEAD_DIM), CMP_HIDDEN ** -0.5),
        'cmp_wv1': nrm(ks[17], (L, CMP_BLOCK * HEAD_DIM, CMP_HIDDEN), (CMP_BLOCK * HEAD_DIM) ** -0.5),
        'cmp_wv2': nrm(ks[18], (L, CMP_HIDDEN, HEAD_DIM), CMP_HIDDEN ** -0.5),
        'q_norm_g': 1.0 + nrm(ks[19], (L, HEAD_DIM), 0.02),
        'k_norm_g': 1.0 + nrm(ks[20], (L, 3, HEAD_DIM), 0.02),
        'w_proj_ssm': nrm(ks[21], (L, SSM_WIDTH, D_MODEL), SSM_WIDTH ** -0.5),
        'w_proj_nsa': nrm(ks[22], (L, NSA_WIDTH, D_MODEL), NSA_WIDTH ** -0.5),
        'w_out': nrm(ks[23], (L, D_MODEL, D_MODEL), D_MODEL ** -0.5),
        'norm2_g': 1.0 + nrm(ks[24], (L, D_MODEL), 0.02),
        'w_up': nrm(ks[25], (L, D_MODEL, D_FF), D_MODEL ** -0.5),
        'w_down': nrm(ks[26], (L, D_FF, D_MODEL), D_FF ** -0.5),
    }


def reference(x, norm1_g, w_in, ssm_a_re, ssm_a_im, ssm_log_dt, ssm_b_re, ssm_b_im, ssm_c_re, ssm_c_im,
              ssm_d, ssm_w_glu, ssm_b_glu, cmp_pe_k, cmp_pe_v, cmp_wk1, cmp_wk2, cmp_wv1, cmp_wv2,
              q_norm_g, k_norm_g, w_proj_ssm, w_proj_nsa, w_out, norm2_g, w_up, w_down):
    bsz, s, _ = x.shape
    o1 = SSM_WIDTH
    o2 = o1 + NSA_WIDTH
    o3 = o2 + KV_WIDTH
    o4 = o3 + 3 * NSA_HEADS
    for l in range(DEPTH):
        h = rmsnorm(x, norm1_g[l])
        hp = h @ w_in[l]
        y_ssm = s5_mixer(hp[..., :o1], ssm_a_re[l], ssm_a_im[l], ssm_log_dt[l], ssm_b_re[l], ssm_b_im[l],
                         ssm_c_re[l], ssm_c_im[l], ssm_d[l], ssm_w_glu[l], ssm_b_glu[l])
        y_nsa = nsa_mixer(hp[..., o1:o2], hp[..., o2:o3], hp[..., o3:o4], cmp_pe_k[l], cmp_pe_v[l],
                          cmp_wk1[l], cmp_wk2[l], cmp_wv1[l], cmp_wv2[l], q_norm_g[l], k_norm_g[l])
        gate = jax.nn.sigmoid(hp[..., o4:].reshape(bsz, s, N_MIXERS, D_MODEL))
        merged = gate[..., 0, :] * (y_ssm @ w_proj_ssm[l]) + gate[..., 1, :] * (y_nsa @ w_proj_nsa[l])
        x = x + merged @ w_out[l]
        h2 = rmsnorm(x, norm2_g[l])
        x = x + jnp.square(jax.nn.relu(h2 @ w_up[l])) @ w_down[l]
    return x
```

```python
import os
import numpy as np
import ml_dtypes
from contextlib import ExitStack
import concourse.bass as bass
import concourse.mybir as mybir
from concourse.bass_utils import run_bass_kernel_spmd

F32 = mybir.dt.float32
BF16 = mybir.dt.bfloat16
I32 = mybir.dt.int32
AF = mybir.ActivationFunctionType
ALU = mybir.AluOpType
AX = mybir.AxisListType
SAME_ENGINE_SYNC = True
NEG = -30000.0
EPS = 1e-6
D = 1024
IN_W = 3864
O1, O2, O3, O4 = 512, 1024, 1792, 1816
BF = ml_dtypes.bfloat16


class TS:
    def __init__(self, nc, es):
        self.nc = nc
        self.es = es
        self.E = dict(pe=nc.tensor, act=nc.scalar, dve=nc.vector, pool=nc.gpsimd, sp=nc.sync)
        self.sem = {k: es.enter_context(nc.semaphore("s_" + k)) for k in self.E}
        self.cnt = {k: 0 for k in self.E}
        self.waited = {k: {} for k in self.E}
        self.lastw = {}
        self.readers = {}
        self.nins = 0

    def _need(self, e, reads, writes):
        deps = {}
        for k in reads:
            t = self.lastw.get(k)
            if t is not None and deps.get(t[0], 0) < t[1]:
                deps[t[0]] = t[1]
        for k in writes:
            t = self.lastw.get(k)
            if t is not None and deps.get(t[0], 0) < t[1]:
                deps[t[0]] = t[1]
            for s, v in self.readers.get(k, {}).items():
                if deps.get(s, 0) < v:
                    deps[s] = v
        for s, v in deps.items():
            if s == e and (e == 'pe' or not SAME_ENGINE_SYNC):
                continue
            if self.waited[e].get(s, 0) >= v:
                continue
            self.E[e].wait_ge(self.sem[s], v)
            self.waited[e][s] = v

    def _record(self, tok, reads, writes):
        s, v = tok
        for k in writes:
            self.lastw[k] = tok
            self.readers[k] = {}
        for k in reads:
            d = self.readers.setdefault(k, {})
            if d.get(s, 0) < v:
                d[s] = v

    def op(self, e, fn, r=(), w=()):
        self._need(e, r, w)
        ins = fn(self.E[e])
        self.cnt[e] += 1
        ins.then_inc(self.sem[e], 1)
        self._record((e, self.cnt[e]), r, w)
        self.nins += 1

    def dma(self, q, out, in_, r=(), w=(), slot=None):
        self._need(q, r, w)
        name = 'd_' + slot
        if name not in self.sem:
            self.sem[name] = self.es.enter_context(self.nc.semaphore("s_" + name))
            self.cnt[name] = 0
        if self.cnt[name] > 0 and self.waited[q].get(name, 0) < self.cnt[name]:
            self.E[q].wait_ge(self.sem[name], self.cnt[name])
            self.waited[q][name] = self.cnt[name]
        ins = self.E[q].dma_start(out=out, in_=in_)
        self.cnt[name] += 16
        ins.then_inc(self.sem[name], 16)
        self._record((name, self.cnt[name]), r, w)
        self.nins += 1

    def barrier(self):
        for e in self.E:
            for s, h in self.sem.items():
                if self.cnt[s] > 0 and self.waited[e].get(s, 0) < self.cnt[s]:
                    self.E[e].wait_ge(h, self.cnt[s])
                    self.waited[e][s] = self.cnt[s]

    def finish(self):
        for s, h in self.sem.items():
            if self.cnt[s] > 0 and self.waited['sp'].get(s, 0) < self.cnt[s]:
                self.E['sp'].wait_ge(h, self.cnt[s])


def host_consts(SEQ):
    NT = SEQ // 128
    nblk = SEQ // 16 - 1
    NCT = (nblk + 127) // 128
    n_sel = SEQ // 64
    c = {}
    c['identf'] = np.eye(128, dtype=np.float32)
    i4 = np.tile(np.eye(128, dtype=np.float32), (1, 4))
    c['i4'] = i4.astype(BF)
    qi = np.arange(128)[:, None]
    ki = np.arange(128)[None, :]
    c['negtri'] = np.where(ki > qi, NEG, 0.0).astype(BF)
    c['negtric'] = np.where(ki <= qi, NEG, 0.0).astype(BF)
    keys = np.arange(SEQ)
    c['eall'] = (keys[None, :] // 64 == np.arange(128)[:, None]).astype(np.float32).astype(BF)
    A = np.zeros((128, NCT, 128), np.float32)
    for cc in range(NCT * 128):
        for n in ((cc + 1) // 4, cc // 4):
            if 4 * n - 1 <= cc <= 4 * n + 3 and n < n_sel and n != 127:
                A[cc % 128, cc // 128, n] = 1.0
    A[:, :, 127] = 1.0
    c['aaug'] = A.reshape(128, NCT * 128).astype(BF)
    m = np.arange(-127, 145)[None, :]
    f = ((np.arange(128) + 1) // 16)[:, None]
    c['wc'] = np.where(m > f, NEG, 0.0).astype(BF)
    t = np.arange(SEQ)
    cur = t // 64
    n = np.arange(128)[None, :]
    fb = np.zeros((SEQ, 128), np.float32)
    forced = (n == 0) | (n == cur[:, None]) | (n == cur[:, None] - 1)
    fb[forced] = 1e4
    fb[(n > cur[:, None]) | (n >= n_sel)] = -1e30
    c['fb'] = fb
    kpos = np.stack([64 * (t // 64), t % 64, np.ones_like(t), np.ones_like(t)]).astype(np.float32)
    c['kpos'] = kpos.astype(BF)
    ce = 16 * np.arange(NCT * 128) + 31
    c['cpos'] = np.stack([64 * (ce // 64), ce % 64, np.ones_like(ce), np.ones_like(ce)]).astype(np.float32).astype(BF)
    slopes = 2.0 ** -(np.arange(8) + 1.0)
    qpos = np.zeros((4, 8, SEQ), np.float32)
    qpos[0] = slopes[:, None]
    qpos[1] = slopes[:, None]
    qpos[2] = -slopes[:, None] * (64 * (t // 64))[None, :]
    qpos[3] = -slopes[:, None] * (t % 64)[None, :]
    c['qpos'] = qpos.astype(BF)
    c['tp1'] = np.tile(np.arange(1, 129, dtype=np.float32)[None, :], (128, 1))
    c['ones64'] = np.full((64, 64), 1.0 / 64, np.float32)
    return c


def host_layouts(inp):
    L = {}
    f = lambda a: np.ascontiguousarray(np.asarray(a, dtype=np.float32))
    L['g1'] = f(inp['norm1_g'][0].reshape(8, 128).T)
    L['g2'] = f(inp['norm2_g'][0].reshape(8, 128).T)
    L['w_in'] = f(inp['w_in'][0])
    L['w_glu'] = f(inp['ssm_w_glu'][0])
    L['w_ps'] = f(inp['w_proj_ssm'][0])
    L['w_pn'] = f(inp['w_proj_nsa'][0])
    L['w_out'] = f(inp['w_out'][0])
    L['w_up'] = f(inp['w_up'][0])
    L['w_down'] = f(inp['w_down'][0])
    L['gq'] = f(inp['q_norm_g'][0].reshape(64, 1))
    L['gk'] = f(inp['k_norm_g'][0].T)
    for nm, key in (('w1k', 'cmp_wk1'), ('w1v', 'cmp_wv1')):
        w = np.asarray(inp[key][0]).reshape(32, 64, 128).transpose(1, 0, 2)
        L[nm] = f(np.concatenate([w, w], 0).reshape(128, 32 * 128))
    L['w2k'] = f(inp['cmp_wk2'][0])
    L['w2v'] = f(inp['cmp_wv2'][0])
    L['pek'] = f(np.asarray(inp['cmp_pe_k'][0]).T)
    L['pev'] = f(np.asarray(inp['cmp_pe_v'][0]).T)

    def sm(a):
        a = np.asarray(a)
        return f(a.reshape(16, 2, 64).transpose(1, 2, 0).reshape(128, 16))
    L['are'] = sm(inp['ssm_a_re'][0])
    L['aim'] = sm(inp['ssm_a_im'][0])
    L['ldt'] = sm(np.repeat(np.asarray(inp['ssm_log_dt'][0])[:, None], 64, 1))

    def pad_b(b):
        b = np.asarray(b)
        o = np.zeros((128, 16, 128), np.float32)
        for g in range(32):
            j, gl = g // 2, g % 2
            o[gl * 64:(gl + 1) * 64, j, (j % 4) * 32 + gl * 16:(j % 4) * 32 + gl * 16 + 16] = b[g]
        return o.reshape(128, 2048)
    L['bre'] = pad_b(inp['ssm_b_re'][0])
    L['bim'] = pad_b(inp['ssm_b_im'][0])
    L['cre'] = pad_b(np.asarray(inp['ssm_c_re'][0]).transpose(0, 2, 1))
    L['cim'] = pad_b(np.asarray(inp['ssm_c_im'][0]).transpose(0, 2, 1))
    L['dsk'] = f(inp['ssm_d'][0].reshape(4, 128).T)
    L['bglu'] = f(inp['ssm_b_glu'][0].reshape(4, 128).T)
    return L


def build(SEQ, dbg=False):
    NT = SEQ // 128
    nblk = SEQ // 16 - 1
    NCT = (nblk + 127) // 128
    nc = bass.Bass("TRN2", target_bir_lowering=False)
    hc = host_consts(SEQ)
    di = {}

    def din(name, shape, dt=F32):
        di[name] = nc.dram_tensor(name, list(shape), dt, kind="ExternalInput").ap()
        return di[name]

    x_d = din('x', [SEQ, D])
    for k, v in hc.items():
        din('c_' + k, v.shape, BF16 if v.dtype == BF else F32)
    shapes = dict(g1=[128, 8], g2=[128, 8], w_in=[D, IN_W], w_glu=[512, 512], w_ps=[512, D], w_pn=[512, D],
                  w_out=[D, D], w_up=[D, 4096], w_down=[4096, D], gq=[64, 1], gk=[64, 3], w1k=[128, 4096],
                  w1v=[128, 4096], w2k=[128, 64], w2v=[128, 64], pek=[64, 32], pev=[64, 32], are=[128, 16],
                  aim=[128, 16], ldt=[128, 16], bre=[128, 2048], bim=[128, 2048], cre=[128, 2048],
                  cim=[128, 2048], dsk=[128, 4], bglu=[128, 4])
    for k, s in shapes.items():
        din(k, s)
    out_d = nc.dram_tensor('out', [SEQ, D], F32, kind="ExternalOutput").ap()
    if dbg:
        yn_t = nc.dram_tensor('yn_scr', [NT, 128, 512], BF16, kind="ExternalOutput").ap()
        ys_t = nc.dram_tensor('ys_scr', [NT, 128, 512], BF16, kind="ExternalOutput").ap()
        x1_d = nc.dram_tensor('x1_scr', [SEQ, D], F32, kind="ExternalOutput").ap()
        yn_d = [yn_t[i] for i in range(NT)]
        ys_d = [ys_t[i] for i in range(NT)]
    else:
        out_bf = out_d.bitcast(BF16)
        yn_d = [out_bf[i * 128:(i + 1) * 128, 0:512] for i in range(NT)]
        ys_d = [out_bf[i * 128:(i + 1) * 128, 512:1024] for i in range(NT)]
        x1_d = out_d

    with ExitStack() as es:
        T = TS(nc, es)

        def sbt(st, n, s, d=F32):
            return st.enter_context(nc.sbuf_tensor('sb_' + n, list(s), d))

        def pst(st, n, s, d=F32):
            return st.enter_context(nc.psum_tensor('ps_' + n, list(s), d))

        identf = sbt(es, 'identf', [128, 128])
        identb = sbt(es, 'identb', [128, 128], BF16)
        i4 = sbt(es, 'i4', [128, 512], BF16)
        negtri = sbt(es, 'negtri', [128, 128], BF16)
        negtric = sbt(es, 'negtric', [128, 128], BF16)
        epsb = sbt(es, 'epsb', [128, 1])
        g1t = sbt(es, 'g1t', [128, 8])
        g2t = sbt(es, 'g2t', [128, 8])
        xts = [sbt(es, f'xt{s}', [128, D]) for s in range(2)]
        hbs = [sbt(es, f'hb{s}', [128, D], BF16) for s in range(2)]
        hTs = [sbt(es, f'hT{s}', [128, D], BF16) for s in range(2)]
        junk = sbt(es, 'junk', [128, D], BF16)
        ssqs = [sbt(es, f'ssq{s}', [128, 1]) for s in range(2)]
        rstds = [sbt(es, f'rstd{s}', [128, 1]) for s in range(2)]
        stg = [sbt(es, f'stg{s}', [128, 1024]) for s in range(2)]
        tp = pst(es, 'tp', [128, 1024], BF16)
        wstate = {'slot': 0}

        T.dma('sp', identf[:], di['c_identf'], w=['identf'], slot='c0')
        T.dma('sp', i4[:], di['c_i4'], w=['i4'], slot='c1')
        T.dma('sp', negtri[:], di['c_negtri'], w=['negtri'], slot='c2')
        T.dma('sp', negtric[:], di['c_negtric'], w=['negtric'], slot='c3')
        T.dma('sp', g1t[:], di['g1'], w=['g1t'], slot='c4')
        T.dma('sp', g2t[:], di['g2'], w=['g2t'], slot='c5')
        T.op('dve', lambda e: e.memset(epsb[:], EPS), w=['epsb'])
        T.op('dve', lambda e: e.tensor_copy(out=identb[:], in_=identf[:]), r=['identf'], w=['identb'])

        def load_cast(dst_ap, src_ap, n, key, scale=None, parts=128):
            s = wstate['slot']
            wstate['slot'] ^= 1
            T.dma('sp', stg[s][0:parts, 0:n], src_ap, w=[f'stg{s}'], slot=f'stg{s}')
            if scale is None:
                T.op('act', lambda e: e.activation(out=dst_ap, in_=stg[s][0:parts, 0:n], func=AF.Copy), r=[f'stg{s}'], w=[key])
            else:
                T.op('act', lambda e: e.activation(out=dst_ap, in_=stg[s][0:parts, 0:n], func=AF.Copy, scale=scale),
                     r=[f'stg{s}', 'g1t', 'g2t'], w=[key])

        def load_w(dst, src, rows, cols, name, col0=0, gt=None):
            keys = []
            for c in range(rows // 128):
                for a in range(0, cols, 1024):
                    n = min(1024, cols - a)
                    k = f'{name}_{c}_{a}'
                    load_cast(dst[:, c, a:a + n], src[c * 128:(c + 1) * 128, col0 + a:col0 + a + n], n, k,
                              scale=None if gt is None else gt[:, c:c + 1])
                    keys.append(k)
            return keys

        def rms_hT(i, src_d, keysrc=()):
            s = i % 2
            xt, hb, hT = xts[s], hbs[s], hTs[s]
            T.dma('sp', xt[:], src_d[i * 128:(i + 1) * 128, :], r=list(keysrc), w=[f'xt{s}'], slot=f'xt{s}')
            T.op('act', lambda e: e.activation(out=junk[:], in_=xt[:], func=AF.Square, accum_out=ssqs[s][:]),
                 r=[f'xt{s}'], w=['junk', f'ssq{s}'])
            T.op('act', lambda e: e.activation(out=rstds[s][:], in_=ssqs[s][:], func=AF.Sqrt, bias=epsb[:], scale=1.0 / D),
                 r=[f'ssq{s}', 'epsb'], w=[f'rstd{s}'])
            T.op('dve', lambda e: e.reciprocal(out=rstds[s][:], in_=rstds[s][:]), r=[f'rstd{s}'], w=[f'rstd{s}'])
            T.op('act', lambda e: e.activation(out=hb[:], in_=xt[:], func=AF.Copy, scale=rstds[s][:]),
                 r=[f'xt{s}', f'rstd{s}'], w=[f'hb{s}'])
            for c in range(8):
                T.op('pe', lambda e: e.transpose(tp[:, c * 128:(c + 1) * 128], hb[:, c * 128:(c + 1) * 128], identb[:]),
                     r=[f'hb{s}', 'identb'], w=['tp'])
            T.op('dve', lambda e: e.tensor_copy(out=hT[:], in_=tp[:]), r=['tp'], w=[f'hT{s}'])
            return xt, hT, s

        with ExitStack() as ea:
            KselT = [sbt(ea, f'KselT{g}', [68, SEQ], BF16) for g in range(2)]
            KwinT = [sbt(ea, f'KwinT{g}', [68, SEQ], BF16) for g in range(2)]
            Vsel = sbt(ea, 'Vsel', [128, NT * 130], BF16)
            Vwin = sbt(ea, 'Vwin', [128, NT * 130], BF16)
            KcT = [sbt(ea, f'KcT{g}', [68, NCT * 128], BF16) for g in range(2)]
            Vc = sbt(ea, 'Vc', [128, NCT * 130], BF16)
            gqt = sbt(ea, 'gqt', [64, 1])
            gq8 = sbt(ea, 'gq8', [64, 1])
            gkt = sbt(ea, 'gkt', [64, 3])
            T.dma('sp', gqt[:], di['gq'], w=['gqt'], slot='c0')
            T.dma('sp', gkt[:], di['gk'], w=['gkt'], slot='c1')
            T.op('dve', lambda e: e.tensor_scalar(out=gq8[:], in0=gqt[:], scalar1=0.125, scalar2=None, op0=ALU.mult), r=['gqt'], w=['gq8'])
            for g in range(2):
                T.dma('sp', KselT[g][64:68, :], di['c_kpos'], w=[f'KselTp{g}'], slot='c2')
                T.dma('sp', KwinT[g][64:68, :], di['c_kpos'], w=[f'KwinTp{g}'], slot='c3')
                T.dma('sp', KcT[g][64:68, :], di['c_cpos'], w=[f'KcTp{g}'], slot='c4')
            T.op('pool', lambda e: e.memset(Vsel[:], 1.0), w=['Vsel_init'])
            T.op('pool', lambda e: e.memset(Vwin[:], 1.0), w=['Vwin_init'])
            T.op('pool', lambda e: e.memset(Vc[:], 1.0), w=['Vc_init'])

            with ExitStack() as e1:
                Wkv = sbt(e1, 'Wkv', [128, 8, 768], BF16)
                rawk = sbt(e1, 'rawk', [128, SEQ], BF16)
                rawv = sbt(e1, 'rawv', [128, SEQ], BF16)
                kvp = pst(e1, 'kvp', [128, 512])
                cmp_ = pst(e1, 'cmp_', [128, 512])
                tp2 = pst(e1, 'tp2', [128, 1024], BF16)
                sq = sbt(e1, 'sq', [128, 256])
                ssk = sbt(e1, 'ssk', [128, 4])
                rsk = sbt(e1, 'rsk', [128, 4])
                kn = sbt(e1, 'kn', [128, 256], BF16)
                kW = load_w(Wkv, di['w_in'], D, 768, 'Wkv', col0=O2, gt=g1t)
                for i in range(NT):
                    xt, hT, s = rms_hT(i, x_d)
                    cs = slice(i * 128, (i + 1) * 128)
                    for c in range(8):
                        T.op('pe', lambda e: e.matmul(kvp[:], lhsT=hT[:, c * 128:(c + 1) * 128], rhs=Wkv[:, c, 256:768],
                                                      start=(c == 0), stop=(c == 7)), r=[f'hT{s}'] + kW, w=['kvp'])
                    for half in range(2):
                        for c in range(8):
                            T.op('pe', lambda e: e.matmul(cmp_[:, half * 128:(half + 1) * 128],
                                                          lhsT=Wkv[:, c, half * 128:(half + 1) * 128],
                                                          rhs=hT[:, c * 128:(c + 1) * 128], start=(c == 0), stop=(c == 7)),
                                 r=[f'hT{s}'] + kW, w=['cmp_'])
                    T.op('act', lambda e: e.activation(out=sq[:, 0:128], in_=kvp[:, 0:128], func=AF.Square), r=['kvp'], w=['sq'])
                    T.op('act', lambda e: e.activation(out=sq[:, 128:256], in_=kvp[:, 256:384], func=AF.Square), r=['kvp'], w=['sq'])
                    T.op('dve', lambda e: e.tensor_reduce(out=ssk[:], in_=sq[:].rearrange("p (h d) -> p h d", h=4), axis=AX.X, op=ALU.add),
                         r=['sq'], w=['ssk'])
                    T.op('act', lambda e: e.activation(out=rsk[:], in_=ssk[:], func=AF.Sqrt, bias=epsb[:], scale=1.0 / 64),
                         r=['ssk', 'epsb'], w=['rsk'])
                    T.op('dve', lambda e: e.reciprocal(out=rsk[:], in_=rsk[:]), r=['rsk'], w=['rsk'])
                    for b in range(2):
                        T.op('dve', lambda e: e.tensor_tensor(
                            out=kn[:, b * 128:(b + 1) * 128].rearrange("p (h d) -> p h d", h=2),
                            in0=kvp[:, b * 256:b * 256 + 128].rearrange("p (h d) -> p h d", h=2),
                            in1=rsk[:, 2 * b:2 * b + 2].unsqueeze(2).broadcast_to([128, 2, 64]), op=ALU.mult),
                            r=['kvp', 'rsk'], w=['kn'])
                    for j in range(4):
                        T.op('pe', lambda e: e.transpose(tp2[0:64, j * 128:(j + 1) * 128], kn[:, j * 64:(j + 1) * 64], identb[:]),
                             r=['kn', 'identb'], w=['tp2'])
                    for j in range(4):
                        br, g = (1, j) if j < 2 else (2, j - 2)
                        dst = (KselT if br == 1 else KwinT)[g]
                        T.op('act', lambda e: e.activation(out=dst[0:64, cs], in_=tp2[0:64, j * 128:(j + 1) * 128], func=AF.Copy,
                                                           scale=gkt[:, br:br + 1]), r=['tp2', 'gkt'],
                             w=[f'K{br}_{g}_{i}'])
                    for b, Vt, nm in ((0, Vsel, 'Vs'), (1, Vwin, 'Vw')):
                        T.op('dve', lambda e: e.tensor_copy(
                            out=Vt[:, i * 130:(i + 1) * 130].rearrange("p (g d) -> p g d", g=2)[:, :, 0:64],
                            in_=kvp[:, b * 256 + 128:b * 256 + 256].rearrange("p (g d) -> p g d", g=2)),
                            r=['kvp', 'Vsel_init', 'Vwin_init'], w=[f'{nm}_{i}'])
                    T.op('act', lambda e: e.activation(out=rawk[:, cs], in_=cmp_[:, 0:128], func=AF.Copy), r=['cmp_'], w=['rawk'])
                    T.op('act', lambda e: e.activation(out=rawv[:, cs], in_=cmp_[:, 128:256], func=AF.Copy), r=['cmp_'], w=['rawv'])

                W1 = {'k': sbt(e1, 'W1k', [128, 4096], BF16), 'v': sbt(e1, 'W1v', [128, 4096], BF16)}
                W2 = {'k': sbt(e1, 'W2k', [128, 64], BF16), 'v': sbt(e1, 'W2v', [128, 64], BF16)}
                pe_ = {'k': sbt(e1, 'pek', [64, 32], BF16), 'v': sbt(e1, 'pev', [64, 32], BF16)}
                ones64 = sbt(e1, 'ones64', [64, 64])
                T.dma('sp', ones64[:], di['c_ones64'], w=['ones64'], slot='c0')
                for kd in 'kv':
                    for a in range(4):
                        load_cast(W1[kd][:, a * 1024:(a + 1) * 1024], di['w1' + kd][:, a * 1024:(a + 1) * 1024], 1024, 'W1' + kd)
                    load_cast(W2[kd][:], di['w2' + kd], 64, 'W2' + kd)
                    load_cast(pe_[kd][:], di['pe' + kd], 32, 'pe' + kd, parts=64)
                b1p = pst(e1, 'b1p', [128, 512])
                b1s = sbt(e1, 'b1s', [128, 1])
                hid = sbt(e1, 'hid', [128, NCT * 128], BF16)
                sqc = sbt(e1, 'sqc', [64, 512])
                rsc = sbt(e1, 'rsc', [64, 512])
                T.op('pool', lambda e: e.memset(hid[:], 0.0), w=['hid'])
                for g in range(2):
                    T.op('pool', lambda e: e.memset(KcT[g][0:64, :], 0.0), w=[f'KcT{g}'])
                for kd in 'kv':
                    raw = rawk if kd == 'k' else rawv
                    for r_ in range(32):
                        T.op('pe', lambda e: e.matmul(b1p[:, 0:1], lhsT=W1[kd][0:64, r_ * 128:(r_ + 1) * 128],
                                                      rhs=pe_[kd][0:64, r_:r_ + 1], start=(r_ == 0), stop=(r_ == 31)),
                             r=['W1' + kd, 'pe' + kd], w=['b1p'])
                    T.op('dve', lambda e: e.tensor_copy(out=b1s[:], in_=b1p[:, 0:1]), r=['b1p'], w=['b1s'])
                    for g in range(2):
                        ps_ = slice(g * 64, (g + 1) * 64)
                        for r_ in range(32):
                            T.op('pe', lambda e: e.matmul(kvp[:, 0:nblk], lhsT=W1[kd][ps_, r_ * 128:(r_ + 1) * 128],
                                                          rhs=raw[ps_, r_:r_ + 16 * (nblk - 1) + 1:16],
                                                          start=(r_ == 0), stop=(r_ == 31)),
                                 r=['W1' + kd, 'rawk', 'rawv'], w=['kvp'])
                        T.op('act', lambda e: e.activation(out=hid[:, 0:nblk], in_=kvp[:, 0:nblk], func=AF.Gelu_apprx_tanh, bias=b1s[:]),
                             r=['kvp', 'b1s'], w=['hid'])
                        if kd == 'k':
                            T.op('pe', lambda e: e.matmul(cmp_[0:64, 0:nblk], lhsT=W2['k'][:, :], rhs=hid[:, 0:nblk], start=True, stop=True),
                                 r=['W2k', 'hid'], w=['cmp_'])
                            T.op('act', lambda e: e.activation(out=sqc[:, 0:nblk], in_=cmp_[0:64, 0:nblk], func=AF.Square), r=['cmp_'], w=['sqc'])
                            T.op('pe', lambda e: e.matmul(b1p[0:64, 0:nblk], lhsT=ones64[:, :], rhs=sqc[:, 0:nblk], start=True, stop=True),
                                 r=['ones64', 'sqc'], w=['b1p'])
                            T.op('act', lambda e: e.activation(out=rsc[:, 0:nblk], in_=b1p[0:64, 0:nblk], func=AF.Sqrt, bias=epsb[0:64, :]),
                                 r=['b1p', 'epsb'], w=['rsc'])
                            T.op('dve', lambda e: e.reciprocal(out=rsc[:, 0:nblk], in_=rsc[:, 0:nblk]), r=['rsc'], w=['rsc'])
                            T.op('dve', lambda e: e.scalar_tensor_tensor(out=KcT[g][0:64, 0:nblk], in0=cmp_[0:64, 0:nblk], scalar=gkt[:, 0:1],
                                                                         in1=rsc[:, 0:nblk], op0=ALU.mult, op1=ALU.mult),
                                 r=['cmp_', 'gkt', 'rsc'], w=[f'KcT{g}'])
                        else:
                            for ct in range(NCT):
                                T.op('pe', lambda e: e.matmul(cmp_[:, 0:64], lhsT=hid[:, ct * 128:(ct + 1) * 128], rhs=W2['v'][:, :],
                                                              start=True, stop=True), r=['W2v', 'hid'], w=['cmp_'])
                                T.op('dve', lambda e: e.tensor_copy(out=Vc[:, (ct * 2 + g) * 65:(ct * 2 + g) * 65 + 64], in_=cmp_[:, 0:64]),
                                     r=['cmp_', 'Vc_init'], w=[f'Vc{g}'])

            if dbg and not os.environ.get('NODUMP'):
                for g in range(2):
                    dk = nc.dram_tensor(f'dbg_kc{g}', [68, NCT * 128], BF16, kind="ExternalOutput").ap()
                    T.dma('sp', dk, KcT[g][:], r=[f'KcT{g}', f'KcTp{g}'], slot=f'dbgk{g}')
                dv = nc.dram_tensor('dbg_vc', [128, NCT * 130], BF16, kind="ExternalOutput").ap()
                T.dma('sp', dv, Vc[:], r=['Vc0', 'Vc1', 'Vc_init'], slot='dbgv')
            T.barrier()
            with ExitStack() as e3:
                Wq = sbt(e3, 'Wq', [128, 8, 536], BF16)
                eall = sbt(e3, 'eall', [128, SEQ], BF16)
                aaug = sbt(e3, 'aaug', [128, NCT * 128], BF16)
                wc = sbt(e3, 'wc', [128, 272], BF16)
                T.dma('sp', eall[:], di['c_eall'], w=['eall'], slot='c0')
                T.dma('sp', aaug[:], di['c_aaug'], w=['aaug'], slot='c1')
                T.dma('sp', wc[:], di['c_wc'], w=['wc'], slot='c2')
                kQ = load_w(Wq, di['w_in'], D, 512, 'Wq', col0=O1, gt=g1t)
                kQ += load_w(Wq[:, :, 512:536], di['w_in'], D, 24, 'Wg', col0=O3, gt=g1t)
                bankQ = pst(e3, 'bankQ', [128, 512])
                bankG = pst(e3, 'bankG', [128, 512])
                Sb = [pst(e3, f'S{k}', [128, 512]) for k in range(2)]
                Ob = [pst(e3, f'O{k}', [128, 512]) for k in range(2)]
                IMP = pst(e3, 'IMP', [128, 512])
                Pb = [sbt(e3, f'P{k}', [128, 512], BF16) for k in range(2)]
                QTs = [sbt(e3, f'QT{k}', [68, 1024], BF16) for k in range(2)]
                Pc = [sbt(e3, f'Pc{k}', [128, 512], BF16) for k in range(NCT)]
                sqq = sbt(e3, 'sqq', [128, 512])
                ss8 = sbt(e3, 'ss8', [128, 8])
                rq = sbt(e3, 'rq', [128, 8])
                qn = sbt(e3, 'qn', [128, 512], BF16)
                gsig = sbt(e3, 'gsig', [128, 24])
                fbs = [sbt(e3, f'fb{k}', [128, 128]) for k in range(2)]
                den4 = sbt(e3, 'den4', [128, 4])
                impa = sbt(e3, 'impa', [128, 128])
                m1 = sbt(e3, 'm1', [128, 8])
                m2 = sbt(e3, 'm2', [128, 8])
                tmpm = sbt(e3, 'tmpm', [128, 128])
                negm = sbt(e3, 'negm', [128, 128], BF16)
                nmT4 = sbt(e3, 'nmT4', [128, 512], BF16)
                osb = sbt(e3, 'osb', [65, 512])
                dd = sbt(e3, 'dd', [128, 4])
                rr = sbt(e3, 'rr', [128, 4])
                ynsa = sbt(e3, 'ynsa', [128, 512])
                ynb = sbt(e3, 'ynb', [128, 512], BF16)
                ynT = [sbt(e3, f'ynT{k}', [128, 512], BF16) for k in range(2)]
                cnt = {'s': 0, 'o': 0}

                def score_tile(KT, QT, g, qk, lo, masks, Vt, voff, Oacc, first, last, kkeys, vkeys, imp_ct=None):
                    k = cnt['s'] % 2
                    cnt['s'] += 1
                    S, P = Sb[k], Pb[k]
                    T.op('pe', lambda e: e.matmul(S[:], lhsT=KT[0:68, lo:lo + 128], rhs=QT[0:68, g * 512:(g + 1) * 512],
                                                  start=True, stop=(len(masks) == 0)), r=kkeys + [qk], w=[f'S{k}'])
                    for mi, (ml, mr, mk) in enumerate(masks):
                        T.op('pe', lambda e: e.matmul(S[:], lhsT=ml, rhs=mr, start=False, stop=(mi == len(masks) - 1)),
                             r=mk, w=[f'S{k}'])
                    T.op('act', lambda e: e.activation(out=P[:], in_=S[:], func=AF.Exp), r=[f'S{k}'], w=[f'P{k}'])
                    T.op('pe', lambda e: e.matmul(Oacc[0:65, :], lhsT=Vt[:, voff:voff + 65], rhs=P[:], start=first, stop=last),
                         r=[f'P{k}'] + vkeys, w=['Oacc'])
                    if imp_ct is not None:
                        T.op('pool', lambda e: e.tensor_copy(out=Pc[imp_ct][:], in_=P[:]), r=[f'P{k}'], w=[f'Pc{imp_ct}'])

                DBG_BR = int(os.environ.get('DBG_BR', '-1'))

                def combine(Oacc, g, br):
                    if DBG_BR >= 0 and br != DBG_BR:
                        return
                    T.op('act', lambda e: e.activation(out=osb[:], in_=Oacc[0:65, :], func=AF.Copy), r=['Oacc'], w=['osb'])
                    for hg in range(4):
                        T.op('pe', lambda e: e.transpose(bankG[:, hg * 65:(hg + 1) * 65], osb[0:65, hg * 128:(hg + 1) * 128], identf[0:65, 0:65]),
                             r=['osb', 'identf'], w=['bankG'])
                    ot = bankG[:, 0:260].rearrange("p (h d) -> p h d", h=4)
                    T.op('dve', lambda e: e.tensor_scalar(out=dd[:].unsqueeze(2), in0=ot[:, :, 64:65], scalar1=1e-30, scalar2=None, op0=ALU.max),
                         r=['bankG'], w=['dd'])
                    T.op('dve', lambda e: e.reciprocal(out=dd[:], in_=dd[:]), r=['dd'], w=['dd'])
                    if DBG_BR >= 0:
                        T.op('dve', lambda e: e.tensor_copy(out=rr[:], in_=dd[:]), r=['dd'], w=['rr'])
                    else:
                        T.op('dve', lambda e: e.tensor_tensor(out=rr[:], in0=dd[:], in1=gsig[:, br * 8 + g * 4:br * 8 + g * 4 + 4], op=ALU.mult),
                             r=['dd', 'gsig'], w=['rr'])
                    for hg in range(4):
                        hh = g * 4 + hg
                        ysl = ynsa[:, hh * 64:(hh + 1) * 64]
                        if br == 0 or DBG_BR >= 0:
                            T.op('dve', lambda e: e.tensor_scalar(out=ysl, in0=bankG[:, hg * 65:hg * 65 + 64], scalar1=rr[:, hg:hg + 1],
                                                                  scalar2=None, op0=ALU.mult), r=['bankG', 'rr'], w=['ynsa'])
                        else:
                            T.op('dve', lambda e: e.scalar_tensor_tensor(out=ysl, in0=bankG[:, hg * 65:hg * 65 + 64], scalar=rr[:, hg:hg + 1],
                                                                         in1=ysl, op0=ALU.mult, op1=ALU.add), r=['bankG', 'rr', 'ynsa'], w=['ynsa'])

                for i in range(NT):
                    xt, hT, s = rms_hT(i, x_d)
                    QT = QTs[i % 2]
                    qk = f'QT{i % 2}'
                    fb = fbs[i % 2]
                    for c in range(8):
                        T.op('pe', lambda e: e.matmul(bankQ[:], lhsT=hT[:, c * 128:(c + 1) * 128], rhs=Wq[:, c, 0:512],
                                                      start=(c == 0), stop=(c == 7)), r=[f'hT{s}'] + kQ, w=['bankQ'])
                    for c in range(8):
                        T.op('pe', lambda e: e.matmul(bankG[:, 0:24], lhsT=hT[:, c * 128:(c + 1) * 128], rhs=Wq[:, c, 512:536],
                                                      start=(c == 0), stop=(c == 7)), r=[f'hT{s}'] + kQ, w=['bankG'])
                    T.op('act', lambda e: e.activation(out=sqq[:], in_=bankQ[:], func=AF.Square), r=['bankQ'], w=['sqq'])
                    T.op('dve', lambda e: e.tensor_reduce(out=ss8[:], in_=sqq[:].rearrange("p (h d) -> p h d", h=8), axis=AX.X, op=ALU.add),
                         r=['sqq'], w=['ss8'])
                    T.op('act', lambda e: e.activation(out=rq[:], in_=ss8[:], func=AF.Sqrt, bias=epsb[:], scale=1.0 / 64), r=['ss8', 'epsb'], w=['rq'])
                    T.op('dve', lambda e: e.reciprocal(out=rq[:], in_=rq[:]), r=['rq'], w=['rq'])
                    T.op('dve', lambda e: e.tensor_tensor(out=qn[:].rearrange("p (h d) -> p h d", h=8),
                                                          in0=bankQ[:].rearrange("p (h d) -> p h d", h=8),
                                                          in1=rq[:].unsqueeze(2).broadcast_to([128, 8, 64]), op=ALU.mult),
                         r=['bankQ', 'rq'], w=['qn'])
                    for h in range(8):
                        T.op('pe', lambda e: e.transpose(tp[0:64, h * 128:(h + 1) * 128], qn[:, h * 64:(h + 1) * 64], identb[:]),
                             r=['qn', 'identb'], w=['tp'])
                    T.op('act', lambda e: e.activation(out=QT[0:64, :], in_=tp[0:64, :], func=AF.Copy, scale=gq8[:]), r=['tp', 'gq8'], w=[qk])
                    T.dma('sp', QT[64:68, :].rearrange("p (h q) -> p h q", h=8), di['c_qpos'][:, :, i * 128:(i + 1) * 128],
                          w=[qk + 'p'], slot=qk + 'p')
                    qk2 = [qk, qk + 'p']
                    T.op('act', lambda e: e.activation(out=gsig[:], in_=bankG[:, 0:24], func=AF.Sigmoid), r=['bankG'], w=['gsig'])
                    T.dma('sp', fb[:], di['c_fb'][i * 128:(i + 1) * 128, :], w=[f'fb{i % 2}'], slot=f'fb{i % 2}')
                    if dbg and not os.environ.get('NODUMP'):
                        if i == 0:
                            dgs = nc.dram_tensor('dbg_gsig', [NT * 128, 24], F32, kind="ExternalOutput").ap()
                        T.dma('sp', dgs[i * 128:(i + 1) * 128, :], gsig[:], r=['gsig'], slot='dbggs')
                    for g in range(2):
                        O = Ob[cnt['o'] % 2]
                        cnt['o'] += 1
                        n_ct = min(NCT, (8 * i + 6) // 128 + 1)
                        for ct in range(n_ct):
                            off = 128 * ct - 8 * i + 2
                            masks = []
                            if off > -127:
                                masks.append((wc[:, 127 + off:127 + off + 128], i4[:], ['wc', 'i4']))
                            score_tile(KcT[g], QT, g, qk, ct * 128, masks, Vc, (ct * 2 + g) * 65, O, ct == 0, ct == n_ct - 1,
                                       [f'KcT{g}', f'KcTp{g}', qk + 'p'], [f'Vc{g}', 'Vc_init'], imp_ct=ct)
                        combine(O, g, 0)
                        for h in range(4):
                            for ct in range(n_ct):
                                T.op('pe', lambda e: e.matmul(IMP[:, h * 128:(h + 1) * 128], lhsT=Pc[ct][:, h * 128:(h + 1) * 128],
                                                              rhs=aaug[:, ct * 128:(ct + 1) * 128], start=(ct == 0), stop=(ct == n_ct - 1)),
                                     r=[f'Pc{ct}', 'aaug'], w=['IMP'])
                        iv = IMP[:].rearrange("p (h n) -> p h n", h=4)
                        T.op('dve', lambda e: e.tensor_scalar(out=den4[:].unsqueeze(2), in0=iv[:, :, 127:128], scalar1=1e-30, scalar2=None, op0=ALU.max),
                             r=['IMP'], w=['den4'])
                        T.op('dve', lambda e: e.reciprocal(out=den4[:], in_=den4[:]), r=['den4'], w=['den4'])
                        T.op('dve', lambda e: e.tensor_scalar(out=impa[:], in0=IMP[:, 0:128], scalar1=den4[:, 0:1], scalar2=None, op0=ALU.mult),
                             r=['IMP', 'den4'], w=['impa'])
                        for h in range(1, 4):
                            T.op('dve', lambda e: e.scalar_tensor_tensor(out=impa[:], in0=IMP[:, h * 128:(h + 1) * 128], scalar=den4[:, h:h + 1],
                                                                         in1=impa[:], op0=ALU.mult, op1=ALU.add), r=['IMP', 'den4', 'impa'], w=['impa'])
                        T.op('dve', lambda e: e.tensor_tensor(out=impa[:], in0=impa[:], in1=fb[:], op=ALU.add), r=['impa', f'fb{i % 2}'], w=['impa'])
                        T.op('dve', lambda e: e.max(out=m1[:], in_=impa[:]), r=['impa'], w=['m1'])
                        T.op('dve', lambda e: e.match_replace(out=tmpm[:], in_to_replace=m1[:], in_values=impa[:], imm_value=-3e38),
                             r=['impa', 'm1'], w=['tmpm'])
                        T.op('dve', lambda e: e.max(out=m2[:], in_=tmpm[:]), r=['tmpm'], w=['m2'])
                        T.op('dve', lambda e: e.tensor_scalar(out=negm[:], in0=impa[:], scalar1=m2[:, 7:8], scalar2=NEG, op0=ALU.is_lt, op1=ALU.mult),
                             r=['impa', 'm2'], w=['negm'])
                        T.op('pe', lambda e: e.transpose(tp[:, 0:128], negm[:], identb[:]), r=['negm', 'identb'], w=['tp'])
                        T.op('dve', lambda e: e.tensor_copy(out=nmT4[:].rearrange("p (h q) -> p h q", h=4),
                                                            in_=tp[:, 0:128].unsqueeze(1).broadcast_to([128, 4, 128])), r=['tp'], w=['nmT4'])
                        O = Ob[cnt['o'] % 2]
                        cnt['o'] += 1
                        for kt in range(i + 1):
                            masks = [(eall[:, kt * 128:(kt + 1) * 128], nmT4[:], ['eall', 'nmT4'])]
                            if kt == i:
                                masks.append((negtri[:], i4[:], ['negtri', 'i4']))
                            score_tile(KselT[g], QT, g, qk, kt * 128, masks, Vsel, kt * 130 + g * 65, O, kt == 0, kt == i,
                                       [f'K1_{g}_{kt}', f'KselTp{g}', qk + 'p'], [f'Vs_{kt}'])
                        combine(O, g, 1)
                        O = Ob[cnt['o'] % 2]
                        cnt['o'] += 1
                        k0 = max(0, i - 4)
                        for kt in range(k0, i + 1):
                            masks = []
                            if kt == i:
                                masks.append((negtri[:], i4[:], ['negtri', 'i4']))
                            if kt == i - 4:
                                masks.append((negtric[:], i4[:], ['negtric', 'i4']))
                            score_tile(KwinT[g], QT, g, qk, kt * 128, masks, Vwin, kt * 130 + g * 65, O, kt == k0, kt == i,
                                       [f'K2_{g}_{kt}', f'KwinTp{g}', qk + 'p'], [f'Vw_{kt}'])
                        combine(O, g, 2)
                    yT = ynT[i % 2]
                    T.op('act', lambda e: e.activation(out=ynb[:], in_=ynsa[:], func=AF.Copy), r=['ynsa'], w=['ynb'])
                    for c in range(4):
                        T.op('pe', lambda e: e.transpose(tp[:, c * 128:(c + 1) * 128], ynb[:, c * 128:(c + 1) * 128], identb[:]),
                             r=['ynb', 'identb'], w=['tp'])
                    T.op('dve', lambda e: e.tensor_copy(out=yT[:], in_=tp[:, 0:512]), r=['tp'], w=[f'ynT{i % 2}'])
                    T.dma('sp', yn_d[i], yT[:], r=[f'ynT{i % 2}'], w=[f'yn_{i}'], slot=f'ynT{i % 2}')

        T.barrier()
        with ExitStack() as e4:
            ctab = sbt(e4, 'ctab', [128, 2048])
            stab = sbt(e4, 'stab', [128, 2048])
            BbT = [sbt(e4, f'BbT{k}', [128, 2048], BF16) for k in range(2)]
            Cb = [sbt(e4, f'Cb{k}', [128, 2048], BF16) for k in range(2)]
            mag = sbt(e4, 'mag', [128, 16])
            rotc = sbt(e4, 'rotc', [128, 16])
            rots = sbt(e4, 'rots', [128, 16])
            dsk = sbt(e4, 'dsk', [128, 4])
            bglu = sbt(e4, 'bglu', [128, 4])
            T.dma('sp', dsk[:], di['dsk'], w=['dsk'], slot='c0')
            T.dma('sp', bglu[:], di['bglu'], w=['bglu'], slot='c1')
            Wu = sbt(e4, 'Wu', [128, 8, 512], BF16)
            Wgl = sbt(e4, 'Wgl', [128, 4, 512], BF16)
            kU = load_w(Wu, di['w_in'], D, 512, 'Wu', col0=0, gt=g1t)
            kGl = load_w(Wgl, di['w_glu'], 512, 512, 'Wgl')

            def sincos(st, phi, n, s_out, c_out, nm):
                u = sbt(st, nm + 'u', [128, n])
                ki = sbt(st, nm + 'ki', [128, n], I32)
                kf = sbt(st, nm + 'kf', [128, n])
                fx = sbt(st, nm + 'fx', [128, n])
                r_ = sbt(st, nm + 'r', [128, n])
                for shift, dst in ((0.0, s_out), (np.pi / 2, c_out)):
                    T.op('dve', lambda e: e.tensor_scalar(out=u[:], in0=phi, scalar1=shift, scalar2=1.0 / (2 * np.pi), op0=ALU.add, op1=ALU.mult),
                         r=[nm + 'phi'], w=[nm + 'u'])
                    T.op('dve', lambda e: e.tensor_copy(out=ki[:], in_=u[:]), r=[nm + 'u'], w=[nm + 'ki'])
                    T.op('dve', lambda e: e.tensor_copy(out=kf[:], in_=ki[:]), r=[nm + 'ki'], w=[nm + 'kf'])
                    T.op('dve', lambda e: e.tensor_scalar(out=r_[:], in0=phi, scalar1=shift, scalar2=None, op0=ALU.add), r=[nm + 'phi'], w=[nm + 'r'])
                    T.op('dve', lambda e: e.scalar_tensor_tensor(out=r_[:], in0=kf[:], scalar=-2 * np.pi, in1=r_[:], op0=ALU.mult, op1=ALU.add),
                         r=[nm + 'kf', nm + 'r'], w=[nm + 'r'])
                    T.op('dve', lambda e: e.tensor_scalar(out=fx[:], in0=r_[:], scalar1=np.pi, scalar2=-2 * np.pi, op0=ALU.is_gt, op1=ALU.mult),
                         r=[nm + 'r'], w=[nm + 'fx'])
                    T.op('dve', lambda e: e.tensor_tensor(out=r_[:], in0=r_[:], in1=fx[:], op=ALU.add), r=[nm + 'r', nm + 'fx'], w=[nm + 'r'])
                    T.op('dve', lambda e: e.tensor_scalar(out=fx[:], in0=r_[:], scalar1=-np.pi, scalar2=2 * np.pi, op0=ALU.is_lt, op1=ALU.mult),
                         r=[nm + 'r'], w=[nm + 'fx'])
                    T.op('dve', lambda e: e.tensor_tensor(out=r_[:], in0=r_[:], in1=fx[:], op=ALU.add), r=[nm + 'r', nm + 'fx'], w=[nm + 'r'])
                    T.op('dve', lambda e: e.tensor_scalar(out=r_[:], in0=r_[:], scalar1=3.14159, scalar2=-3.14159, op0=ALU.min, op1=ALU.max),
                         r=[nm + 'r'], w=[nm + 'r'])
                    T.op('act', lambda e: e.activation(out=dst, in_=r_[:], func=AF.Sin), r=[nm + 'r'], w=[nm + 'out'])

            with ExitStack() as e0:
                sm = lambda n, s=[128, 16]: sbt(e0, n, s)
                are, aim, ldt = sm('are'), sm('aim'), sm('ldt')
                T.dma('sp', are[:], di['are'], w=['are'], slot='c2')
                T.dma('sp', aim[:], di['aim'], w=['aim'], slot='c3')
                T.dma('sp', ldt[:], di['ldt'], w=['ldt'], slot='c4')
                dt_, adr, adi, cs_, sn_ = sm('dt_'), sm('adr'), sm('adi'), sm('cs_'), sm('sn_')
                abr, abi, nr, den, fr, fi, t0, t1 = sm('abr'), sm('abi'), sm('nr'), sm('den'), sm('fr'), sm('fi'), sm('t0'), sm('t1')
                a128 = sm('a128')
                V = lambda fn, r, w: T.op('dve', fn, r=r, w=w)
                T.op('act', lambda e: e.activation(out=dt_[:], in_=ldt[:], func=AF.Exp), r=['ldt'], w=['dt_'])
                V(lambda e: e.tensor_tensor(out=adr[:], in0=are[:], in1=dt_[:], op=ALU.mult), ['are', 'dt_'], ['adr'])
                V(lambda e: e.tensor_tensor(out=adi[:], in0=aim[:], in1=dt_[:], op=ALU.mult), ['aim', 'dt_'], ['s0phi'])
                T.op('act', lambda e: e.activation(out=mag[:], in_=adr[:], func=AF.Exp), r=['adr'], w=['mag'])
                sincos(e0, adi[:], 16, sn_[:], cs_[:], 's0')
                V(lambda e: e.tensor_tensor(out=abr[:], in0=mag[:], in1=cs_[:], op=ALU.mult), ['mag', 's0out'], ['abr'])
                V(lambda e: e.tensor_tensor(out=abi[:], in0=mag[:], in1=sn_[:], op=ALU.mult), ['mag', 's0out'], ['abi'])
                V(lambda e: e.tensor_scalar(out=nr[:], in0=abr[:], scalar1=-1.0, scalar2=None, op0=ALU.add), ['abr'], ['nr'])
                V(lambda e: e.tensor_tensor(out=den[:], in0=are[:], in1=are[:], op=ALU.mult), ['are'], ['den'])
                V(lambda e: e.tensor_tensor(out=t0[:], in0=aim[:], in1=aim[:], op=ALU.mult), ['aim'], ['t0'])
                V(lambda e: e.tensor_tensor(out=den[:], in0=den[:], in1=t0[:], op=ALU.add), ['den', 't0'], ['den'])
                V(lambda e: e.reciprocal(out=den[:], in_=den[:]), ['den'], ['den'])
                V(lambda e: e.tensor_tensor(out=t0[:], in0=nr[:], in1=are[:], op=ALU.mult), ['nr', 'are'], ['t0'])
                V(lambda e: e.tensor_tensor(out=t1[:], in0=abi[:], in1=aim[:], op=ALU.mult), ['abi', 'aim'], ['t1'])
                V(lambda e: e.tensor_tensor(out=t0[:], in0=t0[:], in1=t1[:], op=ALU.add), ['t0', 't1'], ['t0'])
                V(lambda e: e.tensor_tensor(out=fr[:], in0=t0[:], in1=den[:], op=ALU.mult), ['t0', 'den'], ['fr'])
                V(lambda e: e.tensor_tensor(out=t0[:], in0=abi[:], in1=are[:], op=ALU.mult), ['abi', 'are'], ['t0'])
                V(lambda e: e.tensor_tensor(out=t1[:], in0=nr[:], in1=aim[:], op=ALU.mult), ['nr', 'aim'], ['t1'])
                V(lambda e: e.tensor_tensor(out=t0[:], in0=t0[:], in1=t1[:], op=ALU.subtract), ['t0', 't1'], ['t0'])
                V(lambda e: e.tensor_tensor(out=fi[:], in0=t0[:], in1=den[:], op=ALU.mult), ['t0', 'den'], ['fi'])
                V(lambda e: e.tensor_scalar(out=a128[:], in0=adi[:], scalar1=128.0, scalar2=None, op0=ALU.mult), ['s0phi'], ['s1phi'])
                sincos(e0, a128[:], 16, rots[:], rotc[:], 's1')
                tp1 = sbt(e0, 'tp1', [128, 128])
                T.dma('sp', tp1[:], di['c_tp1'], w=['tp1'], slot='c0')
                phi = sbt(e0, 'phi', [128, 2048])
                V(lambda e: e.tensor_tensor(out=phi[:].rearrange("p (j t) -> p j t", j=16),
                                            in0=adi[:].unsqueeze(2).broadcast_to([128, 16, 128]),
                                            in1=tp1[:].unsqueeze(1).broadcast_to([128, 16, 128]), op=ALU.mult), ['s0phi', 'tp1'], ['s2phi'])
                sincos(e0, phi[:], 2048, stab[:], ctab[:], 's2')
                bre = sbt(e0, 'bre', [128, 2048])
                bim = sbt(e0, 'bim', [128, 2048])
                u1 = sbt(e0, 'u1', [128, 2048])
                u2 = sbt(e0, 'u2', [128, 2048])
                bb = [sbt(e0, f'bb{k}', [128, 2048]) for k in range(2)]
                T.dma('sp', bre[:], di['bre'], w=['bre'], slot='c1')
                T.dma('sp', bim[:], di['bim'], w=['bim'], slot='c2')
                v3 = lambda t_: t_[:].rearrange("p (j c) -> p j c", j=16)
                bc = lambda t_: t_[:].unsqueeze(2).broadcast_to([128, 16, 128])
                V(lambda e: e.tensor_tensor(out=v3(u1), in0=v3(bre), in1=bc(fr), op=ALU.mult), ['bre', 'fr'], ['u1'])
                V(lambda e: e.tensor_tensor(out=v3(u2), in0=v3(bim), in1=bc(fi), op=ALU.mult), ['bim', 'fi'], ['u2'])
                V(lambda e: e.tensor_tensor(out=bb[0][:], in0=u1[:], in1=u2[:], op=ALU.subtract), ['u1', 'u2'], ['bb0'])
                V(lambda e: e.tensor_tensor(out=v3(u1), in0=v3(bim), in1=bc(fr), op=ALU.mult), ['bim', 'fr', 'bb0'], ['u1'])
                V(lambda e: e.tensor_tensor(out=v3(u2), in0=v3(bre), in1=bc(fi), op=ALU.mult), ['bre', 'fi', 'bb0'], ['u2'])
                V(lambda e: e.tensor_tensor(out=bb[1][:], in0=u1[:], in1=u2[:], op=ALU.add), ['u1', 'u2'], ['bb1'])
                trp = pst(e0, 'trp', [128, 512])
                for k in range(2):
                    for jg in range(4):
                        for jj in range(4):
                            j = jg * 4 + jj
                            T.op('pe', lambda e: e.transpose(trp[:, jj * 128:(jj + 1) * 128], bb[k][:, j * 128:(j + 1) * 128], identf[:]),
                                 r=[f'bb{k}', 'identf'], w=['trp'])
                        T.op('dve', lambda e: e.tensor_copy(out=BbT[k][:, jg * 512:(jg + 1) * 512], in_=trp[:]), r=['trp'], w=[f'BbT{k}'])
                T.dma('sp', bre[:], di['cre'], r=['u1', 'u2'], w=['bre'], slot='c1')
                T.dma('sp', bim[:], di['cim'], r=['u1', 'u2'], w=['bim'], slot='c2')
                T.op('act', lambda e: e.activation(out=Cb[0][:], in_=bre[:], func=AF.Copy), r=['bre'], w=['Cb0'])
                T.op('act', lambda e: e.activation(out=Cb[1][:], in_=bim[:], func=AF.Copy, scale=-1.0), r=['bim'], w=['Cb1'])

            T.barrier()
            with ExitStack() as e5:
                bankU = pst(e5, 'bankU', [128, 512])
                bu = [[pst(e5, f'bu{a}{b}', [128, 512]) for b in range(2)] for a in range(2)]
                yTp = pst(e5, 'yTp', [128, 512])
                zp = pst(e5, 'zp', [128, 512])
                ub = sbt(e5, 'ub', [128, 512], BF16)
                f4 = lambda n: sbt(e5, n, [128, 512])
                t1_, t2_, t3_, t4_, vr, vi, wr, wi, a1, a2, a3, a4 = [f4(n) for n in
                                                                      ('t1_', 't2_', 't3_', 't4_', 'vr', 'vi', 'wr', 'wi', 'a1', 'a2', 'a3', 'a4')]
                xr = sbt(e5, 'xr', [128, 512], BF16)
                xi = sbt(e5, 'xi', [128, 512], BF16)
                w0 = [[sbt(e5, f'w0{p}{k}', [128, 16]) for k in range(2)] for p in range(2)]
                q1, q2 = sbt(e5, 'q1', [128, 4]), sbt(e5, 'q2', [128, 4])
                ysb = sbt(e5, 'ysb', [128, 512])
                ygb = sbt(e5, 'ygb', [128, 512], BF16)
                sgb = sbt(e5, 'sgb', [128, 512])
                yso = [sbt(e5, f'yso{k}', [128, 512], BF16) for k in range(2)]
                for k in range(2):
                    T.op('dve', lambda e: e.memset(w0[0][k][:], 0.0), w=[f'w00{k}'])
                for i in range(NT):
                    xt, hT, s = rms_hT(i, x_d)
                    p, pn = i % 2, (i + 1) % 2
                    for sg in range(4):
                        for c in range(8):
                            T.op('pe', lambda e: e.matmul(bankU[:, sg * 128:(sg + 1) * 128], lhsT=Wu[:, c, sg * 128:(sg + 1) * 128],
                                                          rhs=hT[:, c * 128:(c + 1) * 128], start=(c == 0), stop=(c == 7)),
                                 r=[f'hT{s}'] + kU, w=['bankU'])
                    T.op('act', lambda e: e.activation(out=ub[:], in_=bankU[:], func=AF.Copy), r=['bankU'], w=['ub'])
                    for sg in range(4):
                        bur, bui = bu[sg % 2]
                        kr, ki_ = f'bu{sg % 2}0', f'bu{sg % 2}1'
                        cs = ctab[:, sg * 512:(sg + 1) * 512]
                        ss = stab[:, sg * 512:(sg + 1) * 512]
                        for jj in range(4):
                            j = sg * 4 + jj
                            for k, (bt, kk) in enumerate(((bur, kr), (bui, ki_))):
                                T.op('pe', lambda e: e.matmul(bt[:, jj * 128:(jj + 1) * 128], lhsT=BbT[k][:, j * 128:(j + 1) * 128],
                                                              rhs=ub[:, sg * 128:(sg + 1) * 128], start=True, stop=True),
                                     r=['ub', f'BbT{k}'], w=[kk])
                        T.op('dve', lambda e: e.tensor_tensor(out=t1_[:], in0=bur[:], in1=cs, op=ALU.mult), r=[kr, 's2out'], w=['t1_'])
                        T.op('dve', lambda e: e.tensor_tensor(out=t2_[:], in0=bui[:], in1=ss, op=ALU.mult), r=[ki_, 's2out'], w=['t2_'])
                        T.op('pool', lambda e: e.tensor_tensor(out=vr[:], in0=t1_[:], in1=t2_[:], op=ALU.add), r=['t1_', 't2_'], w=['vr'])
                        T.op('dve', lambda e: e.tensor_tensor(out=t3_[:], in0=bui[:], in1=cs, op=ALU.mult), r=[ki_, 's2out'], w=['t3_'])
                        T.op('dve', lambda e: e.tensor_tensor(out=t4_[:], in0=bur[:], in1=ss, op=ALU.mult), r=[kr, 's2out'], w=['t4_'])
                        T.op('pool', lambda e: e.tensor_tensor(out=vi[:], in0=t3_[:], in1=t4_[:], op=ALU.subtract), r=['t3_', 't4_'], w=['vi'])
                        for jj in range(4):
                            j = sg * 4 + jj
                            sl = slice(jj * 128, (jj + 1) * 128)
                            for k, (vv, ww, nm) in enumerate(((vr, wr, 'wr'), (vi, wi, 'wi'))):
                                T.op('dve', lambda e: e.tensor_tensor_scan(out=ww[:, sl], data0=mag[:, j:j + 1].broadcast_to([128, 128]),
                                                                           data1=vv[:, sl], initial=w0[p][k][:, j:j + 1],
                                                                           op0=ALU.mult, op1=ALU.add),
                                     r=['mag', 'vr' if k == 0 else 'vi', f'w0{p}{k}'], w=[nm])
                        T.op('pool', lambda e: e.tensor_tensor(out=a1[:], in0=wr[:], in1=cs, op=ALU.mult), r=['wr', 's2out'], w=['a1'])
                        T.op('pool', lambda e: e.tensor_tensor(out=a2[:], in0=wi[:], in1=ss, op=ALU.mult), r=['wi', 's2out'], w=['a2'])
                        T.op('pool', lambda e: e.tensor_tensor(out=xr[:], in0=a1[:], in1=a2[:], op=ALU.subtract), r=['a1', 'a2'], w=['xr'])
                        T.op('pool', lambda e: e.tensor_tensor(out=a3[:], in0=wi[:], in1=cs, op=ALU.mult), r=['wi', 's2out'], w=['a3'])
                        T.op('pool', lambda e: e.tensor_tensor(out=a4[:], in0=wr[:], in1=ss, op=ALU.mult), r=['wr', 's2out'], w=['a4'])
                        T.op('pool', lambda e: e.tensor_tensor(out=xi[:], in0=a3[:], in1=a4[:], op=ALU.add), r=['a3', 'a4'], w=['xi'])
                        wl_r = wr[:].rearrange("p (j t) -> p j t", j=4)[:, :, 127:128]
                        wl_i = wi[:].rearrange("p (j t) -> p j t", j=4)[:, :, 127:128]
                        rc = rotc[:, sg * 4:(sg + 1) * 4].unsqueeze(2)
                        rs = rots[:, sg * 4:(sg + 1) * 4].unsqueeze(2)
                        n_r = w0[pn][0][:, sg * 4:(sg + 1) * 4].unsqueeze(2)
                        n_i = w0[pn][1][:, sg * 4:(sg + 1) * 4].unsqueeze(2)
                        T.op('dve', lambda e: e.tensor_tensor(out=q1[:].unsqueeze(2), in0=wl_r, in1=rc, op=ALU.mult), r=['wr', 's1out'], w=['q1'])
                        T.op('dve', lambda e: e.tensor_tensor(out=q2[:].unsqueeze(2), in0=wl_i, in1=rs, op=ALU.mult), r=['wi', 's1out'], w=['q2'])
                        T.op('dve', lambda e: e.tensor_tensor(out=n_r, in0=q1[:].unsqueeze(2), in1=q2[:].unsqueeze(2), op=ALU.subtract),
                             r=['q1', 'q2'], w=[f'w0{pn}0'])
                        T.op('dve', lambda e: e.tensor_tensor(out=q1[:].unsqueeze(2), in0=wl_i, in1=rc, op=ALU.mult), r=['wi', 's1out', f'w0{pn}0'], w=['q1'])
                        T.op('dve', lambda e: e.tensor_tensor(out=q2[:].unsqueeze(2), in0=wl_r, in1=rs, op=ALU.mult), r=['wr', 's1out', f'w0{pn}0'], w=['q2'])
                        T.op('dve', lambda e: e.tensor_tensor(out=n_i, in0=q1[:].unsqueeze(2), in1=q2[:].unsqueeze(2), op=ALU.add),
                             r=['q1', 'q2'], w=[f'w0{pn}1'])
                        for jj in range(4):
                            j = sg * 4 + jj
                            sl = slice(jj * 128, (jj + 1) * 128)
                            T.op('pe', lambda e: e.matmul(yTp[:, sg * 128:(sg + 1) * 128], lhsT=Cb[0][:, j * 128:(j + 1) * 128], rhs=xr[:, sl],
                                                          start=(jj == 0), stop=False), r=['Cb0', 'xr'], w=['yTp'])
                            T.op('pe', lambda e: e.matmul(yTp[:, sg * 128:(sg + 1) * 128], lhsT=Cb[1][:, j * 128:(j + 1) * 128], rhs=xi[:, sl],
                                                          start=False, stop=(jj == 3)), r=['Cb1', 'xi'], w=['yTp'])
                    for sg in range(4):
                        sl = slice(sg * 128, (sg + 1) * 128)
                        T.op('dve', lambda e: e.scalar_tensor_tensor(out=ysb[:, sl], in0=ub[:, sl], scalar=dsk[:, sg:sg + 1], in1=yTp[:, sl],
                                                                     op0=ALU.mult, op1=ALU.add), r=['ub', 'dsk', 'yTp'], w=['ysb'])
                    T.op('act', lambda e: e.activation(out=ygb[:], in_=ysb[:], func=AF.Gelu_apprx_tanh), r=['ysb'], w=['ygb'])
                    for co in range(4):
                        for ci in range(4):
                            T.op('pe', lambda e: e.matmul(zp[:, co * 128:(co + 1) * 128], lhsT=Wgl[:, ci, co * 128:(co + 1) * 128],
                                                          rhs=ygb[:, ci * 128:(ci + 1) * 128], start=(ci == 0), stop=(ci == 3)),
                                 r=['ygb'] + kGl, w=['zp'])
                    for co in range(4):
                        sl = slice(co * 128, (co + 1) * 128)
                        T.op('act', lambda e: e.activation(out=sgb[:, sl], in_=zp[:, sl], func=AF.Sigmoid, bias=bglu[:, co:co + 1]),
                             r=['zp', 'bglu'], w=['sgb'])
                    yo = yso[i % 2]
                    T.op('pool', lambda e: e.tensor_tensor(out=yo[:], in0=ygb[:], in1=sgb[:], op=ALU.mult), r=['ygb', 'sgb'], w=[f'yso{i % 2}'])
                    T.dma('sp', ys_d[i], yo[:], r=[f'yso{i % 2}'], w=[f'ys_{i}'], slot=f'yso{i % 2}')

        T.barrier()
        with ExitStack() as e6:
            Wmg = sbt(e6, 'Wmg', [128, 8, 2048], BF16)
            Wps = sbt(e6, 'Wps', [128, 4, D], BF16)
            Wpn = sbt(e6, 'Wpn', [128, 4, D], BF16)
            Wo = sbt(e6, 'Wo', [128, 8, D], BF16)
            kMg = load_w(Wmg, di['w_in'], D, 2048, 'Wmg', col0=O4, gt=g1t)
            kPs = load_w(Wps, di['w_ps'], 512, D, 'Wps')
            kPn = load_w(Wpn, di['w_pn'], 512, D, 'Wpn')
            kWo = load_w(Wo, di['w_out'], D, D, 'Wo')
            Gp = [pst(e6, f'Gp{k}', [128, 512]) for k in range(2)]
            Pp = [pst(e6, f'Pp{k}', [128, 512]) for k in range(2)]
            Op = [pst(e6, f'Op{k}', [128, 512]) for k in range(2)]
            gs = [sbt(e6, f'gs{k}', [128, 512]) for k in range(2)]
            mt = [sbt(e6, f'mt{k}', [128, 512]) for k in range(2)]
            mrg = sbt(e6, 'mrg', [128, D], BF16)
            mT = sbt(e6, 'mT', [128, D], BF16)
            ysl_ = [sbt(e6, f'ysl{k}', [128, 512], BF16) for k in range(2)]
            ynl_ = [sbt(e6, f'ynl{k}', [128, 512], BF16) for k in range(2)]
            x1s = [sbt(e6, f'x1s{k}', [128, D]) for k in range(2)]
            for i in range(NT):
                xt, hT, s = rms_hT(i, x_d)
                q = i % 2
                T.dma('sp', ysl_[q][:], ys_d[i], r=[f'ys_{i}'], w=[f'ysl{q}'], slot=f'ysl{q}')
                T.dma('sp', ynl_[q][:], yn_d[i], r=[f'yn_{i}'], w=[f'ynl{q}'], slot=f'ynl{q}')
                for hf in range(2):
                    hs = slice(hf * 512, (hf + 1) * 512)
                    for m_ in range(2):
                        for c in range(8):
                            T.op('pe', lambda e: e.matmul(Gp[m_][:], lhsT=hT[:, c * 128:(c + 1) * 128],
                                                          rhs=Wmg[:, c, m_ * 1024 + hf * 512:m_ * 1024 + (hf + 1) * 512],
                                                          start=(c == 0), stop=(c == 7)), r=[f'hT{s}'] + kMg, w=[f'Gp{m_}'])
                        T.op('act', lambda e: e.activation(out=gs[m_][:], in_=Gp[m_][:], func=AF.Sigmoid), r=[f'Gp{m_}'], w=[f'gs{m_}'])
                        yl, Wp, kk, kn_ = ((ysl_[q], Wps, kPs, f'ysl{q}'), (ynl_[q], Wpn, kPn, f'ynl{q}'))[m_]
                        for ci in range(4):
                            T.op('pe', lambda e: e.matmul(Pp[m_][:], lhsT=yl[:, ci * 128:(ci + 1) * 128], rhs=Wp[:, ci, hs],
                                                          start=(ci == 0), stop=(ci == 3)), r=[kn_] + kk, w=[f'Pp{m_}'])
                        T.op('dve', lambda e: e.tensor_tensor(out=mt[m_][:], in0=Pp[m_][:], in1=gs[m_][:], op=ALU.mult),
                             r=[f'Pp{m_}', f'gs{m_}'], w=[f'mt{m_}'])
                    T.op('pool', lambda e: e.tensor_tensor(out=mrg[:, hs], in0=mt[0][:], in1=mt[1][:], op=ALU.add), r=['mt0', 'mt1'], w=['mrg'])
                for c in range(8):
                    T.op('pe', lambda e: e.transpose(tp[:, c * 128:(c + 1) * 128], mrg[:, c * 128:(c + 1) * 128], identb[:]),
                         r=['mrg', 'identb'], w=['tp'])
                T.op('dve', lambda e: e.tensor_copy(out=mT[:], in_=tp[:]), r=['tp'], w=['mT'])
                for hf in range(2):
                    hs = slice(hf * 512, (hf + 1) * 512)
                    for c in range(8):
                        T.op('pe', lambda e: e.matmul(Op[hf][:], lhsT=mT[:, c * 128:(c + 1) * 128], rhs=Wo[:, c, hs],
                                                      start=(c == 0), stop=(c == 7)), r=['mT'] + kWo, w=[f'Op{hf}'])
                    T.op('dve', lambda e: e.tensor_tensor(out=x1s[q][:, hs], in0=Op[hf][:], in1=xt[:, hs], op=ALU.add),
                         r=[f'Op{hf}', f'xt{s}'], w=[f'x1s{q}'])
                T.dma('sp', x1_d[i * 128:(i + 1) * 128, :], x1s[q][:], r=[f'x1s{q}'], w=[f'x1_{i}', f'ys_{i}', f'yn_{i}'], slot=f'x1s{q}')

        T.barrier()
        with ExitStack() as e7:
            Wup = sbt(e7, 'Wup', [128, 8, 4096], BF16)
            Wdn = sbt(e7, 'Wdn', [128, 32, D], BF16)
            kUp = load_w(Wup, di['w_up'], D, 4096, 'Wup', gt=g2t)
            kDn = load_w(Wdn, di['w_down'], 4096, D, 'Wdn')
            Up = [pst(e7, f'Up{k}', [128, 512]) for k in range(2)]
            Dn = [pst(e7, f'Dn{k}', [128, 512]) for k in range(2)]
            rl = [sbt(e7, f'rl{k}', [128, 512]) for k in range(2)]
            acT = [sbt(e7, f'acT{k}', [128, 512], BF16) for k in range(2)]
            os_ = [sbt(e7, f'os{k}', [128, D]) for k in range(2)]
            for i in range(NT):
                xt, hT, s = rms_hT(i, x1_d, keysrc=[f'x1_{i}'])
                q = i % 2
                for fg in range(8):
                    u = fg % 2
                    for q4 in range(4):
                        fc = fg * 4 + q4
                        for c in range(8):
                            T.op('pe', lambda e: e.matmul(Up[u][:, q4 * 128:(q4 + 1) * 128], lhsT=Wup[:, c, fc * 128:(fc + 1) * 128],
                                                          rhs=hT[:, c * 128:(c + 1) * 128], start=(c == 0), stop=(c == 7)),
                                 r=[f'hT{s}'] + kUp, w=[f'Up{u}'])
                    T.op('act', lambda e: e.activation(out=rl[u][:], in_=Up[u][:], func=AF.Relu), r=[f'Up{u}'], w=[f'rl{u}'])
                    T.op('dve', lambda e: e.tensor_tensor(out=acT[u][:], in0=rl[u][:], in1=rl[u][:], op=ALU.mult), r=[f'rl{u}'], w=[f'acT{u}'])
                    for q4 in range(4):
                        fc = fg * 4 + q4
                        for hf in range(2):
                            T.op('pe', lambda e: e.matmul(Dn[hf][:], lhsT=acT[u][:, q4 * 128:(q4 + 1) * 128],
                                                          rhs=Wdn[:, fc, hf * 512:(hf + 1) * 512], start=(fc == 0), stop=(fc == 31)),
                                 r=[f'acT{u}'] + kDn, w=[f'Dn{hf}'])
                for hf in range(2):
                    hs = slice(hf * 512, (hf + 1) * 512)
                    T.op('dve', lambda e: e.tensor_tensor(out=os_[q][:, hs], in0=Dn[hf][:], in1=xt[:, hs], op=ALU.add),
                         r=[f'Dn{hf}', f'xt{s}'], w=[f'os{q}'])
                T.dma('sp', out_d[i * 128:(i + 1) * 128, :], os_[q][:], r=[f'os{q}'], w=[f'out_{i}', f'x1_{i}'], slot=f'os{q}')
        T.finish()
        print("instructions:", T.nins)
    return nc, hc


_CACHE = {}


def run(inputs, SEQ, n_cores, dbg=False):
    key = (SEQ, dbg)
    if key not in _CACHE:
        _CACHE[key] = build(SEQ, dbg)
    nc, hc = _CACHE[key]
    L = host_layouts(inputs)
    x = np.asarray(inputs['x'], dtype=np.float32)
    B = x.shape[0]
    in_maps = []
    for core in range(n_cores):
        m = {'x': np.ascontiguousarray(x[core % B])}
        for k, v in hc.items():
            m['c_' + k] = v
        m.update(L)
        in_maps.append(m)
    res = run_bass_kernel_spmd(nc, in_maps, core_ids=list(range(n_cores)))
    return res


def kernel(**inputs):
    x = np.asarray(inputs['x'])
    B, SEQ, _ = x.shape
    res = run(inputs, SEQ, B)
    return np.stack([np.asarray(res.results[b]['out'], dtype=np.float32) for b in range(B)], 0)
```

```python
import os
import numpy as np
import ml_dtypes
from contextlib import ExitStack
import concourse.bass as bass
import concourse.mybir as mybir
from concourse.bass_utils import run_bass_kernel_spmd

F32 = mybir.dt.float32
BF16 = mybir.dt.bfloat16
I32 = mybir.dt.int32
AF = mybir.ActivationFunctionType
ALU = mybir.AluOpType
AX = mybir.AxisListType
SAME_ENGINE_SYNC = True
NEG = -30000.0
EPS = 1e-6
D = 1024
IN_W = 3864
O1, O2, O3, O4 = 512, 1024, 1792, 1816
BF = ml_dtypes.bfloat16


class TS:
    def __init__(self, nc, es):
        self.nc = nc
        self.es = es
        self.E = dict(pe=nc.tensor, act=nc.scalar, dve=nc.vector, pool=nc.gpsimd, sp=nc.sync)
        self.sem = {k: es.enter_context(nc.semaphore("s_" + k)) for k in self.E}
        self.cnt = {k: 0 for k in self.E}
        self.waited = {k: {} for k in self.E}
        self.lastw = {}
        self.readers = {}
        self.nins = 0

    def _need(self, e, reads, writes):
        deps = {}
        for k in reads:
            t = self.lastw.get(k)
            if t is not None and deps.get(t[0], 0) < t[1]:
                deps[t[0]] = t[1]
        for k in writes:
            t = self.lastw.get(k)
            if t is not None and deps.get(t[0], 0) < t[1]:
                deps[t[0]] = t[1]
            for s, v in self.readers.get(k, {}).items():
                if deps.get(s, 0) < v:
                    deps[s] = v
        for s, v in deps.items():
            if s == e and (e == 'pe' or not SAME_ENGINE_SYNC):
                continue
            if self.waited[e].get(s, 0) >= v:
                continue
            self.E[e].wait_ge(self.sem[s], v)
            self.waited[e][s] = v

    def _record(self, tok, reads, writes):
        s, v = tok
        for k in writes:
            self.lastw[k] = tok
            self.readers[k] = {}
        for k in reads:
            d = self.readers.setdefault(k, {})
            if d.get(s, 0) < v:
                d[s] = v

    def op(self, e, fn, r=(), w=()):
        self._need(e, r, w)
        ins = fn(self.E[e])
        self.cnt[e] += 1
        ins.then_inc(self.sem[e], 1)
        self._record((e, self.cnt[e]), r, w)
        self.nins += 1

    def dma(self, q, out, in_, r=(), w=(), slot=None):
        self._need(q, r, w)
        name = 'd_' + slot
        if name not in self.sem:
            self.sem[name] = self.es.enter_context(self.nc.semaphore("s_" + name))
            self.cnt[name] = 0
        if self.cnt[name] > 0 and self.waited[q].get(name, 0) < self.cnt[name]:
            self.E[q].wait_ge(self.sem[name], self.cnt[name])
            self.waited[q][name] = self.cnt[name]
        ins = self.E[q].dma_start(out=out, in_=in_)
        self.cnt[name] += 16
        ins.then_inc(self.sem[name], 16)
        self._record((name, self.cnt[name]), r, w)
        self.nins += 1

    def barrier(self):
        for e in self.E:
            for s, h in self.sem.items():
                if self.cnt[s] > 0 and self.waited[e].get(s, 0) < self.cnt[s]:
                    self.E[e].wait_ge(h, self.cnt[s])
                    self.waited[e][s] = self.cnt[s]

    def finish(self):
        for s, h in self.sem.items():
            if self.cnt[s] > 0 and self.waited['sp'].get(s, 0) < self.cnt[s]:
                self.E['sp'].wait_ge(h, self.cnt[s])


def host_consts(SEQ):
    NT = SEQ // 128
    nblk = SEQ // 16 - 1
    NCT = (nblk + 127) // 128
    n_sel = SEQ // 64
    c = {}
    c['identf'] = np.eye(128, dtype=np.float32)
    i4 = np.tile(np.eye(128, dtype=np.float32), (1, 4))
    c['i4'] = i4.astype(BF)
    qi = np.arange(128)[:, None]
    ki = np.arange(128)[None, :]
    c['negtri'] = np.where(ki > qi, NEG, 0.0).astype(BF)
    c['negtric'] = np.where(ki <= qi, NEG, 0.0).astype(BF)
    keys = np.arange(SEQ)
    c['eall'] = (keys[None, :] // 64 == np.arange(128)[:, None]).astype(np.float32).astype(BF)
    A = np.zeros((128, NCT, 128), np.float32)
    for cc in range(NCT * 128):
        for n in ((cc + 1) // 4, cc // 4):
            if 4 * n - 1 <= cc <= 4 * n + 3 and n < n_sel and n != 127:
                A[cc % 128, cc // 128, n] = 1.0
    A[:, :, 127] = 1.0
    c['aaug'] = A.reshape(128, NCT * 128).astype(BF)
    m = np.arange(-127, 145)[None, :]
    f = ((np.arange(128) + 1) // 16)[:, None]
    c['wc'] = np.where(m > f, NEG, 0.0).astype(BF)
    t = np.arange(SEQ)
    cur = t // 64
    n = np.arange(128)[None, :]
    fb = np.zeros((SEQ, 128), np.float32)
    forced = (n == 0) | (n == cur[:, None]) | (n == cur[:, None] - 1)
    fb[forced] = 1e4
    fb[(n > cur[:, None]) | (n >= n_sel)] = -1e30
    c['fb'] = fb
    kpos = np.stack([64 * (t // 64), t % 64, np.ones_like(t), np.ones_like(t)]).astype(np.float32)
    c['kpos'] = kpos.astype(BF)
    ce = 16 * np.arange(NCT * 128) + 31
    c['cpos'] = np.stack([64 * (ce // 64), ce % 64, np.ones_like(ce), np.ones_like(ce)]).astype(np.float32).astype(BF)
    slopes = 2.0 ** -(np.arange(8) + 1.0)
    qpos = np.zeros((4, 8, SEQ), np.float32)
    qpos[0] = slopes[:, None]
    qpos[1] = slopes[:, None]
    qpos[2] = -slopes[:, None] * (64 * (t // 64))[None, :]
    qpos[3] = -slopes[:, None] * (t % 64)[None, :]
    c['qpos'] = qpos.astype(BF)
    c['tp1'] = np.tile(np.arange(1, 129, dtype=np.float32)[None, :], (128, 1))
    c['ones64'] = np.full((64, 64), 1.0 / 64, np.float32)
    return c


def host_layouts(inp):
    L = {}
    f = lambda a: np.ascontiguousarray(np.asarray(a, dtype=np.float32))
    L['g1'] = f(inp['norm1_g'][0].reshape(8, 128).T)
    L['g2'] = f(inp['norm2_g'][0].reshape(8, 128).T)
    L['w_in'] = f(inp['w_in'][0])
    L['w_glu'] = f(inp['ssm_w_glu'][0])
    L['w_ps'] = f(inp['w_proj_ssm'][0])
    L['w_pn'] = f(inp['w_proj_nsa'][0])
    L['w_out'] = f(inp['w_out'][0])
    L['w_up'] = f(inp['w_up'][0])
    L['w_down'] = f(inp['w_down'][0])
    L['gq'] = f(inp['q_norm_g'][0].reshape(64, 1))
    L['gk'] = f(inp['k_norm_g'][0].T)
    for nm, key in (('w1k', 'cmp_wk1'), ('w1v', 'cmp_wv1')):
        w = np.asarray(inp[key][0]).reshape(32, 64, 128).transpose(1, 0, 2)
        L[nm] = f(np.concatenate([w, w], 0).reshape(128, 32 * 128))
    L['w2k'] = f(inp['cmp_wk2'][0])
    L['w2v'] = f(inp['cmp_wv2'][0])
    L['pek'] = f(np.asarray(inp['cmp_pe_k'][0]).T)
    L['pev'] = f(np.asarray(inp['cmp_pe_v'][0]).T)

    def sm(a):
        a = np.asarray(a)
        return f(a.reshape(16, 2, 64).transpose(1, 2, 0).reshape(128, 16))
    L['are'] = sm(inp['ssm_a_re'][0])
    L['aim'] = sm(inp['ssm_a_im'][0])
    L['ldt'] = sm(np.repeat(np.asarray(inp['ssm_log_dt'][0])[:, None], 64, 1))

    def pad_b(b):
        b = np.asarray(b)
        o = np.zeros((128, 16, 128), np.float32)
        for g in range(32):
            j, gl = g // 2, g % 2
            o[gl * 64:(gl + 1) * 64, j, (j % 4) * 32 + gl * 16:(j % 4) * 32 + gl * 16 + 16] = b[g]
        return o.reshape(128, 2048)
    L['bre'] = pad_b(inp['ssm_b_re'][0])
    L['bim'] = pad_b(inp['ssm_b_im'][0])
    L['cre'] = pad_b(np.asarray(inp['ssm_c_re'][0]).transpose(0, 2, 1))
    L['cim'] = pad_b(np.asarray(inp['ssm_c_im'][0]).transpose(0, 2, 1))
    L['dsk'] = f(inp['ssm_d'][0].reshape(4, 128).T)
    L['bglu'] = f(inp['ssm_b_glu'][0].reshape(4, 128).T)
    return L


def build(SEQ, dbg=False):
    PH = os.environ.get('PHASES', '123S45')
    NT = SEQ // 128
    nblk = SEQ // 16 - 1
    NCT = (nblk + 127) // 128
    nc = bass.Bass("TRN2", target_bir_lowering=False)
    hc = host_consts(SEQ)
    di = {}

    def din(name, shape, dt=F32):
        di[name] = nc.dram_tensor(name, list(shape), dt, kind="ExternalInput").ap()
        return di[name]

    x_d = din('x', [SEQ, D])
    for k, v in hc.items():
        din('c_' + k, v.shape, BF16 if v.dtype == BF else F32)
    shapes = dict(g1=[128, 8], g2=[128, 8], w_in=[D, IN_W], w_glu=[512, 512], w_ps=[512, D], w_pn=[512, D],
                  w_out=[D, D], w_up=[D, 4096], w_down=[4096, D], gq=[64, 1], gk=[64, 3], w1k=[128, 4096],
                  w1v=[128, 4096], w2k=[128, 64], w2v=[128, 64], pek=[64, 32], pev=[64, 32], are=[128, 16],
                  aim=[128, 16], ldt=[128, 16], bre=[128, 2048], bim=[128, 2048], cre=[128, 2048],
                  cim=[128, 2048], dsk=[128, 4], bglu=[128, 4])
    for k, s in shapes.items():
        din(k, s)
    out_d = nc.dram_tensor('out', [SEQ, D], F32, kind="ExternalOutput").ap()
    if dbg:
        yn_t = nc.dram_tensor('yn_scr', [NT, 128, 512], BF16, kind="ExternalOutput").ap()
        ys_t = nc.dram_tensor('ys_scr', [NT, 128, 512], BF16, kind="ExternalOutput").ap()
        x1_d = nc.dram_tensor('x1_scr', [SEQ, D], F32, kind="ExternalOutput").ap()
        yn_d = [yn_t[i] for i in range(NT)]
        ys_d = [ys_t[i] for i in range(NT)]
    else:
        out_bf = out_d.bitcast(BF16)
        yn_d = [out_bf[i * 128:(i + 1) * 128, 0:512] for i in range(NT)]
        ys_d = [out_bf[i * 128:(i + 1) * 128, 512:1024] for i in range(NT)]
        x1_d = out_d

    with ExitStack() as es:
        T = TS(nc, es)

        def sbt(st, n, s, d=F32):
            return st.enter_context(nc.sbuf_tensor('sb_' + n, list(s), d))

        def pst(st, n, s, d=F32):
            return st.enter_context(nc.psum_tensor('ps_' + n, list(s), d))

        identf = sbt(es, 'identf', [128, 128])
        identb = sbt(es, 'identb', [128, 128], BF16)
        i4 = sbt(es, 'i4', [128, 512], BF16)
        negtri = sbt(es, 'negtri', [128, 128], BF16)
        negtric = sbt(es, 'negtric', [128, 128], BF16)
        epsb = sbt(es, 'epsb', [128, 1])
        g1t = sbt(es, 'g1t', [128, 8])
        g2t = sbt(es, 'g2t', [128, 8])
        xts = [sbt(es, f'xt{s}', [128, D]) for s in range(2)]
        hbs = [sbt(es, f'hb{s}', [128, D], BF16) for s in range(2)]
        hTs = [sbt(es, f'hT{s}', [128, D], BF16) for s in range(2)]
        junk = sbt(es, 'junk', [128, D], BF16)
        ssqs = [sbt(es, f'ssq{s}', [128, 1]) for s in range(2)]
        rstds = [sbt(es, f'rstd{s}', [128, 1]) for s in range(2)]
        stg = [sbt(es, f'stg{s}', [128, 1024]) for s in range(2)]
        tp = pst(es, 'tp', [128, 1024], BF16)
        wstate = {'slot': 0}

        T.dma('sp', identf[:], di['c_identf'], w=['identf'], slot='c0')
        T.dma('sp', i4[:], di['c_i4'], w=['i4'], slot='c1')
        T.dma('sp', negtri[:], di['c_negtri'], w=['negtri'], slot='c2')
        T.dma('sp', negtric[:], di['c_negtric'], w=['negtric'], slot='c3')
        T.dma('sp', g1t[:], di['g1'], w=['g1t'], slot='c4')
        T.dma('sp', g2t[:], di['g2'], w=['g2t'], slot='c5')
        T.op('dve', lambda e: e.memset(epsb[:], EPS), w=['epsb'])
        T.op('dve', lambda e: e.tensor_copy(out=identb[:], in_=identf[:]), r=['identf'], w=['identb'])

        def load_cast(dst_ap, src_ap, n, key, scale=None, parts=128):
            s = wstate['slot']
            wstate['slot'] ^= 1
            T.dma('sp', stg[s][0:parts, 0:n], src_ap, w=[f'stg{s}'], slot=f'stg{s}')
            if scale is None:
                T.op('act', lambda e: e.activation(out=dst_ap, in_=stg[s][0:parts, 0:n], func=AF.Copy), r=[f'stg{s}'], w=[key])
            else:
                T.op('act', lambda e: e.activation(out=dst_ap, in_=stg[s][0:parts, 0:n], func=AF.Copy, scale=scale),
                     r=[f'stg{s}', 'g1t', 'g2t'], w=[key])

        def load_w(dst, src, rows, cols, name, col0=0, gt=None):
            keys = []
            for c in range(rows // 128):
                for a in range(0, cols, 1024):
                    n = min(1024, cols - a)
                    k = f'{name}_{c}_{a}'
                    load_cast(dst[:, c, a:a + n], src[c * 128:(c + 1) * 128, col0 + a:col0 + a + n], n, k,
                              scale=None if gt is None else gt[:, c:c + 1])
                    keys.append(k)
            return keys

        def rms_hT(i, src_d, keysrc=()):
            s = i % 2
            xt, hb, hT = xts[s], hbs[s], hTs[s]
            T.dma('sp', xt[:], src_d[i * 128:(i + 1) * 128, :], r=list(keysrc), w=[f'xt{s}'], slot=f'xt{s}')
            T.op('act', lambda e: e.activation(out=junk[:], in_=xt[:], func=AF.Square, accum_out=ssqs[s][:]),
                 r=[f'xt{s}'], w=['junk', f'ssq{s}'])
            T.op('act', lambda e: e.activation(out=rstds[s][:], in_=ssqs[s][:], func=AF.Sqrt, bias=epsb[:], scale=1.0 / D),
                 r=[f'ssq{s}', 'epsb'], w=[f'rstd{s}'])
            T.op('dve', lambda e: e.reciprocal(out=rstds[s][:], in_=rstds[s][:]), r=[f'rstd{s}'], w=[f'rstd{s}'])
            T.op('act', lambda e: e.activation(out=hb[:], in_=xt[:], func=AF.Copy, scale=rstds[s][:]),
                 r=[f'xt{s}', f'rstd{s}'], w=[f'hb{s}'])
            for c in range(8):
                T.op('pe', lambda e: e.transpose(tp[:, c * 128:(c + 1) * 128], hb[:, c * 128:(c + 1) * 128], identb[:]),
                     r=[f'hb{s}', 'identb'], w=['tp'])
            T.op('dve', lambda e: e.tensor_copy(out=hT[:], in_=tp[:]), r=['tp'], w=[f'hT{s}'])
            return xt, hT, s

        with ExitStack() as ea:
            KselT = [sbt(ea, f'KselT{g}', [68, SEQ], BF16) for g in range(2)]
            KwinT = [sbt(ea, f'KwinT{g}', [68, SEQ], BF16) for g in range(2)]
            Vsel = sbt(ea, 'Vsel', [128, NT * 130], BF16)
            Vwin = sbt(ea, 'Vwin', [128, NT * 130], BF16)
            KcT = [sbt(ea, f'KcT{g}', [68, NCT * 128], BF16) for g in range(2)]
            Vc = sbt(ea, 'Vc', [128, NCT * 130], BF16)
            gqt = sbt(ea, 'gqt', [64, 1])
            gq8 = sbt(ea, 'gq8', [64, 1])
            gkt = sbt(ea, 'gkt', [64, 3])
            T.dma('sp', gqt[:], di['gq'], w=['gqt'], slot='c0')
            T.dma('sp', gkt[:], di['gk'], w=['gkt'], slot='c1')
            T.op('dve', lambda e: e.tensor_scalar(out=gq8[:], in0=gqt[:], scalar1=0.125, scalar2=None, op0=ALU.mult), r=['gqt'], w=['gq8'])
            for g in range(2):
                T.dma('sp', KselT[g][64:68, :], di['c_kpos'], w=[f'KselTp{g}'], slot='c2')
                T.dma('sp', KwinT[g][64:68, :], di['c_kpos'], w=[f'KwinTp{g}'], slot='c3')
                T.dma('sp', KcT[g][64:68, :], di['c_cpos'], w=[f'KcTp{g}'], slot='c4')
            T.op('pool', lambda e: e.memset(Vsel[:], 1.0), w=['Vsel_init'])
            T.op('pool', lambda e: e.memset(Vwin[:], 1.0), w=['Vwin_init'])
            T.op('pool', lambda e: e.memset(Vc[:], 1.0), w=['Vc_init'])

            with ExitStack() as e1:
                Wkv = sbt(e1, 'Wkv', [128, 8, 768], BF16)
                rawk = sbt(e1, 'rawk', [128, SEQ], BF16)
                rawv = sbt(e1, 'rawv', [128, SEQ], BF16)
                kvp = pst(e1, 'kvp', [128, 512])
                cmp_ = pst(e1, 'cmp_', [128, 512])
                tp2 = pst(e1, 'tp2', [128, 1024], BF16)
                sq = sbt(e1, 'sq', [128, 256])
                ssk = sbt(e1, 'ssk', [128, 4])
                rsk = sbt(e1, 'rsk', [128, 4])
                kn = sbt(e1, 'kn', [128, 256], BF16)
                kW = load_w(Wkv, di['w_in'], D, 768, 'Wkv', col0=O2, gt=g1t)
                for i in range(NT if '1' in PH else 0):
                    xt, hT, s = rms_hT(i, x_d)
                    cs = slice(i * 128, (i + 1) * 128)
                    for c in range(8):
                        T.op('pe', lambda e: e.matmul(kvp[:], lhsT=hT[:, c * 128:(c + 1) * 128], rhs=Wkv[:, c, 256:768],
                                                      start=(c == 0), stop=(c == 7)), r=[f'hT{s}'] + kW, w=['kvp'])
                    for half in range(2):
                        for c in range(8):
                            T.op('pe', lambda e: e.matmul(cmp_[:, half * 128:(half + 1) * 128],
                                                          lhsT=Wkv[:, c, half * 128:(half + 1) * 128],
                                                          rhs=hT[:, c * 128:(c + 1) * 128], start=(c == 0), stop=(c == 7)),
                                 r=[f'hT{s}'] + kW, w=['cmp_'])
                    T.op('act', lambda e: e.activation(out=sq[:, 0:128], in_=kvp[:, 0:128], func=AF.Square), r=['kvp'], w=['sq'])
                    T.op('act', lambda e: e.activation(out=sq[:, 128:256], in_=kvp[:, 256:384], func=AF.Square), r=['kvp'], w=['sq'])
                    T.op('dve', lambda e: e.tensor_reduce(out=ssk[:], in_=sq[:].rearrange("p (h d) -> p h d", h=4), axis=AX.X, op=ALU.add),
                         r=['sq'], w=['ssk'])
                    T.op('act', lambda e: e.activation(out=rsk[:], in_=ssk[:], func=AF.Sqrt, bias=epsb[:], scale=1.0 / 64),
                         r=['ssk', 'epsb'], w=['rsk'])
                    T.op('dve', lambda e: e.reciprocal(out=rsk[:], in_=rsk[:]), r=['rsk'], w=['rsk'])
                    for b in range(2):
                        T.op('dve', lambda e: e.tensor_tensor(
                            out=kn[:, b * 128:(b + 1) * 128].rearrange("p (h d) -> p h d", h=2),
                            in0=kvp[:, b * 256:b * 256 + 128].rearrange("p (h d) -> p h d", h=2),
                            in1=rsk[:, 2 * b:2 * b + 2].unsqueeze(2).broadcast_to([128, 2, 64]), op=ALU.mult),
                            r=['kvp', 'rsk'], w=['kn'])
                    for j in range(4):
                        T.op('pe', lambda e: e.transpose(tp2[0:64, j * 128:(j + 1) * 128], kn[:, j * 64:(j + 1) * 64], identb[:]),
                             r=['kn', 'identb'], w=['tp2'])
                    for j in range(4):
                        br, g = (1, j) if j < 2 else (2, j - 2)
                        dst = (KselT if br == 1 else KwinT)[g]
                        T.op('act', lambda e: e.activation(out=dst[0:64, cs], in_=tp2[0:64, j * 128:(j + 1) * 128], func=AF.Copy,
                                                           scale=gkt[:, br:br + 1]), r=['tp2', 'gkt'],
                             w=[f'K{br}_{g}_{i}'])
                    for b, Vt, nm in ((0, Vsel, 'Vs'), (1, Vwin, 'Vw')):
                        T.op('dve', lambda e: e.tensor_copy(
                            out=Vt[:, i * 130:(i + 1) * 130].rearrange("p (g d) -> p g d", g=2)[:, :, 0:64],
                            in_=kvp[:, b * 256 + 128:b * 256 + 256].rearrange("p (g d) -> p g d", g=2)),
                            r=['kvp', 'Vsel_init', 'Vwin_init'], w=[f'{nm}_{i}'])
                    T.op('act', lambda e: e.activation(out=rawk[:, cs], in_=cmp_[:, 0:128], func=AF.Copy), r=['cmp_'], w=['rawk'])
                    T.op('act', lambda e: e.activation(out=rawv[:, cs], in_=cmp_[:, 128:256], func=AF.Copy), r=['cmp_'], w=['rawv'])

                W1 = {'k': sbt(e1, 'W1k', [128, 4096], BF16), 'v': sbt(e1, 'W1v', [128, 4096], BF16)}
                W2 = {'k': sbt(e1, 'W2k', [128, 64], BF16), 'v': sbt(e1, 'W2v', [128, 64], BF16)}
                pe_ = {'k': sbt(e1, 'pek', [64, 32], BF16), 'v': sbt(e1, 'pev', [64, 32], BF16)}
                ones64 = sbt(e1, 'ones64', [64, 64])
                T.dma('sp', ones64[:], di['c_ones64'], w=['ones64'], slot='c0')
                for kd in 'kv':
                    for a in range(4):
                        load_cast(W1[kd][:, a * 1024:(a + 1) * 1024], di['w1' + kd][:, a * 1024:(a + 1) * 1024], 1024, 'W1' + kd)
                    load_cast(W2[kd][:], di['w2' + kd], 64, 'W2' + kd)
                    load_cast(pe_[kd][:], di['pe' + kd], 32, 'pe' + kd, parts=64)
                b1p = pst(e1, 'b1p', [128, 512])
                b1s = sbt(e1, 'b1s', [128, 1])
                hid = sbt(e1, 'hid', [128, NCT * 128], BF16)
                sqc = sbt(e1, 'sqc', [64, 512])
                rsc = sbt(e1, 'rsc', [64, 512])
                T.op('pool', lambda e: e.memset(hid[:], 0.0), w=['hid'])
                for g in range(2):
                    T.op('pool', lambda e: e.memset(KcT[g][0:64, :], 0.0), w=[f'KcT{g}'])
                for kd in ('kv' if '2' in PH else ''):
                    raw = rawk if kd == 'k' else rawv
                    for r_ in range(32):
                        T.op('pe', lambda e: e.matmul(b1p[:, 0:1], lhsT=W1[kd][0:64, r_ * 128:(r_ + 1) * 128],
                                                      rhs=pe_[kd][0:64, r_:r_ + 1], start=(r_ == 0), stop=(r_ == 31)),
                             r=['W1' + kd, 'pe' + kd], w=['b1p'])
                    T.op('dve', lambda e: e.tensor_copy(out=b1s[:], in_=b1p[:, 0:1]), r=['b1p'], w=['b1s'])
                    for g in range(2):
                        ps_ = slice(g * 64, (g + 1) * 64)
                        for r_ in range(32):
                            T.op('pe', lambda e: e.matmul(kvp[:, 0:nblk], lhsT=W1[kd][ps_, r_ * 128:(r_ + 1) * 128],
                                                          rhs=raw[ps_, r_:r_ + 16 * (nblk - 1) + 1:16],
                                                          start=(r_ == 0), stop=(r_ == 31)),
                                 r=['W1' + kd, 'rawk', 'rawv'], w=['kvp'])
                        T.op('act', lambda e: e.activation(out=hid[:, 0:nblk], in_=kvp[:, 0:nblk], func=AF.Gelu_apprx_tanh, bias=b1s[:]),
                             r=['kvp', 'b1s'], w=['hid'])
                        if kd == 'k':
                            T.op('pe', lambda e: e.matmul(cmp_[0:64, 0:nblk], lhsT=W2['k'][:, :], rhs=hid[:, 0:nblk], start=True, stop=True),
                                 r=['W2k', 'hid'], w=['cmp_'])
                            T.op('act', lambda e: e.activation(out=sqc[:, 0:nblk], in_=cmp_[0:64, 0:nblk], func=AF.Square), r=['cmp_'], w=['sqc'])
                            T.op('pe', lambda e: e.matmul(b1p[0:64, 0:nblk], lhsT=ones64[:, :], rhs=sqc[:, 0:nblk], start=True, stop=True),
                                 r=['ones64', 'sqc'], w=['b1p'])
                            T.op('act', lambda e: e.activation(out=rsc[:, 0:nblk], in_=b1p[0:64, 0:nblk], func=AF.Sqrt, bias=epsb[0:64, :]),
                                 r=['b1p', 'epsb'], w=['rsc'])
                            T.op('dve', lambda e: e.reciprocal(out=rsc[:, 0:nblk], in_=rsc[:, 0:nblk]), r=['rsc'], w=['rsc'])
                            T.op('dve', lambda e: e.scalar_tensor_tensor(out=KcT[g][0:64, 0:nblk], in0=cmp_[0:64, 0:nblk], scalar=gkt[:, 0:1],
                                                                         in1=rsc[:, 0:nblk], op0=ALU.mult, op1=ALU.mult),
                                 r=['cmp_', 'gkt', 'rsc'], w=[f'KcT{g}'])
                        else:
                            for ct in range(NCT):
                                T.op('pe', lambda e: e.matmul(cmp_[:, 0:64], lhsT=hid[:, ct * 128:(ct + 1) * 128], rhs=W2['v'][:, :],
                                                              start=True, stop=True), r=['W2v', 'hid'], w=['cmp_'])
                                T.op('dve', lambda e: e.tensor_copy(out=Vc[:, (ct * 2 + g) * 65:(ct * 2 + g) * 65 + 64], in_=cmp_[:, 0:64]),
                                     r=['cmp_', 'Vc_init'], w=[f'Vc{g}'])

            if dbg and not os.environ.get('NODUMP'):
                for g in range(2):
                    dk = nc.dram_tensor(f'dbg_kc{g}', [68, NCT * 128], BF16, kind="ExternalOutput").ap()
                    T.dma('sp', dk, KcT[g][:], r=[f'KcT{g}', f'KcTp{g}'], slot=f'dbgk{g}')
                dv = nc.dram_tensor('dbg_vc', [128, NCT * 130], BF16, kind="ExternalOutput").ap()
                T.dma('sp', dv, Vc[:], r=['Vc0', 'Vc1', 'Vc_init'], slot='dbgv')
            T.barrier()
            with ExitStack() as e3:
                Wq = sbt(e3, 'Wq', [128, 8, 536], BF16)
                eall = sbt(e3, 'eall', [128, SEQ], BF16)
                aaug = sbt(e3, 'aaug', [128, NCT * 128], BF16)
                wc = sbt(e3, 'wc', [128, 272], BF16)
                T.dma('sp', eall[:], di['c_eall'], w=['eall'], slot='c0')
                T.dma('sp', aaug[:], di['c_aaug'], w=['aaug'], slot='c1')
                T.dma('sp', wc[:], di['c_wc'], w=['wc'], slot='c2')
                kQ = load_w(Wq, di['w_in'], D, 512, 'Wq', col0=O1, gt=g1t)
                kQ += load_w(Wq[:, :, 512:536], di['w_in'], D, 24, 'Wg', col0=O3, gt=g1t)
                bankQ = pst(e3, 'bankQ', [128, 512])
                bankG = pst(e3, 'bankG', [128, 512])
                Sb = [pst(e3, f'S{k}', [128, 512]) for k in range(2)]
                Ob = [pst(e3, f'O{k}', [128, 512]) for k in range(2)]
                IMP = pst(e3, 'IMP', [128, 512])
                Pb = [sbt(e3, f'P{k}', [128, 512], BF16) for k in range(3)]
                QTs = [sbt(e3, f'QT{k}', [68, 1024], BF16) for k in range(2)]
                Pc = [sbt(e3, f'Pc{k}', [128, 512], BF16) for k in range(NCT)]
                sqq = sbt(e3, 'sqq', [128, 512])
                ss8 = sbt(e3, 'ss8', [128, 8])
                rq = sbt(e3, 'rq', [128, 8])
                qn = sbt(e3, 'qn', [128, 512], BF16)
                gsig = sbt(e3, 'gsig', [128, 24])
                fbs = [sbt(e3, f'fb{k}', [128, 128]) for k in range(2)]
                den4 = sbt(e3, 'den4', [128, 4])
                impa = sbt(e3, 'impa', [128, 128])
                m1 = sbt(e3, 'm1', [128, 8])
                m2 = sbt(e3, 'm2', [128, 8])
                tmpm = sbt(e3, 'tmpm', [128, 128])
                negm = sbt(e3, 'negm', [128, 128], BF16)
                nmT4 = [sbt(e3, f'nmT4{k}', [128, 512], BF16) for k in range(2)]
                osb = sbt(e3, 'osb', [65, 512])
                dd = sbt(e3, 'dd', [128, 4])
                rr = sbt(e3, 'rr', [128, 4])
                ynsa = sbt(e3, 'ynsa', [128, 512])
                ynb = sbt(e3, 'ynb', [128, 512], BF16)
                ynT = [sbt(e3, f'ynT{k}', [128, 512], BF16) for k in range(2)]
                cnt = {'s': 0, 'o': 0}

                pend = []

                def emit_S(u):
                    k = cnt['s'] % 2
                    kp = cnt['s'] % 3
                    cnt['s'] += 1
                    u['kp'] = kp
                    S, P = Sb[k], Pb[kp]
                    masks = u['masks']
                    T.op('pe', lambda e: e.matmul(S[:], lhsT=u['KT'][0:68, u['lo']:u['lo'] + 128],
                                                  rhs=u['QT'][0:68, u['g'] * 512:(u['g'] + 1) * 512],
                                                  start=True, stop=(len(masks) == 0)), r=u['kkeys'] + [u['qk']], w=[f'S{k}'])
                    for mi, (ml, mr, mk) in enumerate(masks):
                        T.op('pe', lambda e: e.matmul(S[:], lhsT=ml, rhs=mr, start=False, stop=(mi == len(masks) - 1)),
                             r=mk, w=[f'S{k}'])
                    T.op('act', lambda e: e.activation(out=P[:], in_=S[:], func=AF.Exp), r=[f'S{k}'], w=[f'P{kp}'])

                def emit_PV(u):
                    kp = u['kp']
                    P = Pb[kp]
                    O = Ob[u['O']]
                    T.op('pe', lambda e: e.matmul(O[0:65, :], lhsT=u['Vt'][:, u['voff']:u['voff'] + 65], rhs=P[:],
                                                  start=u['first'], stop=u['last']), r=[f'P{kp}'] + u['vkeys'], w=[f"O{u['O']}"])
                    if u.get('imp_ct') is not None:
                        ic = u['imp_ct']
                        T.op('pool', lambda e: e.tensor_copy(out=Pc[ic][:], in_=P[:]), r=[f'P{kp}'], w=[f'Pc{ic}'])
                    if u.get('post') is not None:
                        u['post']()

                def push(u):
                    emit_S(u)
                    if pend:
                        emit_PV(pend.pop())
                    pend.append(u)

                def flush():
                    while pend:
                        emit_PV(pend.pop())

                DBG_BR = int(os.environ.get('DBG_BR', '-1'))

                def combine(oi, g, br):
                    if DBG_BR >= 0 and br != DBG_BR:
                        return
                    Oacc = Ob[oi]
                    T.op('act', lambda e: e.activation(out=osb[:], in_=Oacc[0:65, :], func=AF.Copy), r=[f'O{oi}'], w=['osb'])
                    for hg in range(4):
                        T.op('pe', lambda e: e.transpose(bankG[:, hg * 65:(hg + 1) * 65], osb[0:65, hg * 128:(hg + 1) * 128], identf[0:65, 0:65]),
                             r=['osb', 'identf'], w=['bankG'])
                    ot = bankG[:, 0:260].rearrange("p (h d) -> p h d", h=4)
                    T.op('dve', lambda e: e.tensor_scalar(out=dd[:].unsqueeze(2), in0=ot[:, :, 64:65], scalar1=1e-30, scalar2=None, op0=ALU.max),
                         r=['bankG'], w=['dd'])
                    T.op('dve', lambda e: e.reciprocal(out=dd[:], in_=dd[:]), r=['dd'], w=['dd'])
                    if DBG_BR >= 0:
                        T.op('dve', lambda e: e.tensor_copy(out=rr[:], in_=dd[:]), r=['dd'], w=['rr'])
                    else:
                        T.op('dve', lambda e: e.tensor_tensor(out=rr[:], in0=dd[:], in1=gsig[:, br * 8 + g * 4:br * 8 + g * 4 + 4], op=ALU.mult),
                             r=['dd', 'gsig'], w=['rr'])
                    for hg in range(4):
                        hh = g * 4 + hg
                        ysl = ynsa[:, hh * 64:(hh + 1) * 64]
                        if br == 0 or DBG_BR >= 0:
                            T.op('dve', lambda e: e.tensor_scalar(out=ysl, in0=bankG[:, hg * 65:hg * 65 + 64], scalar1=rr[:, hg:hg + 1],
                                                                  scalar2=None, op0=ALU.mult), r=['bankG', 'rr'], w=['ynsa'])
                        else:
                            T.op('dve', lambda e: e.scalar_tensor_tensor(out=ysl, in0=bankG[:, hg * 65:hg * 65 + 64], scalar=rr[:, hg:hg + 1],
                                                                         in1=ysl, op0=ALU.mult, op1=ALU.add), r=['bankG', 'rr', 'ynsa'], w=['ynsa'])

                for i in range(NT if '3' in PH else 0):
                    xt, hT, s = rms_hT(i, x_d)
                    QT = QTs[i % 2]
                    qk = f'QT{i % 2}'
                    fb = fbs[i % 2]
                    for c in range(8):
                        T.op('pe', lambda e: e.matmul(bankQ[:], lhsT=hT[:, c * 128:(c + 1) * 128], rhs=Wq[:, c, 0:512],
                                                      start=(c == 0), stop=(c == 7)), r=[f'hT{s}'] + kQ, w=['bankQ'])
                    for c in range(8):
                        T.op('pe', lambda e: e.matmul(bankG[:, 0:24], lhsT=hT[:, c * 128:(c + 1) * 128], rhs=Wq[:, c, 512:536],
                                                      start=(c == 0), stop=(c == 7)), r=[f'hT{s}'] + kQ, w=['bankG'])
                    T.op('act', lambda e: e.activation(out=sqq[:], in_=bankQ[:], func=AF.Square), r=['bankQ'], w=['sqq'])
                    T.op('dve', lambda e: e.tensor_reduce(out=ss8[:], in_=sqq[:].rearrange("p (h d) -> p h d", h=8), axis=AX.X, op=ALU.add),
                         r=['sqq'], w=['ss8'])
                    T.op('act', lambda e: e.activation(out=rq[:], in_=ss8[:], func=AF.Sqrt, bias=epsb[:], scale=1.0 / 64), r=['ss8', 'epsb'], w=['rq'])
                    T.op('dve', lambda e: e.reciprocal(out=rq[:], in_=rq[:]), r=['rq'], w=['rq'])
                    T.op('dve', lambda e: e.tensor_tensor(out=qn[:].rearrange("p (h d) -> p h d", h=8),
                                                          in0=bankQ[:].rearrange("p (h d) -> p h d", h=8),
                                                          in1=rq[:].unsqueeze(2).broadcast_to([128, 8, 64]), op=ALU.mult),
                         r=['bankQ', 'rq'], w=['qn'])
                    for h in range(8):
                        T.op('pe', lambda e: e.transpose(tp[0:64, h * 128:(h + 1) * 128], qn[:, h * 64:(h + 1) * 64], identb[:]),
                             r=['qn', 'identb'], w=['tp'])
                    T.op('act', lambda e: e.activation(out=QT[0:64, :], in_=tp[0:64, :], func=AF.Copy, scale=gq8[:]), r=['tp', 'gq8'], w=[qk])
                    T.dma('sp', QT[64:68, :].rearrange("p (h q) -> p h q", h=8), di['c_qpos'][:, :, i * 128:(i + 1) * 128],
                          w=[qk + 'p'], slot=qk + 'p')
                    qk2 = [qk, qk + 'p']
                    T.op('act', lambda e: e.activation(out=gsig[:], in_=bankG[:, 0:24], func=AF.Sigmoid), r=['bankG'], w=['gsig'])
                    T.dma('sp', fb[:], di['c_fb'][i * 128:(i + 1) * 128, :], w=[f'fb{i % 2}'], slot=f'fb{i % 2}')
                    if dbg and not os.environ.get('NODUMP'):
                        if i == 0:
                            dgs = nc.dram_tensor('dbg_gsig', [NT * 128, 24], F32, kind="ExternalOutput").ap()
                        T.dma('sp', dgs[i * 128:(i + 1) * 128, :], gsig[:], r=['gsig'], slot='dbggs')
                    for g in range(2):
                        n_ct = min(NCT, (8 * i + 6) // 128 + 1)

                        def post_cmp(oi, g=g, n_ct=n_ct, fb=fb, i=i):
                            combine(oi, g, 0)
                            for h in range(4):
                                for ct in range(n_ct):
                                    T.op('pe', lambda e: e.matmul(IMP[:, h * 128:(h + 1) * 128], lhsT=Pc[ct][:, h * 128:(h + 1) * 128],
                                                                  rhs=aaug[:, ct * 128:(ct + 1) * 128], start=(ct == 0), stop=(ct == n_ct - 1)),
                                         r=[f'Pc{ct}', 'aaug'], w=['IMP'])
                            iv = IMP[:].rearrange("p (h n) -> p h n", h=4)
                            T.op('dve', lambda e: e.tensor_scalar(out=den4[:].unsqueeze(2), in0=iv[:, :, 127:128], scalar1=1e-30, scalar2=None, op0=ALU.max),
                                 r=['IMP'], w=['den4'])
                            T.op('dve', lambda e: e.reciprocal(out=den4[:], in_=den4[:]), r=['den4'], w=['den4'])
                            T.op('dve', lambda e: e.tensor_scalar(out=impa[:], in0=IMP[:, 0:128], scalar1=den4[:, 0:1], scalar2=None, op0=ALU.mult),
                                 r=['IMP', 'den4'], w=['impa'])
                            for h in range(1, 4):
                                T.op('dve', lambda e: e.scalar_tensor_tensor(out=impa[:], in0=IMP[:, h * 128:(h + 1) * 128], scalar=den4[:, h:h + 1],
                                                                             in1=impa[:], op0=ALU.mult, op1=ALU.add), r=['IMP', 'den4', 'impa'], w=['impa'])
                            T.op('dve', lambda e: e.tensor_tensor(out=impa[:], in0=impa[:], in1=fb[:], op=ALU.add), r=['impa', f'fb{i % 2}'], w=['impa'])
                            T.op('dve', lambda e: e.max(out=m1[:], in_=impa[:]), r=['impa'], w=['m1'])
                            T.op('dve', lambda e: e.match_replace(out=tmpm[:], in_to_replace=m1[:], in_values=impa[:], imm_value=-3e38),
                                 r=['impa', 'm1'], w=['tmpm'])
                            T.op('dve', lambda e: e.max(out=m2[:], in_=tmpm[:]), r=['tmpm'], w=['m2'])
                            T.op('dve', lambda e: e.tensor_scalar(out=negm[:], in0=impa[:], scalar1=m2[:, 7:8], scalar2=NEG, op0=ALU.is_lt, op1=ALU.mult),
                                 r=['impa', 'm2'], w=['negm'])
                            T.op('pe', lambda e: e.transpose(tp[:, 0:128], negm[:], identb[:]), r=['negm', 'identb'], w=['tp'])
                            T.op('dve', lambda e: e.tensor_copy(out=nmT4[g][:].rearrange("p (h q) -> p h q", h=4),
                                                                in_=tp[:, 0:128].unsqueeze(1).broadcast_to([128, 4, 128])), r=['tp'], w=[f'nmT4{g}'])

                        oi = cnt['o'] % 2
                        cnt['o'] += 1
                        for ct in range(n_ct):
                            off = 128 * ct - 8 * i + 2
                            masks = []
                            if off > -127:
                                masks.append((wc[:, 127 + off:127 + off + 128], i4[:], ['wc', 'i4']))
                            push(dict(KT=KcT[g], QT=QT, g=g, qk=qk, lo=ct * 128, masks=masks, Vt=Vc, voff=(ct * 2 + g) * 65, O=oi,
                                      first=(ct == 0), last=(ct == n_ct - 1), kkeys=[f'KcT{g}', f'KcTp{g}', qk + 'p'],
                                      vkeys=[f'Vc{g}', 'Vc_init'], imp_ct=ct,
                                      post=(lambda oi=oi: post_cmp(oi)) if ct == n_ct - 1 else None))
                        oi = cnt['o'] % 2
                        cnt['o'] += 1
                        k0 = max(0, i - 4)
                        for kt in range(k0, i + 1):
                            masks = []
                            if kt == i:
                                masks.append((negtri[:], i4[:], ['negtri', 'i4']))
                            if kt == i - 4:
                                masks.append((negtric[:], i4[:], ['negtric', 'i4']))
                            push(dict(KT=KwinT[g], QT=QT, g=g, qk=qk, lo=kt * 128, masks=masks, Vt=Vwin, voff=kt * 130 + g * 65, O=oi,
                                      first=(kt == k0), last=(kt == i), kkeys=[f'K2_{g}_{kt}', f'KwinTp{g}', qk + 'p'], vkeys=[f'Vw_{kt}'],
                                      post=(lambda oi=oi, g=g: combine(oi, g, 2)) if kt == i else None))
                        flush()
                        oi = cnt['o'] % 2
                        cnt['o'] += 1
                        for kt in range(i + 1):
                            masks = [(eall[:, kt * 128:(kt + 1) * 128], nmT4[g][:], ['eall', f'nmT4{g}'])]
                            if kt == i:
                                masks.append((negtri[:], i4[:], ['negtri', 'i4']))
                            push(dict(KT=KselT[g], QT=QT, g=g, qk=qk, lo=kt * 128, masks=masks, Vt=Vsel, voff=kt * 130 + g * 65, O=oi,
                                      first=(kt == 0), last=(kt == i), kkeys=[f'K1_{g}_{kt}', f'KselTp{g}', qk + 'p'], vkeys=[f'Vs_{kt}'],
                                      post=(lambda oi=oi, g=g: combine(oi, g, 1)) if kt == i else None))
                    flush()
                    yT = ynT[i % 2]
                    T.op('act', lambda e: e.activation(out=ynb[:], in_=ynsa[:], func=AF.Copy), r=['ynsa'], w=['ynb'])
                    for c in range(4):
                        T.op('pe', lambda e: e.transpose(tp[:, c * 128:(c + 1) * 128], ynb[:, c * 128:(c + 1) * 128], identb[:]),
                             r=['ynb', 'identb'], w=['tp'])
                    T.op('dve', lambda e: e.tensor_copy(out=yT[:], in_=tp[:, 0:512]), r=['tp'], w=[f'ynT{i % 2}'])
                    T.dma('sp', yn_d[i], yT[:], r=[f'ynT{i % 2}'], w=[f'yn_{i}'], slot=f'ynT{i % 2}')

        T.barrier()
        with ExitStack() as e4:
            ctab = sbt(e4, 'ctab', [128, 2048])
            stab = sbt(e4, 'stab', [128, 2048])
            BbT = [sbt(e4, f'BbT{k}', [128, 2048], BF16) for k in range(2)]
            Cb = [sbt(e4, f'Cb{k}', [128, 2048], BF16) for k in range(2)]
            mag = sbt(e4, 'mag', [128, 16])
            rotc = sbt(e4, 'rotc', [128, 16])
            rots = sbt(e4, 'rots', [128, 16])
            dsk = sbt(e4, 'dsk', [128, 4])
            bglu = sbt(e4, 'bglu', [128, 4])
            T.dma('sp', dsk[:], di['dsk'], w=['dsk'], slot='c0')
            T.dma('sp', bglu[:], di['bglu'], w=['bglu'], slot='c1')
            Wu = sbt(e4, 'Wu', [128, 8, 512], BF16)
            Wgl = sbt(e4, 'Wgl', [128, 4, 512], BF16)
            kU = load_w(Wu, di['w_in'], D, 512, 'Wu', col0=0, gt=g1t)
            kGl = load_w(Wgl, di['w_glu'], 512, 512, 'Wgl')

            def sincos(st, phi, n, s_out, c_out, nm):
                u = sbt(st, nm + 'u', [128, n])
                ki = sbt(st, nm + 'ki', [128, n], I32)
                kf = sbt(st, nm + 'kf', [128, n])
                fx = sbt(st, nm + 'fx', [128, n])
                r_ = sbt(st, nm + 'r', [128, n])
                for shift, dst in ((0.0, s_out), (np.pi / 2, c_out)):
                    T.op('dve', lambda e: e.tensor_scalar(out=u[:], in0=phi, scalar1=shift, scalar2=1.0 / (2 * np.pi), op0=ALU.add, op1=ALU.mult),
                         r=[nm + 'phi'], w=[nm + 'u'])
                    T.op('dve', lambda e: e.tensor_copy(out=ki[:], in_=u[:]), r=[nm + 'u'], w=[nm + 'ki'])
                    T.op('dve', lambda e: e.tensor_copy(out=kf[:], in_=ki[:]), r=[nm + 'ki'], w=[nm + 'kf'])
                    T.op('dve', lambda e: e.tensor_scalar(out=r_[:], in0=phi, scalar1=shift, scalar2=None, op0=ALU.add), r=[nm + 'phi'], w=[nm + 'r'])
                    T.op('dve', lambda e: e.scalar_tensor_tensor(out=r_[:], in0=kf[:], scalar=-2 * np.pi, in1=r_[:], op0=ALU.mult, op1=ALU.add),
                         r=[nm + 'kf', nm + 'r'], w=[nm + 'r'])
                    T.op('dve', lambda e: e.tensor_scalar(out=fx[:], in0=r_[:], scalar1=np.pi, scalar2=-2 * np.pi, op0=ALU.is_gt, op1=ALU.mult),
                         r=[nm + 'r'], w=[nm + 'fx'])
                    T.op('dve', lambda e: e.tensor_tensor(out=r_[:], in0=r_[:], in1=fx[:], op=ALU.add), r=[nm + 'r', nm + 'fx'], w=[nm + 'r'])
                    T.op('dve', lambda e: e.tensor_scalar(out=fx[:], in0=r_[:], scalar1=-np.pi, scalar2=2 * np.pi, op0=ALU.is_lt, op1=ALU.mult),
                         r=[nm + 'r'], w=[nm + 'fx'])
                    T.op('dve', lambda e: e.tensor_tensor(out=r_[:], in0=r_[:], in1=fx[:], op=ALU.add), r=[nm + 'r', nm + 'fx'], w=[nm + 'r'])
                    T.op('dve', lambda e: e.tensor_scalar(out=r_[:], in0=r_[:], scalar1=3.14159, scalar2=-3.14159, op0=ALU.min, op1=ALU.max),
                         r=[nm + 'r'], w=[nm + 'r'])
                    T.op('act', lambda e: e.activation(out=dst, in_=r_[:], func=AF.Sin), r=[nm + 'r'], w=[nm + 'out'])

            with ExitStack() as e0:
                sm = lambda n, s=[128, 16]: sbt(e0, n, s)
                are, aim, ldt = sm('are'), sm('aim'), sm('ldt')
                T.dma('sp', are[:], di['are'], w=['are'], slot='c2')
                T.dma('sp', aim[:], di['aim'], w=['aim'], slot='c3')
                T.dma('sp', ldt[:], di['ldt'], w=['ldt'], slot='c4')
                dt_, adr, adi, cs_, sn_ = sm('dt_'), sm('adr'), sm('adi'), sm('cs_'), sm('sn_')
                abr, abi, nr, den, fr, fi, t0, t1 = sm('abr'), sm('abi'), sm('nr'), sm('den'), sm('fr'), sm('fi'), sm('t0'), sm('t1')
                a128 = sm('a128')
                V = lambda fn, r, w: T.op('dve', fn, r=r, w=w)
                T.op('act', lambda e: e.activation(out=dt_[:], in_=ldt[:], func=AF.Exp), r=['ldt'], w=['dt_'])
                V(lambda e: e.tensor_tensor(out=adr[:], in0=are[:], in1=dt_[:], op=ALU.mult), ['are', 'dt_'], ['adr'])
                V(lambda e: e.tensor_tensor(out=adi[:], in0=aim[:], in1=dt_[:], op=ALU.mult), ['aim', 'dt_'], ['s0phi'])
                T.op('act', lambda e: e.activation(out=mag[:], in_=adr[:], func=AF.Exp), r=['adr'], w=['mag'])
                sincos(e0, adi[:], 16, sn_[:], cs_[:], 's0')
                V(lambda e: e.tensor_tensor(out=abr[:], in0=mag[:], in1=cs_[:], op=ALU.mult), ['mag', 's0out'], ['abr'])
                V(lambda e: e.tensor_tensor(out=abi[:], in0=mag[:], in1=sn_[:], op=ALU.mult), ['mag', 's0out'], ['abi'])
                V(lambda e: e.tensor_scalar(out=nr[:], in0=abr[:], scalar1=-1.0, scalar2=None, op0=ALU.add), ['abr'], ['nr'])
                V(lambda e: e.tensor_tensor(out=den[:], in0=are[:], in1=are[:], op=ALU.mult), ['are'], ['den'])
                V(lambda e: e.tensor_tensor(out=t0[:], in0=aim[:], in1=aim[:], op=ALU.mult), ['aim'], ['t0'])
                V(lambda e: e.tensor_tensor(out=den[:], in0=den[:], in1=t0[:], op=ALU.add), ['den', 't0'], ['den'])
                V(lambda e: e.reciprocal(out=den[:], in_=den[:]), ['den'], ['den'])
                V(lambda e: e.tensor_tensor(out=t0[:], in0=nr[:], in1=are[:], op=ALU.mult), ['nr', 'are'], ['t0'])
                V(lambda e: e.tensor_tensor(out=t1[:], in0=abi[:], in1=aim[:], op=ALU.mult), ['abi', 'aim'], ['t1'])
                V(lambda e: e.tensor_tensor(out=t0[:], in0=t0[:], in1=t1[:], op=ALU.add), ['t0', 't1'], ['t0'])
                V(lambda e: e.tensor_tensor(out=fr[:], in0=t0[:], in1=den[:], op=ALU.mult), ['t0', 'den'], ['fr'])
                V(lambda e: e.tensor_tensor(out=t0[:], in0=abi[:], in1=are[:], op=ALU.mult), ['abi', 'are'], ['t0'])
                V(lambda e: e.tensor_tensor(out=t1[:], in0=nr[:], in1=aim[:], op=ALU.mult), ['nr', 'aim'], ['t1'])
                V(lambda e: e.tensor_tensor(out=t0[:], in0=t0[:], in1=t1[:], op=ALU.subtract), ['t0', 't1'], ['t0'])
                V(lambda e: e.tensor_tensor(out=fi[:], in0=t0[:], in1=den[:], op=ALU.mult), ['t0', 'den'], ['fi'])
                V(lambda e: e.tensor_scalar(out=a128[:], in0=adi[:], scalar1=128.0, scalar2=None, op0=ALU.mult), ['s0phi'], ['s1phi'])
                sincos(e0, a128[:], 16, rots[:], rotc[:], 's1')
                tp1 = sbt(e0, 'tp1', [128, 128])
                T.dma('sp', tp1[:], di['c_tp1'], w=['tp1'], slot='c0')
                phi = sbt(e0, 'phi', [128, 2048])
                V(lambda e: e.tensor_tensor(out=phi[:].rearrange("p (j t) -> p j t", j=16),
                                            in0=adi[:].unsqueeze(2).broadcast_to([128, 16, 128]),
                                            in1=tp1[:].unsqueeze(1).broadcast_to([128, 16, 128]), op=ALU.mult), ['s0phi', 'tp1'], ['s2phi'])
                sincos(e0, phi[:], 2048, stab[:], ctab[:], 's2')
                bre = sbt(e0, 'bre', [128, 2048])
                bim = sbt(e0, 'bim', [128, 2048])
                u1 = sbt(e0, 'u1', [128, 2048])
                u2 = sbt(e0, 'u2', [128, 2048])
                bb = [sbt(e0, f'bb{k}', [128, 2048]) for k in range(2)]
                T.dma('sp', bre[:], di['bre'], w=['bre'], slot='c1')
                T.dma('sp', bim[:], di['bim'], w=['bim'], slot='c2')
                v3 = lambda t_: t_[:].rearrange("p (j c) -> p j c", j=16)
                bc = lambda t_: t_[:].unsqueeze(2).broadcast_to([128, 16, 128])
                V(lambda e: e.tensor_tensor(out=v3(u1), in0=v3(bre), in1=bc(fr), op=ALU.mult), ['bre', 'fr'], ['u1'])
                V(lambda e: e.tensor_tensor(out=v3(u2), in0=v3(bim), in1=bc(fi), op=ALU.mult), ['bim', 'fi'], ['u2'])
                V(lambda e: e.tensor_tensor(out=bb[0][:], in0=u1[:], in1=u2[:], op=ALU.subtract), ['u1', 'u2'], ['bb0'])
                V(lambda e: e.tensor_tensor(out=v3(u1), in0=v3(bim), in1=bc(fr), op=ALU.mult), ['bim', 'fr', 'bb0'], ['u1'])
                V(lambda e: e.tensor_tensor(out=v3(u2), in0=v3(bre), in1=bc(fi), op=ALU.mult), ['bre', 'fi', 'bb0'], ['u2'])
                V(lambda e: e.tensor_tensor(out=bb[1][:], in0=u1[:], in1=u2[:], op=ALU.add), ['u1', 'u2'], ['bb1'])
                trp = pst(e0, 'trp', [128, 512])
                for k in range(2):
                    for jg in range(4):
                        for jj in range(4):
                            j = jg * 4 + jj
                            T.op('pe', lambda e: e.transpose(trp[:, jj * 128:(jj + 1) * 128], bb[k][:, j * 128:(j + 1) * 128], identf[:]),
                                 r=[f'bb{k}', 'identf'], w=['trp'])
                        T.op('dve', lambda e: e.tensor_copy(out=BbT[k][:, jg * 512:(jg + 1) * 512], in_=trp[:]), r=['trp'], w=[f'BbT{k}'])
                T.dma('sp', bre[:], di['cre'], r=['u1', 'u2'], w=['bre'], slot='c1')
                T.dma('sp', bim[:], di['cim'], r=['u1', 'u2'], w=['bim'], slot='c2')
                T.op('act', lambda e: e.activation(out=Cb[0][:], in_=bre[:], func=AF.Copy), r=['bre'], w=['Cb0'])
                T.op('act', lambda e: e.activation(out=Cb[1][:], in_=bim[:], func=AF.Copy, scale=-1.0), r=['bim'], w=['Cb1'])

            T.barrier()
            with ExitStack() as e5:
                bankU = pst(e5, 'bankU', [128, 512])
                bu = [[pst(e5, f'bu{a}{b}', [128, 512]) for b in range(2)] for a in range(2)]
                yTp = pst(e5, 'yTp', [128, 512])
                zp = pst(e5, 'zp', [128, 512])
                ub = sbt(e5, 'ub', [128, 512], BF16)
                nm12 = ('t1_', 't2_', 't3_', 't4_', 'vr', 'vi', 'wr', 'wi', 'a1', 'a2', 'a3', 'a4')
                f4b = {n: [sbt(e5, f'{n}{b}', [128, 512]) for b in range(2)] for n in nm12}
                xrb = [sbt(e5, f'xr{b}', [128, 512], BF16) for b in range(2)]
                xib = [sbt(e5, f'xi{b}', [128, 512], BF16) for b in range(2)]
                w0 = [[sbt(e5, f'w0{p}{k}', [128, 16]) for k in range(2)] for p in range(2)]
                q1b = [sbt(e5, f'q1{b}', [128, 4]) for b in range(2)]
                q2b = [sbt(e5, f'q2{b}', [128, 4]) for b in range(2)]
                ysb = sbt(e5, 'ysb', [128, 512])
                ygb = sbt(e5, 'ygb', [128, 512], BF16)
                sgb = sbt(e5, 'sgb', [128, 512])
                yso = [sbt(e5, f'yso{k}', [128, 512], BF16) for k in range(2)]
                for k in range(2):
                    T.op('dve', lambda e: e.memset(w0[0][k][:], 0.0), w=[f'w00{k}'])
                for i in range(NT if 'S' in PH else 0):
                    xt, hT, s = rms_hT(i, x_d)
                    p, pn = i % 2, (i + 1) % 2
                    for sg in range(4):
                        for c in range(8):
                            T.op('pe', lambda e: e.matmul(bankU[:, sg * 128:(sg + 1) * 128], lhsT=Wu[:, c, sg * 128:(sg + 1) * 128],
                                                          rhs=hT[:, c * 128:(c + 1) * 128], start=(c == 0), stop=(c == 7)),
                                 r=[f'hT{s}'] + kU, w=['bankU'])
                    T.op('act', lambda e: e.activation(out=ub[:], in_=bankU[:], func=AF.Copy), r=['bankU'], w=['ub'])
                    for sg in range(4):
                        B_ = sg % 2
                        t1_, t2_, t3_, t4_, vr, vi, wr, wi, a1, a2, a3, a4 = [f4b[n][B_] for n in nm12]
                        xr, xi, q1, q2 = xrb[B_], xib[B_], q1b[B_], q2b[B_]
                        bur, bui = bu[sg % 2]
                        kr, ki_ = f'bu{sg % 2}0', f'bu{sg % 2}1'
                        cs = ctab[:, sg * 512:(sg + 1) * 512]
                        ss = stab[:, sg * 512:(sg + 1) * 512]
                        for jj in range(4):
                            j = sg * 4 + jj
                            for k, (bt, kk) in enumerate(((bur, kr), (bui, ki_))):
                                T.op('pe', lambda e: e.matmul(bt[:, jj * 128:(jj + 1) * 128], lhsT=BbT[k][:, j * 128:(j + 1) * 128],
                                                              rhs=ub[:, sg * 128:(sg + 1) * 128], start=True, stop=True),
                                     r=['ub', f'BbT{k}'], w=[kk])
                        T.op('dve', lambda e: e.tensor_tensor(out=t1_[:], in0=bur[:], in1=cs, op=ALU.mult), r=[kr, 's2out'], w=['t1_' + str(B_)])
                        T.op('dve', lambda e: e.tensor_tensor(out=t2_[:], in0=bui[:], in1=ss, op=ALU.mult), r=[ki_, 's2out'], w=['t2_' + str(B_)])
                        T.op('pool', lambda e: e.tensor_tensor(out=vr[:], in0=t1_[:], in1=t2_[:], op=ALU.add), r=['t1_' + str(B_), 't2_' + str(B_)], w=['vr' + str(B_)])
                        T.op('dve', lambda e: e.tensor_tensor(out=t3_[:], in0=bui[:], in1=cs, op=ALU.mult), r=[ki_, 's2out'], w=['t3_' + str(B_)])
                        T.op('dve', lambda e: e.tensor_tensor(out=t4_[:], in0=bur[:], in1=ss, op=ALU.mult), r=[kr, 's2out'], w=['t4_' + str(B_)])
                        T.op('pool', lambda e: e.tensor_tensor(out=vi[:], in0=t3_[:], in1=t4_[:], op=ALU.subtract), r=['t3_' + str(B_), 't4_' + str(B_)], w=['vi' + str(B_)])
                        for jj in range(4):
                            j = sg * 4 + jj
                            sl = slice(jj * 128, (jj + 1) * 128)
                            for k, (vv, ww, nm) in enumerate(((vr, wr, 'wr' + str(B_)), (vi, wi, 'wi' + str(B_)))):
                                T.op('dve', lambda e: e.tensor_tensor_scan(out=ww[:, sl], data0=mag[:, j:j + 1].broadcast_to([128, 128]),
                                                                           data1=vv[:, sl], initial=w0[p][k][:, j:j + 1],
                                                                           op0=ALU.mult, op1=ALU.add),
                                     r=['mag', 'vr' + str(B_) if k == 0 else 'vi' + str(B_), f'w0{p}{k}'], w=[nm])
                        T.op('pool', lambda e: e.tensor_tensor(out=a1[:], in0=wr[:], in1=cs, op=ALU.mult), r=['wr' + str(B_), 's2out'], w=['a1' + str(B_)])
                        T.op('pool', lambda e: e.tensor_tensor(out=a2[:], in0=wi[:], in1=ss, op=ALU.mult), r=['wi' + str(B_), 's2out'], w=['a2' + str(B_)])
                        T.op('pool', lambda e: e.tensor_tensor(out=xr[:], in0=a1[:], in1=a2[:], op=ALU.subtract), r=['a1' + str(B_), 'a2' + str(B_)], w=['xr' + str(B_)])
                        T.op('pool', lambda e: e.tensor_tensor(out=a3[:], in0=wi[:], in1=cs, op=ALU.mult), r=['wi' + str(B_), 's2out'], w=['a3' + str(B_)])
                        T.op('pool', lambda e: e.tensor_tensor(out=a4[:], in0=wr[:], in1=ss, op=ALU.mult), r=['wr' + str(B_), 's2out'], w=['a4' + str(B_)])
                        T.op('pool', lambda e: e.tensor_tensor(out=xi[:], in0=a3[:], in1=a4[:], op=ALU.add), r=['a3' + str(B_), 'a4' + str(B_)], w=['xi' + str(B_)])
                        wl_r = wr[:].rearrange("p (j t) -> p j t", j=4)[:, :, 127:128]
                        wl_i = wi[:].rearrange("p (j t) -> p j t", j=4)[:, :, 127:128]
                        rc = rotc[:, sg * 4:(sg + 1) * 4].unsqueeze(2)
                        rs = rots[:, sg * 4:(sg + 1) * 4].unsqueeze(2)
                        n_r = w0[pn][0][:, sg * 4:(sg + 1) * 4].unsqueeze(2)
                        n_i = w0[pn][1][:, sg * 4:(sg + 1) * 4].unsqueeze(2)
                        T.op('dve', lambda e: e.tensor_tensor(out=q1[:].unsqueeze(2), in0=wl_r, in1=rc, op=ALU.mult), r=['wr' + str(B_), 's1out'], w=['q1' + str(B_)])
                        T.op('dve', lambda e: e.tensor_tensor(out=q2[:].unsqueeze(2), in0=wl_i, in1=rs, op=ALU.mult), r=['wi' + str(B_), 's1out'], w=['q2' + str(B_)])
                        T.op('dve', lambda e: e.tensor_tensor(out=n_r, in0=q1[:].unsqueeze(2), in1=q2[:].unsqueeze(2), op=ALU.subtract),
                             r=['q1' + str(B_), 'q2' + str(B_)], w=[f'w0{pn}0'])
                        T.op('dve', lambda e: e.tensor_tensor(out=q1[:].unsqueeze(2), in0=wl_i, in1=rc, op=ALU.mult), r=['wi' + str(B_), 's1out', f'w0{pn}0'], w=['q1' + str(B_)])
                        T.op('dve', lambda e: e.tensor_tensor(out=q2[:].unsqueeze(2), in0=wl_r, in1=rs, op=ALU.mult), r=['wr' + str(B_), 's1out', f'w0{pn}0'], w=['q2' + str(B_)])
                        T.op('dve', lambda e: e.tensor_tensor(out=n_i, in0=q1[:].unsqueeze(2), in1=q2[:].unsqueeze(2), op=ALU.add),
                             r=['q1' + str(B_), 'q2' + str(B_)], w=[f'w0{pn}1'])
                        for jj in range(4):
                            j = sg * 4 + jj
                            sl = slice(jj * 128, (jj + 1) * 128)
                            T.op('pe', lambda e: e.matmul(yTp[:, sg * 128:(sg + 1) * 128], lhsT=Cb[0][:, j * 128:(j + 1) * 128], rhs=xr[:, sl],
                                                          start=(jj == 0), stop=False), r=['Cb0', 'xr' + str(B_)], w=['yTp'])
                            T.op('pe', lambda e: e.matmul(yTp[:, sg * 128:(sg + 1) * 128], lhsT=Cb[1][:, j * 128:(j + 1) * 128], rhs=xi[:, sl],
                                                          start=False, stop=(jj == 3)), r=['Cb1', 'xi' + str(B_)], w=['yTp'])
                    for sg in range(4):
                        sl = slice(sg * 128, (sg + 1) * 128)
                        T.op('dve', lambda e: e.scalar_tensor_tensor(out=ysb[:, sl], in0=ub[:, sl], scalar=dsk[:, sg:sg + 1], in1=yTp[:, sl],
                                                                     op0=ALU.mult, op1=ALU.add), r=['ub', 'dsk', 'yTp'], w=['ysb'])
                    T.op('act', lambda e: e.activation(out=ygb[:], in_=ysb[:], func=AF.Gelu_apprx_tanh), r=['ysb'], w=['ygb'])
                    for co in range(4):
                        for ci in range(4):
                            T.op('pe', lambda e: e.matmul(zp[:, co * 128:(co + 1) * 128], lhsT=Wgl[:, ci, co * 128:(co + 1) * 128],
                                                          rhs=ygb[:, ci * 128:(ci + 1) * 128], start=(ci == 0), stop=(ci == 3)),
                                 r=['ygb'] + kGl, w=['zp'])
                    for co in range(4):
                        sl = slice(co * 128, (co + 1) * 128)
                        T.op('act', lambda e: e.activation(out=sgb[:, sl], in_=zp[:, sl], func=AF.Sigmoid, bias=bglu[:, co:co + 1]),
                             r=['zp', 'bglu'], w=['sgb'])
                    yo = yso[i % 2]
                    T.op('pool', lambda e: e.tensor_tensor(out=yo[:], in0=ygb[:], in1=sgb[:], op=ALU.mult), r=['ygb', 'sgb'], w=[f'yso{i % 2}'])
                    T.dma('sp', ys_d[i], yo[:], r=[f'yso{i % 2}'], w=[f'ys_{i}'], slot=f'yso{i % 2}')

        T.barrier()
        with ExitStack() as e6:
            Wmg = sbt(e6, 'Wmg', [128, 8, 2048], BF16)
            Wps = sbt(e6, 'Wps', [128, 4, D], BF16)
            Wpn = sbt(e6, 'Wpn', [128, 4, D], BF16)
            Wo = sbt(e6, 'Wo', [128, 8, D], BF16)
            kMg = load_w(Wmg, di['w_in'], D, 2048, 'Wmg', col0=O4, gt=g1t)
            kPs = load_w(Wps, di['w_ps'], 512, D, 'Wps')
            kPn = load_w(Wpn, di['w_pn'], 512, D, 'Wpn')
            kWo = load_w(Wo, di['w_out'], D, D, 'Wo')
            Gp = [pst(e6, f'Gp{k}', [128, 512]) for k in range(2)]
            Pp = [pst(e6, f'Pp{k}', [128, 512]) for k in range(2)]
            Op = [pst(e6, f'Op{k}', [128, 512]) for k in range(2)]
            gs = [sbt(e6, f'gs{k}', [128, 512]) for k in range(2)]
            mt = [sbt(e6, f'mt{k}', [128, 512]) for k in range(2)]
            mrg = sbt(e6, 'mrg', [128, D], BF16)
            mT = sbt(e6, 'mT', [128, D], BF16)
            ysl_ = [sbt(e6, f'ysl{k}', [128, 512], BF16) for k in range(2)]
            ynl_ = [sbt(e6, f'ynl{k}', [128, 512], BF16) for k in range(2)]
            x1s = [sbt(e6, f'x1s{k}', [128, D]) for k in range(2)]
            for i in range(NT if '4' in PH else 0):
                xt, hT, s = rms_hT(i, x_d)
                q = i % 2
                T.dma('sp', ysl_[q][:], ys_d[i], r=[f'ys_{i}'], w=[f'ysl{q}'], slot=f'ysl{q}')
                T.dma('sp', ynl_[q][:], yn_d[i], r=[f'yn_{i}'], w=[f'ynl{q}'], slot=f'ynl{q}')
                for hf in range(2):
                    hs = slice(hf * 512, (hf + 1) * 512)
                    for m_ in range(2):
                        for c in range(8):
                            T.op('pe', lambda e: e.matmul(Gp[m_][:], lhsT=hT[:, c * 128:(c + 1) * 128],
                                                          rhs=Wmg[:, c, m_ * 1024 + hf * 512:m_ * 1024 + (hf + 1) * 512],
                                                          start=(c == 0), stop=(c == 7)), r=[f'hT{s}'] + kMg, w=[f'Gp{m_}'])
                        T.op('act', lambda e: e.activation(out=gs[m_][:], in_=Gp[m_][:], func=AF.Sigmoid), r=[f'Gp{m_}'], w=[f'gs{m_}'])
                        yl, Wp, kk, kn_ = ((ysl_[q], Wps, kPs, f'ysl{q}'), (ynl_[q], Wpn, kPn, f'ynl{q}'))[m_]
                        for ci in range(4):
                            T.op('pe', lambda e: e.matmul(Pp[m_][:], lhsT=yl[:, ci * 128:(ci + 1) * 128], rhs=Wp[:, ci, hs],
                                                          start=(ci == 0), stop=(ci == 3)), r=[kn_] + kk, w=[f'Pp{m_}'])
                        T.op('dve', lambda e: e.tensor_tensor(out=mt[m_][:], in0=Pp[m_][:], in1=gs[m_][:], op=ALU.mult),
                             r=[f'Pp{m_}', f'gs{m_}'], w=[f'mt{m_}'])
                    T.op('pool', lambda e: e.tensor_tensor(out=mrg[:, hs], in0=mt[0][:], in1=mt[1][:], op=ALU.add), r=['mt0', 'mt1'], w=['mrg'])
                for c in range(8):
                    T.op('pe', lambda e: e.transpose(tp[:, c * 128:(c + 1) * 128], mrg[:, c * 128:(c + 1) * 128], identb[:]),
                         r=['mrg', 'identb'], w=['tp'])
                T.op('dve', lambda e: e.tensor_copy(out=mT[:], in_=tp[:]), r=['tp'], w=['mT'])
                for hf in range(2):
                    hs = slice(hf * 512, (hf + 1) * 512)
                    for c in range(8):
                        T.op('pe', lambda e: e.matmul(Op[hf][:], lhsT=mT[:, c * 128:(c + 1) * 128], rhs=Wo[:, c, hs],
                                                      start=(c == 0), stop=(c == 7)), r=['mT'] + kWo, w=[f'Op{hf}'])
                    T.op('dve', lambda e: e.tensor_tensor(out=x1s[q][:, hs], in0=Op[hf][:], in1=xt[:, hs], op=ALU.add),
                         r=[f'Op{hf}', f'xt{s}'], w=[f'x1s{q}'])
                T.dma('sp', x1_d[i * 128:(i + 1) * 128, :], x1s[q][:], r=[f'x1s{q}'], w=[f'x1_{i}', f'ys_{i}', f'yn_{i}'], slot=f'x1s{q}')

        T.barrier()
        with ExitStack() as e7:
            Wup = sbt(e7, 'Wup', [128, 8, 4096], BF16)
            Wdn = sbt(e7, 'Wdn', [128, 32, D], BF16)
            kUp = load_w(Wup, di['w_up'], D, 4096, 'Wup', gt=g2t)
            kDn = load_w(Wdn, di['w_down'], 4096, D, 'Wdn')
            Up = [pst(e7, f'Up{k}', [128, 512]) for k in range(2)]
            Dn = [pst(e7, f'Dn{k}', [128, 512]) for k in range(2)]
            rl = [sbt(e7, f'rl{k}', [128, 512]) for k in range(2)]
            acT = [sbt(e7, f'acT{k}', [128, 512], BF16) for k in range(2)]
            os_ = [sbt(e7, f'os{k}', [128, D]) for k in range(2)]
            for i in range(NT if '5' in PH else 0):
                xt, hT, s = rms_hT(i, x1_d, keysrc=[f'x1_{i}'])
                q = i % 2
                def down(fg):
                    u = fg % 2
                    for q4 in range(4):
                        fc = fg * 4 + q4
                        for hf in range(2):
                            T.op('pe', lambda e: e.matmul(Dn[hf][:], lhsT=acT[u][:, q4 * 128:(q4 + 1) * 128],
                                                          rhs=Wdn[:, fc, hf * 512:(hf + 1) * 512], start=(fc == 0), stop=(fc == 31)),
                                 r=[f'acT{u}'] + kDn, w=[f'Dn{hf}'])

                for fg in range(8):
                    u = fg % 2
                    for q4 in range(4):
                        fc = fg * 4 + q4
                        for c in range(8):
                            T.op('pe', lambda e: e.matmul(Up[u][:, q4 * 128:(q4 + 1) * 128], lhsT=Wup[:, c, fc * 128:(fc + 1) * 128],
                                                          rhs=hT[:, c * 128:(c + 1) * 128], start=(c == 0), stop=(c == 7)),
                                 r=[f'hT{s}'] + kUp, w=[f'Up{u}'])
                    T.op('act', lambda e: e.activation(out=rl[u][:], in_=Up[u][:], func=AF.Relu), r=[f'Up{u}'], w=[f'rl{u}'])
                    T.op('dve', lambda e: e.tensor_tensor(out=acT[u][:], in0=rl[u][:], in1=rl[u][:], op=ALU.mult), r=[f'rl{u}'], w=[f'acT{u}'])
                    if fg > 0:
                        down(fg - 1)
                down(7)
                for hf in range(2):
                    hs = slice(hf * 512, (hf + 1) * 512)
                    T.op('dve', lambda e: e.tensor_tensor(out=os_[q][:, hs], in0=Dn[hf][:], in1=xt[:, hs], op=ALU.add),
                         r=[f'Dn{hf}', f'xt{s}'], w=[f'os{q}'])
                T.dma('sp', out_d[i * 128:(i + 1) * 128, :], os_[q][:], r=[f'os{q}'], w=[f'out_{i}', f'x1_{i}'], slot=f'os{q}')
        T.finish()
        print("instructions:", T.nins)
    return nc, hc


_CACHE = {}


def run(inputs, SEQ, n_cores, dbg=False):
    key = (SEQ, dbg)
    if key not in _CACHE:
        _CACHE[key] = build(SEQ, dbg)
    nc, hc = _CACHE[key]
    L = host_layouts(inputs)
    x = np.asarray(inputs['x'], dtype=np.float32)
    B = x.shape[0]
    in_maps = []
    for core in range(n_cores):
        m = {'x': np.ascontiguousarray(x[core % B])}
        for k, v in hc.items():
            m['c_' + k] = v
        m.update(L)
        in_maps.append(m)
    res = run_bass_kernel_spmd(nc, in_maps, core_ids=list(range(n_cores)))
    return res


def kernel(**inputs):
    x = np.asarray(inputs['x'])
    B, SEQ, _ = x.shape
    res = run(inputs, SEQ, B)
    return np.stack([np.asarray(res.results[b]['out'], dtype=np.float32) for b in range(B)], 0)
```

```python
import os
import numpy as np
import ml_dtypes
from contextlib import ExitStack
import concourse.bass as bass
import concourse.mybir as mybir
from concourse.bass_utils import run_bass_kernel_spmd

F32 = mybir.dt.float32
BF16 = mybir.dt.bfloat16
I32 = mybir.dt.int32
AF = mybir.ActivationFunctionType
ALU = mybir.AluOpType
AX = mybir.AxisListType
SAME_ENGINE_SYNC = True
NEG = -30000.0
EPS = 1e-6
D = 1024
IN_W = 3864
O1, O2, O3, O4 = 512, 1024, 1792, 1816
BF = ml_dtypes.bfloat16


class TS:
    def __init__(self, nc, es):
        self.nc = nc
        self.es = es
        self.E = dict(pe=nc.tensor, act=nc.scalar, dve=nc.vector, pool=nc.gpsimd, sp=nc.sync)
        self.sem = {k: es.enter_context(nc.semaphore("s_" + k)) for k in self.E}
        self.cnt = {k: 0 for k in self.E}
        self.waited = {k: {} for k in self.E}
        self.lastw = {}
        self.readers = {}
        self.nins = 0

    def _need(self, e, reads, writes):
        deps = {}
        for k in reads:
            t = self.lastw.get(k)
            if t is not None and deps.get(t[0], 0) < t[1]:
                deps[t[0]] = t[1]
        for k in writes:
            t = self.lastw.get(k)
            if t is not None and deps.get(t[0], 0) < t[1]:
                deps[t[0]] = t[1]
            for s, v in self.readers.get(k, {}).items():
                if deps.get(s, 0) < v:
                    deps[s] = v
        for s, v in deps.items():
            if s == e and (e == 'pe' or not SAME_ENGINE_SYNC):
                continue
            if self.waited[e].get(s, 0) >= v:
                continue
            self.E[e].wait_ge(self.sem[s], v)
            self.waited[e][s] = v

    def _record(self, tok, reads, writes):
        s, v = tok
        for k in writes:
            self.lastw[k] = tok
            self.readers[k] = {}
        for k in reads:
            d = self.readers.setdefault(k, {})
            if d.get(s, 0) < v:
                d[s] = v

    def op(self, e, fn, r=(), w=()):
        self._need(e, r, w)
        ins = fn(self.E[e])
        self.cnt[e] += 1
        ins.then_inc(self.sem[e], 1)
        self._record((e, self.cnt[e]), r, w)
        self.nins += 1

    def dma(self, q, out, in_, r=(), w=(), slot=None):
        self._need(q, r, w)
        name = 'd_' + slot
        if name not in self.sem:
            self.sem[name] = self.es.enter_context(self.nc.semaphore("s_" + name))
            self.cnt[name] = 0
        if self.cnt[name] > 0 and self.waited[q].get(name, 0) < self.cnt[name]:
            self.E[q].wait_ge(self.sem[name], self.cnt[name])
            self.waited[q][name] = self.cnt[name]
        ins = self.E[q].dma_start(out=out, in_=in_)
        self.cnt[name] += 16
        ins.then_inc(self.sem[name], 16)
        self._record((name, self.cnt[name]), r, w)
        self.nins += 1

    def barrier(self):
        for e in self.E:
            for s, h in self.sem.items():
                if self.cnt[s] > 0 and self.waited[e].get(s, 0) < self.cnt[s]:
                    self.E[e].wait_ge(h, self.cnt[s])
                    self.waited[e][s] = self.cnt[s]

    def finish(self):
        for s, h in self.sem.items():
            if self.cnt[s] > 0 and self.waited['sp'].get(s, 0) < self.cnt[s]:
                self.E['sp'].wait_ge(h, self.cnt[s])


def host_consts(SEQ):
    NT = SEQ // 128
    nblk = SEQ // 16 - 1
    NCT = (nblk + 127) // 128
    n_sel = SEQ // 64
    c = {}
    c['identf'] = np.eye(128, dtype=np.float32)
    i4 = np.tile(np.eye(128, dtype=np.float32), (1, 4))
    c['i4'] = i4.astype(BF)
    qi = np.arange(128)[:, None]
    ki = np.arange(128)[None, :]
    c['negtri'] = np.where(ki > qi, NEG, 0.0).astype(BF)
    c['negtric'] = np.where(ki <= qi, NEG, 0.0).astype(BF)
    keys = np.arange(SEQ)
    c['eall'] = (keys[None, :] // 64 == np.arange(128)[:, None]).astype(np.float32).astype(BF)
    A = np.zeros((128, NCT, 128), np.float32)
    for cc in range(NCT * 128):
        for n in ((cc + 1) // 4, cc // 4):
            if 4 * n - 1 <= cc <= 4 * n + 3 and n < n_sel and n != 127:
                A[cc % 128, cc // 128, n] = 1.0
    A[:, :, 127] = 1.0
    c['aaug'] = A.reshape(128, NCT * 128).astype(BF)
    m = np.arange(-127, 145)[None, :]
    f = ((np.arange(128) + 1) // 16)[:, None]
    c['wc'] = np.where(m > f, NEG, 0.0).astype(BF)
    t = np.arange(SEQ)
    cur = t // 64
    n = np.arange(128)[None, :]
    fb = np.zeros((SEQ, 128), np.float32)
    forced = (n == 0) | (n == cur[:, None]) | (n == cur[:, None] - 1)
    fb[forced] = 1e4
    fb[(n > cur[:, None]) | (n >= n_sel)] = -1e30
    c['fb'] = fb
    kpos = np.stack([64 * (t // 64), t % 64, np.ones_like(t), np.ones_like(t)]).astype(np.float32)
    c['kpos'] = kpos.astype(BF)
    ce = 16 * np.arange(NCT * 128) + 31
    c['cpos'] = np.stack([64 * (ce // 64), ce % 64, np.ones_like(ce), np.ones_like(ce)]).astype(np.float32).astype(BF)
    slopes = 2.0 ** -(np.arange(8) + 1.0)
    qpos = np.zeros((4, 8, SEQ), np.float32)
    qpos[0] = slopes[:, None]
    qpos[1] = slopes[:, None]
    qpos[2] = -slopes[:, None] * (64 * (t // 64))[None, :]
    qpos[3] = -slopes[:, None] * (t % 64)[None, :]
    c['qpos'] = qpos.astype(BF)
    c['tp1'] = np.tile(np.arange(1, 129, dtype=np.float32)[None, :], (128, 1))
    c['ones64'] = np.full((64, 64), 1.0 / 64, np.float32)
    return c


def host_layouts(inp):
    L = {}
    f = lambda a: np.ascontiguousarray(np.asarray(a, dtype=np.float32))
    L['g1'] = f(inp['norm1_g'][0].reshape(8, 128).T)
    L['g2'] = f(inp['norm2_g'][0].reshape(8, 128).T)
    L['w_in'] = f(inp['w_in'][0])
    L['w_glu'] = f(inp['ssm_w_glu'][0])
    L['w_ps'] = f(inp['w_proj_ssm'][0])
    L['w_pn'] = f(inp['w_proj_nsa'][0])
    L['w_out'] = f(inp['w_out'][0])
    L['w_up'] = f(inp['w_up'][0])
    L['w_down'] = f(inp['w_down'][0])
    L['gq'] = f(inp['q_norm_g'][0].reshape(64, 1))
    L['gk'] = f(inp['k_norm_g'][0].T)
    for nm, key in (('w1k', 'cmp_wk1'), ('w1v', 'cmp_wv1')):
        w = np.asarray(inp[key][0]).reshape(32, 64, 128).transpose(1, 0, 2)
        L[nm] = f(np.concatenate([w, w], 0).reshape(128, 32 * 128))
    L['w2k'] = f(inp['cmp_wk2'][0])
    L['w2v'] = f(inp['cmp_wv2'][0])
    L['pek'] = f(np.asarray(inp['cmp_pe_k'][0]).T)
    L['pev'] = f(np.asarray(inp['cmp_pe_v'][0]).T)

    def sm(a):
        a = np.asarray(a)
        return f(a.reshape(16, 2, 64).transpose(1, 2, 0).reshape(128, 16))
    L['are'] = sm(inp['ssm_a_re'][0])
    L['aim'] = sm(inp['ssm_a_im'][0])
    L['ldt'] = sm(np.repeat(np.asarray(inp['ssm_log_dt'][0])[:, None], 64, 1))

    def pad_b(b):
        b = np.asarray(b)
        o = np.zeros((128, 16, 128), np.float32)
        for g in range(32):
            j, gl = g // 2, g % 2
            o[gl * 64:(gl + 1) * 64, j, (j % 4) * 32 + gl * 16:(j % 4) * 32 + gl * 16 + 16] = b[g]
        return o.reshape(128, 2048)
    L['bre'] = pad_b(inp['ssm_b_re'][0])
    L['bim'] = pad_b(inp['ssm_b_im'][0])
    L['cre'] = pad_b(np.asarray(inp['ssm_c_re'][0]).transpose(0, 2, 1))
    L['cim'] = pad_b(np.asarray(inp['ssm_c_im'][0]).transpose(0, 2, 1))
    L['dsk'] = f(inp['ssm_d'][0].reshape(4, 128).T)
    L['bglu'] = f(inp['ssm_b_glu'][0].reshape(4, 128).T)
    return L


def build(SEQ, dbg=False):
    PH = os.environ.get('PHASES', '123S45')
    NT = SEQ // 128
    nblk = SEQ // 16 - 1
    NCT = (nblk + 127) // 128
    nc = bass.Bass("TRN2", target_bir_lowering=False)
    hc = host_consts(SEQ)
    di = {}

    def din(name, shape, dt=F32):
        di[name] = nc.dram_tensor(name, list(shape), dt, kind="ExternalInput").ap()
        return di[name]

    x_d = din('x', [SEQ, D])
    for k, v in hc.items():
        din('c_' + k, v.shape, BF16 if v.dtype == BF else F32)
    shapes = dict(g1=[128, 8], g2=[128, 8], w_in=[D, IN_W], w_glu=[512, 512], w_ps=[512, D], w_pn=[512, D],
                  w_out=[D, D], w_up=[D, 4096], w_down=[4096, D], gq=[64, 1], gk=[64, 3], w1k=[128, 4096],
                  w1v=[128, 4096], w2k=[128, 64], w2v=[128, 64], pek=[64, 32], pev=[64, 32], are=[128, 16],
                  aim=[128, 16], ldt=[128, 16], bre=[128, 2048], bim=[128, 2048], cre=[128, 2048],
                  cim=[128, 2048], dsk=[128, 4], bglu=[128, 4])
    for k, s in shapes.items():
        din(k, s)
    out_d = nc.dram_tensor('out', [SEQ, D], F32, kind="ExternalOutput").ap()
    if dbg:
        yn_t = nc.dram_tensor('yn_scr', [NT, 128, 512], BF16, kind="ExternalOutput").ap()
        ys_t = nc.dram_tensor('ys_scr', [NT, 128, 512], BF16, kind="ExternalOutput").ap()
        x1_d = nc.dram_tensor('x1_scr', [SEQ, D], F32, kind="ExternalOutput").ap()
        yn_d = [yn_t[i] for i in range(NT)]
        ys_d = [ys_t[i] for i in range(NT)]
    else:
        out_bf = out_d.bitcast(BF16)
        yn_d = [out_bf[i * 128:(i + 1) * 128, 0:512] for i in range(NT)]
        ys_d = [out_bf[i * 128:(i + 1) * 128, 512:1024] for i in range(NT)]
        x1_d = out_d

    with ExitStack() as es:
        T = TS(nc, es)

        def sbt(st, n, s, d=F32):
            return st.enter_context(nc.sbuf_tensor('sb_' + n, list(s), d))

        def pst(st, n, s, d=F32):
            return st.enter_context(nc.psum_tensor('ps_' + n, list(s), d))

        identf = sbt(es, 'identf', [128, 128])
        identb = sbt(es, 'identb', [128, 128], BF16)
        i4 = sbt(es, 'i4', [128, 512], BF16)
        negtri = sbt(es, 'negtri', [128, 128], BF16)
        negtric = sbt(es, 'negtric', [128, 128], BF16)
        epsb = sbt(es, 'epsb', [128, 1])
        g1t = sbt(es, 'g1t', [128, 8])
        g2t = sbt(es, 'g2t', [128, 8])
        xts = [sbt(es, f'xt{s}', [128, D]) for s in range(2)]
        hbs = [sbt(es, f'hb{s}', [128, D], BF16) for s in range(2)]
        hTs = [sbt(es, f'hT{s}', [128, D], BF16) for s in range(2)]
        junk = sbt(es, 'junk', [128, D], BF16)
        ssqs = [sbt(es, f'ssq{s}', [128, 1]) for s in range(2)]
        rstds = [sbt(es, f'rstd{s}', [128, 1]) for s in range(2)]
        stg = [sbt(es, f'stg{s}', [128, 1024]) for s in range(2)]
        tp = pst(es, 'tp', [128, 1024], BF16)
        wstate = {'slot': 0}

        T.dma('sp', identf[:], di['c_identf'], w=['identf'], slot='c0')
        T.dma('sp', i4[:], di['c_i4'], w=['i4'], slot='c1')
        T.dma('sp', negtri[:], di['c_negtri'], w=['negtri'], slot='c2')
        T.dma('sp', negtric[:], di['c_negtric'], w=['negtric'], slot='c3')
        T.dma('sp', g1t[:], di['g1'], w=['g1t'], slot='c4')
        T.dma('sp', g2t[:], di['g2'], w=['g2t'], slot='c5')
        T.op('dve', lambda e: e.memset(epsb[:], EPS), w=['epsb'])
        T.op('dve', lambda e: e.tensor_copy(out=identb[:], in_=identf[:]), r=['identf'], w=['identb'])

        def load_cast(dst_ap, src_ap, n, key, scale=None, parts=128):
            s = wstate['slot']
            wstate['slot'] ^= 1
            T.dma('sp', stg[s][0:parts, 0:n], src_ap, w=[f'stg{s}'], slot=f'stg{s}')
            if scale is None:
                T.op('act', lambda e: e.activation(out=dst_ap, in_=stg[s][0:parts, 0:n], func=AF.Copy), r=[f'stg{s}'], w=[key])
            else:
                T.op('act', lambda e: e.activation(out=dst_ap, in_=stg[s][0:parts, 0:n], func=AF.Copy, scale=scale),
                     r=[f'stg{s}', 'g1t', 'g2t'], w=[key])

        def load_w(dst, src, rows, cols, name, col0=0, gt=None):
            keys = []
            for c in range(rows // 128):
                for a in range(0, cols, 1024):
                    n = min(1024, cols - a)
                    k = f'{name}_{c}_{a}'
                    load_cast(dst[:, c, a:a + n], src[c * 128:(c + 1) * 128, col0 + a:col0 + a + n], n, k,
                              scale=None if gt is None else gt[:, c:c + 1])
                    keys.append(k)
            return keys

        def rms_hT(i, src_d, keysrc=()):
            s = i % 2
            xt, hb, hT = xts[s], hbs[s], hTs[s]
            T.dma('sp', xt[:], src_d[i * 128:(i + 1) * 128, :], r=list(keysrc), w=[f'xt{s}'], slot=f'xt{s}')
            T.op('act', lambda e: e.activation(out=junk[:], in_=xt[:], func=AF.Square, accum_out=ssqs[s][:]),
                 r=[f'xt{s}'], w=['junk', f'ssq{s}'])
            T.op('act', lambda e: e.activation(out=rstds[s][:], in_=ssqs[s][:], func=AF.Sqrt, bias=epsb[:], scale=1.0 / D),
                 r=[f'ssq{s}', 'epsb'], w=[f'rstd{s}'])
            T.op('dve', lambda e: e.reciprocal(out=rstds[s][:], in_=rstds[s][:]), r=[f'rstd{s}'], w=[f'rstd{s}'])
            T.op('act', lambda e: e.activation(out=hb[:], in_=xt[:], func=AF.Copy, scale=rstds[s][:]),
                 r=[f'xt{s}', f'rstd{s}'], w=[f'hb{s}'])
            for c in range(8):
                T.op('pe', lambda e: e.transpose(tp[:, c * 128:(c + 1) * 128], hb[:, c * 128:(c + 1) * 128], identb[:]),
                     r=[f'hb{s}', 'identb'], w=['tp'])
            T.op('dve', lambda e: e.tensor_copy(out=hT[:], in_=tp[:]), r=['tp'], w=[f'hT{s}'])
            return xt, hT, s

        with ExitStack() as ea:
            KselT = [sbt(ea, f'KselT{g}', [68, SEQ], BF16) for g in range(2)]
            KwinT = [sbt(ea, f'KwinT{g}', [68, SEQ], BF16) for g in range(2)]
            Vsel = sbt(ea, 'Vsel', [128, NT * 130], BF16)
            Vwin = sbt(ea, 'Vwin', [128, NT * 130], BF16)
            KcT = [sbt(ea, f'KcT{g}', [68, NCT * 128], BF16) for g in range(2)]
            Vc = sbt(ea, 'Vc', [128, NCT * 130], BF16)
            gqt = sbt(ea, 'gqt', [64, 1])
            gq8 = sbt(ea, 'gq8', [64, 1])
            gkt = sbt(ea, 'gkt', [64, 3])
            T.dma('sp', gqt[:], di['gq'], w=['gqt'], slot='c0')
            T.dma('sp', gkt[:], di['gk'], w=['gkt'], slot='c1')
            T.op('dve', lambda e: e.tensor_scalar(out=gq8[:], in0=gqt[:], scalar1=0.125, scalar2=None, op0=ALU.mult), r=['gqt'], w=['gq8'])
            for g in range(2):
                T.dma('sp', KselT[g][64:68, :], di['c_kpos'], w=[f'KselTp{g}'], slot='c2')
                T.dma('sp', KwinT[g][64:68, :], di['c_kpos'], w=[f'KwinTp{g}'], slot='c3')
                T.dma('sp', KcT[g][64:68, :], di['c_cpos'], w=[f'KcTp{g}'], slot='c4')
            T.op('pool', lambda e: e.memset(Vsel[:], 1.0), w=['Vsel_init'])
            T.op('pool', lambda e: e.memset(Vwin[:], 1.0), w=['Vwin_init'])
            T.op('pool', lambda e: e.memset(Vc[:], 1.0), w=['Vc_init'])

            with ExitStack() as e1:
                Wkv = sbt(e1, 'Wkv', [128, 8, 768], BF16)
                rawk = sbt(e1, 'rawk', [128, SEQ], BF16)
                rawv = sbt(e1, 'rawv', [128, SEQ], BF16)
                kvp = pst(e1, 'kvp', [128, 512])
                cmp_ = pst(e1, 'cmp_', [128, 512])
                tp2 = pst(e1, 'tp2', [128, 1024], BF16)
                sq = sbt(e1, 'sq', [128, 256])
                ssk = sbt(e1, 'ssk', [128, 4])
                rsk = sbt(e1, 'rsk', [128, 4])
                kn = sbt(e1, 'kn', [128, 256], BF16)
                kW = load_w(Wkv, di['w_in'], D, 768, 'Wkv', col0=O2, gt=g1t)
                for i in range(NT if '1' in PH else 0):
                    xt, hT, s = rms_hT(i, x_d)
                    cs = slice(i * 128, (i + 1) * 128)
                    for c in range(8):
                        T.op('pe', lambda e: e.matmul(kvp[:], lhsT=hT[:, c * 128:(c + 1) * 128], rhs=Wkv[:, c, 256:768],
                                                      start=(c == 0), stop=(c == 7)), r=[f'hT{s}'] + kW, w=['kvp'])
                    for half in range(2):
                        for c in range(8):
                            T.op('pe', lambda e: e.matmul(cmp_[:, half * 128:(half + 1) * 128],
                                                          lhsT=Wkv[:, c, half * 128:(half + 1) * 128],
                                                          rhs=hT[:, c * 128:(c + 1) * 128], start=(c == 0), stop=(c == 7)),
                                 r=[f'hT{s}'] + kW, w=['cmp_'])
                    T.op('act', lambda e: e.activation(out=sq[:, 0:128], in_=kvp[:, 0:128], func=AF.Square), r=['kvp'], w=['sq'])
                    T.op('act', lambda e: e.activation(out=sq[:, 128:256], in_=kvp[:, 256:384], func=AF.Square), r=['kvp'], w=['sq'])
                    T.op('dve', lambda e: e.tensor_reduce(out=ssk[:], in_=sq[:].rearrange("p (h d) -> p h d", h=4), axis=AX.X, op=ALU.add),
                         r=['sq'], w=['ssk'])
                    T.op('act', lambda e: e.activation(out=rsk[:], in_=ssk[:], func=AF.Sqrt, bias=epsb[:], scale=1.0 / 64),
                         r=['ssk', 'epsb'], w=['rsk'])
                    T.op('dve', lambda e: e.reciprocal(out=rsk[:], in_=rsk[:]), r=['rsk'], w=['rsk'])
                    for b in range(2):
                        T.op('dve', lambda e: e.tensor_tensor(
                            out=kn[:, b * 128:(b + 1) * 128].rearrange("p (h d) -> p h d", h=2),
                            in0=kvp[:, b * 256:b * 256 + 128].rearrange("p (h d) -> p h d", h=2),
                            in1=rsk[:, 2 * b:2 * b + 2].unsqueeze(2).broadcast_to([128, 2, 64]), op=ALU.mult),
                            r=['kvp', 'rsk'], w=['kn'])
                    for j in range(4):
                        T.op('pe', lambda e: e.transpose(tp2[0:64, j * 128:(j + 1) * 128], kn[:, j * 64:(j + 1) * 64], identb[:]),
                             r=['kn', 'identb'], w=['tp2'])
                    for j in range(4):
                        br, g = (1, j) if j < 2 else (2, j - 2)
                        dst = (KselT if br == 1 else KwinT)[g]
                        T.op('act', lambda e: e.activation(out=dst[0:64, cs], in_=tp2[0:64, j * 128:(j + 1) * 128], func=AF.Copy,
                                                           scale=gkt[:, br:br + 1]), r=['tp2', 'gkt'],
                             w=[f'K{br}_{g}_{i}'])
                    for b, Vt, nm in ((0, Vsel, 'Vs'), (1, Vwin, 'Vw')):
                        T.op('dve', lambda e: e.tensor_copy(
                            out=Vt[:, i * 130:(i + 1) * 130].rearrange("p (g d) -> p g d", g=2)[:, :, 0:64],
                            in_=kvp[:, b * 256 + 128:b * 256 + 256].rearrange("p (g d) -> p g d", g=2)),
                            r=['kvp', 'Vsel_init', 'Vwin_init'], w=[f'{nm}_{i}'])
                    T.op('act', lambda e: e.activation(out=rawk[:, cs], in_=cmp_[:, 0:128], func=AF.Copy), r=['cmp_'], w=['rawk'])
                    T.op('act', lambda e: e.activation(out=rawv[:, cs], in_=cmp_[:, 128:256], func=AF.Copy), r=['cmp_'], w=['rawv'])

                W1 = {'k': sbt(e1, 'W1k', [128, 4096], BF16), 'v': sbt(e1, 'W1v', [128, 4096], BF16)}
                W2 = {'k': sbt(e1, 'W2k', [128, 64], BF16), 'v': sbt(e1, 'W2v', [128, 64], BF16)}
                pe_ = {'k': sbt(e1, 'pek', [64, 32], BF16), 'v': sbt(e1, 'pev', [64, 32], BF16)}
                ones64 = sbt(e1, 'ones64', [64, 64])
                T.dma('sp', ones64[:], di['c_ones64'], w=['ones64'], slot='c0')
                for kd in 'kv':
                    for a in range(4):
                        load_cast(W1[kd][:, a * 1024:(a + 1) * 1024], di['w1' + kd][:, a * 1024:(a + 1) * 1024], 1024, 'W1' + kd)
                    load_cast(W2[kd][:], di['w2' + kd], 64, 'W2' + kd)
                    load_cast(pe_[kd][:], di['pe' + kd], 32, 'pe' + kd, parts=64)
                b1p = pst(e1, 'b1p', [128, 512])
                b1s = sbt(e1, 'b1s', [128, 1])
                hid = sbt(e1, 'hid', [128, NCT * 128], BF16)
                sqc = sbt(e1, 'sqc', [64, 512])
                rsc = sbt(e1, 'rsc', [64, 512])
                T.op('pool', lambda e: e.memset(hid[:], 0.0), w=['hid'])
                for g in range(2):
                    T.op('pool', lambda e: e.memset(KcT[g][0:64, :], 0.0), w=[f'KcT{g}'])
                for kd in ('kv' if '2' in PH else ''):
                    raw = rawk if kd == 'k' else rawv
                    for r_ in range(32):
                        T.op('pe', lambda e: e.matmul(b1p[:, 0:1], lhsT=W1[kd][0:64, r_ * 128:(r_ + 1) * 128],
                                                      rhs=pe_[kd][0:64, r_:r_ + 1], start=(r_ == 0), stop=(r_ == 31)),
                             r=['W1' + kd, 'pe' + kd], w=['b1p'])
                    T.op('dve', lambda e: e.tensor_copy(out=b1s[:], in_=b1p[:, 0:1]), r=['b1p'], w=['b1s'])
                    for g in range(2):
                        ps_ = slice(g * 64, (g + 1) * 64)
                        for r_ in range(32):
                            T.op('pe', lambda e: e.matmul(kvp[:, 0:nblk], lhsT=W1[kd][ps_, r_ * 128:(r_ + 1) * 128],
                                                          rhs=raw[ps_, r_:r_ + 16 * (nblk - 1) + 1:16],
                                                          start=(r_ == 0), stop=(r_ == 31)),
                                 r=['W1' + kd, 'rawk', 'rawv'], w=['kvp'])
                        T.op('act', lambda e: e.activation(out=hid[:, 0:nblk], in_=kvp[:, 0:nblk], func=AF.Gelu_apprx_tanh, bias=b1s[:]),
                             r=['kvp', 'b1s'], w=['hid'])
                        if kd == 'k':
                            T.op('pe', lambda e: e.matmul(cmp_[0:64, 0:nblk], lhsT=W2['k'][:, :], rhs=hid[:, 0:nblk], start=True, stop=True),
                                 r=['W2k', 'hid'], w=['cmp_'])
                            T.op('act', lambda e: e.activation(out=sqc[:, 0:nblk], in_=cmp_[0:64, 0:nblk], func=AF.Square), r=['cmp_'], w=['sqc'])
                            T.op('pe', lambda e: e.matmul(b1p[0:64, 0:nblk], lhsT=ones64[:, :], rhs=sqc[:, 0:nblk], start=True, stop=True),
                                 r=['ones64', 'sqc'], w=['b1p'])
                            T.op('act', lambda e: e.activation(out=rsc[:, 0:nblk], in_=b1p[0:64, 0:nblk], func=AF.Sqrt, bias=epsb[0:64, :]),
                                 r=['b1p', 'epsb'], w=['rsc'])
                            T.op('dve', lambda e: e.reciprocal(out=rsc[:, 0:nblk], in_=rsc[:, 0:nblk]), r=['rsc'], w=['rsc'])
                            T.op('dve', lambda e: e.scalar_tensor_tensor(out=KcT[g][0:64, 0:nblk], in0=cmp_[0:64, 0:nblk], scalar=gkt[:, 0:1],
                                                                         in1=rsc[:, 0:nblk], op0=ALU.mult, op1=ALU.mult),
                                 r=['cmp_', 'gkt', 'rsc'], w=[f'KcT{g}'])
                        else:
                            for ct in range(NCT):
                                T.op('pe', lambda e: e.matmul(cmp_[:, 0:64], lhsT=hid[:, ct * 128:(ct + 1) * 128], rhs=W2['v'][:, :],
                                                              start=True, stop=True), r=['W2v', 'hid'], w=['cmp_'])
                                T.op('dve', lambda e: e.tensor_copy(out=Vc[:, (ct * 2 + g) * 65:(ct * 2 + g) * 65 + 64], in_=cmp_[:, 0:64]),
                                     r=['cmp_', 'Vc_init'], w=[f'Vc{g}'])

            if dbg and not os.environ.get('NODUMP'):
                for g in range(2):
                    dk = nc.dram_tensor(f'dbg_kc{g}', [68, NCT * 128], BF16, kind="ExternalOutput").ap()
                    T.dma('sp', dk, KcT[g][:], r=[f'KcT{g}', f'KcTp{g}'], slot=f'dbgk{g}')
                dv = nc.dram_tensor('dbg_vc', [128, NCT * 130], BF16, kind="ExternalOutput").ap()
                T.dma('sp', dv, Vc[:], r=['Vc0', 'Vc1', 'Vc_init'], slot='dbgv')
            T.barrier()
            with ExitStack() as e3:
                Wq = sbt(e3, 'Wq', [128, 8, 536], BF16)
                eall = sbt(e3, 'eall', [128, SEQ], BF16)
                aaug = sbt(e3, 'aaug', [128, NCT * 128], BF16)
                wc = sbt(e3, 'wc', [128, 272], BF16)
                T.dma('sp', eall[:], di['c_eall'], w=['eall'], slot='c0')
                T.dma('sp', aaug[:], di['c_aaug'], w=['aaug'], slot='c1')
                T.dma('sp', wc[:], di['c_wc'], w=['wc'], slot='c2')
                kQ = load_w(Wq, di['w_in'], D, 512, 'Wq', col0=O1, gt=g1t)
                kQ += load_w(Wq[:, :, 512:536], di['w_in'], D, 24, 'Wg', col0=O3, gt=g1t)
                bankQ = pst(e3, 'bankQ', [128, 512])
                bankG = pst(e3, 'bankG', [128, 512])
                Sb = [pst(e3, f'S{k}', [128, 512]) for k in range(3)]
                Ob = [pst(e3, f'O{k}', [128, 512]) for k in range(2)]
                IMP = bankQ
                Pb = [sbt(e3, f'P{k}', [128, 512], BF16) for k in range(4)]
                QTs = [sbt(e3, f'QT{k}', [68, 1024], BF16) for k in range(2)]
                Pc = [sbt(e3, f'Pc{k}', [128, 512], BF16) for k in range(NCT)]
                sqq = sbt(e3, 'sqq', [128, 512])
                ss8 = sbt(e3, 'ss8', [128, 8])
                rq = sbt(e3, 'rq', [128, 8])
                qn = sbt(e3, 'qn', [128, 512], BF16)
                gsig = sbt(e3, 'gsig', [128, 24])
                fbs = [sbt(e3, f'fb{k}', [128, 128]) for k in range(2)]
                den4 = sbt(e3, 'den4', [128, 4])
                impa = sbt(e3, 'impa', [128, 128])
                m1 = sbt(e3, 'm1', [128, 8])
                m2 = sbt(e3, 'm2', [128, 8])
                tmpm = sbt(e3, 'tmpm', [128, 128])
                negms = [sbt(e3, f'negm{k}', [128, 128], BF16) for k in range(2)]
                nmT4 = [sbt(e3, f'nmT4{k}', [128, 512], BF16) for k in range(2)]
                osbs = [sbt(e3, f'osb{k}', [65, 512]) for k in range(4)]
                dd = sbt(e3, 'dd', [128, 4])
                rr = sbt(e3, 'rr', [128, 4])
                ynsa = sbt(e3, 'ynsa', [128, 512])
                ynb = sbt(e3, 'ynb', [128, 512], BF16)
                ynT = [sbt(e3, f'ynT{k}', [128, 512], BF16) for k in range(2)]
                cnt = {'s': 0, 'o': 0, 'c': 0}

                pend = []

                deferred = []

                def defer(n, fn):
                    deferred.append([n, fn])

                def tick():
                    for d in deferred:
                        d[0] -= 1
                    while deferred and deferred[0][0] <= 0:
                        deferred.pop(0)[1]()

                def emit_S(u):
                    k = cnt['s'] % 3
                    kp = cnt['s'] % 4
                    cnt['s'] += 1
                    u['kp'] = kp
                    S, P = Sb[k], Pb[kp]
                    masks = u['masks']
                    T.op('pe', lambda e: e.matmul(S[:], lhsT=u['KT'][0:68, u['lo']:u['lo'] + 128],
                                                  rhs=u['QT'][0:68, u['g'] * 512:(u['g'] + 1) * 512],
                                                  start=True, stop=(len(masks) == 0)), r=u['kkeys'] + [u['qk']], w=[f'S{k}'])
                    for mi, (ml, mr, mk) in enumerate(masks):
                        T.op('pe', lambda e: e.matmul(S[:], lhsT=ml, rhs=mr, start=False, stop=(mi == len(masks) - 1)),
                             r=mk, w=[f'S{k}'])
                    T.op('act', lambda e: e.activation(out=P[:], in_=S[:], func=AF.Exp), r=[f'S{k}'], w=[f'P{kp}'])

                def emit_PV(u):
                    kp = u['kp']
                    P = Pb[kp]
                    O = Ob[u['O']]
                    T.op('pe', lambda e: e.matmul(O[0:65, :], lhsT=u['Vt'][:, u['voff']:u['voff'] + 65], rhs=P[:],
                                                  start=u['first'], stop=u['last']), r=[f'P{kp}'] + u['vkeys'], w=[f"O{u['O']}"])
                    if u.get('imp_ct') is not None:
                        ic = u['imp_ct']
                        T.op('pool', lambda e: e.tensor_copy(out=Pc[ic][:], in_=P[:]), r=[f'P{kp}'], w=[f'Pc{ic}'])
                    if u.get('post') is not None:
                        u['post']()

                def push(u):
                    emit_S(u)
                    if len(pend) >= 2:
                        emit_PV(pend.pop(0))
                    pend.append(u)
                    tick()

                def flush():
                    while pend:
                        emit_PV(pend.pop(0))
                    while deferred:
                        deferred.pop(0)[1]()

                DBG_BR = int(os.environ.get('DBG_BR', '-1'))

                def combine(oi, g, br):
                    if DBG_BR >= 0 and br != DBG_BR:
                        return
                    ko = cnt['c'] % 4
                    cnt['c'] += 1
                    osb = osbs[ko]
                    Oacc = Ob[oi]
                    T.op('act', lambda e: e.activation(out=osb[:], in_=Oacc[0:65, :], func=AF.Copy), r=[f'O{oi}'], w=[f'osb{ko}'])
                    defer(2, lambda: combine2(ko, g, br))

                def combine2(ko, g, br):
                    osb = osbs[ko]
                    for hg in range(4):
                        T.op('pe', lambda e: e.transpose(bankG[:, hg * 65:(hg + 1) * 65], osb[0:65, hg * 128:(hg + 1) * 128], identf[0:65, 0:65]),
                             r=[f'osb{ko}', 'identf'], w=['bankG'])
                    ot = bankG[:, 0:260].rearrange("p (h d) -> p h d", h=4)
                    T.op('dve', lambda e: e.tensor_scalar(out=dd[:].unsqueeze(2), in0=ot[:, :, 64:65], scalar1=1e-30, scalar2=None, op0=ALU.max),
                         r=['bankG'], w=['dd'])
                    T.op('dve', lambda e: e.reciprocal(out=dd[:], in_=dd[:]), r=['dd'], w=['dd'])
                    if DBG_BR >= 0:
                        T.op('dve', lambda e: e.tensor_copy(out=rr[:], in_=dd[:]), r=['dd'], w=['rr'])
                    else:
                        T.op('dve', lambda e: e.tensor_tensor(out=rr[:], in0=dd[:], in1=gsig[:, br * 8 + g * 4:br * 8 + g * 4 + 4], op=ALU.mult),
                             r=['dd', 'gsig'], w=['rr'])
                    for hg in range(4):
                        hh = g * 4 + hg
                        ysl = ynsa[:, hh * 64:(hh + 1) * 64]
                        if br == 0 or DBG_BR >= 0:
                            T.op('dve', lambda e: e.tensor_scalar(out=ysl, in0=bankG[:, hg * 65:hg * 65 + 64], scalar1=rr[:, hg:hg + 1],
                                                                  scalar2=None, op0=ALU.mult), r=['bankG', 'rr'], w=['ynsa'])
                        else:
                            T.op('dve', lambda e: e.scalar_tensor_tensor(out=ysl, in0=bankG[:, hg * 65:hg * 65 + 64], scalar=rr[:, hg:hg + 1],
                                                                         in1=ysl, op0=ALU.mult, op1=ALU.add), r=['bankG', 'rr', 'ynsa'], w=['ynsa'])

                for i in range(NT if '3' in PH else 0):
                    xt, hT, s = rms_hT(i, x_d)
                    QT = QTs[i % 2]
                    qk = f'QT{i % 2}'
                    fb = fbs[i % 2]
                    for c in range(8):
                        T.op('pe', lambda e: e.matmul(bankQ[:], lhsT=hT[:, c * 128:(c + 1) * 128], rhs=Wq[:, c, 0:512],
                                                      start=(c == 0), stop=(c == 7)), r=[f'hT{s}'] + kQ, w=['bankQ'])
                    for c in range(8):
                        T.op('pe', lambda e: e.matmul(bankG[:, 0:24], lhsT=hT[:, c * 128:(c + 1) * 128], rhs=Wq[:, c, 512:536],
                                                      start=(c == 0), stop=(c == 7)), r=[f'hT{s}'] + kQ, w=['bankG'])
                    T.op('act', lambda e: e.activation(out=sqq[:], in_=bankQ[:], func=AF.Square), r=['bankQ'], w=['sqq'])
                    T.op('dve', lambda e: e.tensor_reduce(out=ss8[:], in_=sqq[:].rearrange("p (h d) -> p h d", h=8), axis=AX.X, op=ALU.add),
                         r=['sqq'], w=['ss8'])
                    T.op('act', lambda e: e.activation(out=rq[:], in_=ss8[:], func=AF.Sqrt, bias=epsb[:], scale=1.0 / 64), r=['ss8', 'epsb'], w=['rq'])
                    T.op('dve', lambda e: e.reciprocal(out=rq[:], in_=rq[:]), r=['rq'], w=['rq'])
                    T.op('dve', lambda e: e.tensor_tensor(out=qn[:].rearrange("p (h d) -> p h d", h=8),
                                                          in0=bankQ[:].rearrange("p (h d) -> p h d", h=8),
                                                          in1=rq[:].unsqueeze(2).broadcast_to([128, 8, 64]), op=ALU.mult),
                         r=['bankQ', 'rq'], w=['qn'])
                    for h in range(8):
                        T.op('pe', lambda e: e.transpose(tp[0:64, h * 128:(h + 1) * 128], qn[:, h * 64:(h + 1) * 64], identb[:]),
                             r=['qn', 'identb'], w=['tp'])
                    T.op('act', lambda e: e.activation(out=QT[0:64, :], in_=tp[0:64, :], func=AF.Copy, scale=gq8[:]), r=['tp', 'gq8'], w=[qk])
                    T.dma('sp', QT[64:68, :].rearrange("p (h q) -> p h q", h=8), di['c_qpos'][:, :, i * 128:(i + 1) * 128],
                          w=[qk + 'p'], slot=qk + 'p')
                    qk2 = [qk, qk + 'p']
                    T.op('act', lambda e: e.activation(out=gsig[:], in_=bankG[:, 0:24], func=AF.Sigmoid), r=['bankG'], w=['gsig'])
                    T.dma('sp', fb[:], di['c_fb'][i * 128:(i + 1) * 128, :], w=[f'fb{i % 2}'], slot=f'fb{i % 2}')
                    if dbg and not os.environ.get('NODUMP'):
                        if i == 0:
                            dgs = nc.dram_tensor('dbg_gsig', [NT * 128, 24], F32, kind="ExternalOutput").ap()
                        T.dma('sp', dgs[i * 128:(i + 1) * 128, :], gsig[:], r=['gsig'], slot='dbggs')
                    n_ct = min(NCT, (8 * i + 6) // 128 + 1)

                    def imp_a(g, n_ct=n_ct, fb=fb, i=i):
                        for h in range(4):
                            for ct in range(n_ct):
                                T.op('pe', lambda e: e.matmul(IMP[:, h * 128:(h + 1) * 128], lhsT=Pc[ct][:, h * 128:(h + 1) * 128],
                                                              rhs=aaug[:, ct * 128:(ct + 1) * 128], start=(ct == 0), stop=(ct == n_ct - 1)),
                                     r=[f'Pc{ct}', 'aaug'], w=['bankQ'])
                        iv = IMP[:].rearrange("p (h n) -> p h n", h=4)
                        T.op('dve', lambda e: e.tensor_scalar(out=den4[:].unsqueeze(2), in0=iv[:, :, 127:128], scalar1=1e-30, scalar2=None, op0=ALU.max),
                             r=['bankQ'], w=['den4'])
                        T.op('dve', lambda e: e.reciprocal(out=den4[:], in_=den4[:]), r=['den4'], w=['den4'])
                        T.op('dve', lambda e: e.tensor_scalar(out=impa[:], in0=IMP[:, 0:128], scalar1=den4[:, 0:1], scalar2=None, op0=ALU.mult),
                             r=['bankQ', 'den4'], w=['impa'])
                        for h in range(1, 4):
                            T.op('dve', lambda e: e.scalar_tensor_tensor(out=impa[:], in0=IMP[:, h * 128:(h + 1) * 128], scalar=den4[:, h:h + 1],
                                                                         in1=impa[:], op0=ALU.mult, op1=ALU.add), r=['bankQ', 'den4', 'impa'], w=['impa'])
                        T.op('dve', lambda e: e.tensor_tensor(out=impa[:], in0=impa[:], in1=fb[:], op=ALU.add), r=['impa', f'fb{i % 2}'], w=['impa'])
                        T.op('dve', lambda e: e.max(out=m1[:], in_=impa[:]), r=['impa'], w=['m1'])
                        T.op('dve', lambda e: e.match_replace(out=tmpm[:], in_to_replace=m1[:], in_values=impa[:], imm_value=-3e38),
                             r=['impa', 'm1'], w=['tmpm'])
                        T.op('dve', lambda e: e.max(out=m2[:], in_=tmpm[:]), r=['tmpm'], w=['m2'])
                        T.op('dve', lambda e: e.tensor_scalar(out=negms[g][:], in0=impa[:], scalar1=m2[:, 7:8], scalar2=NEG, op0=ALU.is_lt, op1=ALU.mult),
                             r=['impa', 'm2'], w=[f'negm{g}'])
                        defer(3, lambda: imp_b(g))

                    def imp_b(g):
                        T.op('pe', lambda e: e.transpose(tp[:, 0:128], negms[g][:], identb[:]), r=[f'negm{g}', 'identb'], w=['tp'])
                        T.op('dve', lambda e: e.tensor_copy(out=nmT4[g][:].rearrange("p (h q) -> p h q", h=4),
                                                            in_=tp[:, 0:128].unsqueeze(1).broadcast_to([128, 4, 128])), r=['tp'], w=[f'nmT4{g}'])

                    def post_cmp(oi, g):
                        combine(oi, g, 0)
                        defer(2, lambda: imp_a(g))

                    obank = {}
                    for g in range(2):
                        oi = cnt['o'] % 2
                        cnt['o'] += 1
                        for ct in range(n_ct):
                            off = 128 * ct - 8 * i + 2
                            masks = []
                            if off > -127:
                                masks.append((wc[:, 127 + off:127 + off + 128], i4[:], ['wc', 'i4']))
                            push(dict(KT=KcT[g], QT=QT, g=g, qk=qk, lo=ct * 128, masks=masks, Vt=Vc, voff=(ct * 2 + g) * 65, O=oi,
                                      first=(ct == 0), last=(ct == n_ct - 1), kkeys=[f'KcT{g}', f'KcTp{g}', qk + 'p'],
                                      vkeys=[f'Vc{g}', 'Vc_init'], imp_ct=ct,
                                      post=(lambda oi=oi, g=g: post_cmp(oi, g)) if ct == n_ct - 1 else None))
                        if g == 0:
                            flush()
                    for g in range(2):
                        oi = cnt['o'] % 2
                        cnt['o'] += 1
                        k0 = max(0, i - 4)
                        for kt in range(k0, i + 1):
                            masks = []
                            if kt == i:
                                masks.append((negtri[:], i4[:], ['negtri', 'i4']))
                            if kt == i - 4:
                                masks.append((negtric[:], i4[:], ['negtric', 'i4']))
                            push(dict(KT=KwinT[g], QT=QT, g=g, qk=qk, lo=kt * 128, masks=masks, Vt=Vwin, voff=kt * 130 + g * 65, O=oi,
                                      first=(kt == k0), last=(kt == i), kkeys=[f'K2_{g}_{kt}', f'KwinTp{g}', qk + 'p'], vkeys=[f'Vw_{kt}'],
                                      post=(lambda oi=oi, g=g: combine(oi, g, 2)) if kt == i else None))
                    flush()
                    for g in range(2):
                        oi = cnt['o'] % 2
                        cnt['o'] += 1
                        for kt in range(i + 1):
                            masks = [(eall[:, kt * 128:(kt + 1) * 128], nmT4[g][:], ['eall', f'nmT4{g}'])]
                            if kt == i:
                                masks.append((negtri[:], i4[:], ['negtri', 'i4']))
                            push(dict(KT=KselT[g], QT=QT, g=g, qk=qk, lo=kt * 128, masks=masks, Vt=Vsel, voff=kt * 130 + g * 65, O=oi,
                                      first=(kt == 0), last=(kt == i), kkeys=[f'K1_{g}_{kt}', f'KselTp{g}', qk + 'p'], vkeys=[f'Vs_{kt}'],
                                      post=(lambda oi=oi, g=g: combine(oi, g, 1)) if kt == i else None))
                    flush()
                    yT = ynT[i % 2]
                    T.op('act', lambda e: e.activation(out=ynb[:], in_=ynsa[:], func=AF.Copy), r=['ynsa'], w=['ynb'])
                    for c in range(4):
                        T.op('pe', lambda e: e.transpose(tp[:, c * 128:(c + 1) * 128], ynb[:, c * 128:(c + 1) * 128], identb[:]),
                             r=['ynb', 'identb'], w=['tp'])
                    T.op('dve', lambda e: e.tensor_copy(out=yT[:], in_=tp[:, 0:512]), r=['tp'], w=[f'ynT{i % 2}'])
                    T.dma('sp', yn_d[i], yT[:], r=[f'ynT{i % 2}'], w=[f'yn_{i}'], slot=f'ynT{i % 2}')

        T.barrier()
        with ExitStack() as e4:
            ctab = sbt(e4, 'ctab', [128, 2048])
            stab = sbt(e4, 'stab', [128, 2048])
            BbT = [sbt(e4, f'BbT{k}', [128, 2048], BF16) for k in range(2)]
            Cb = [sbt(e4, f'Cb{k}', [128, 2048], BF16) for k in range(2)]
            mag = sbt(e4, 'mag', [128, 16])
            rotc = sbt(e4, 'rotc', [128, 16])
            rots = sbt(e4, 'rots', [128, 16])
            dsk = sbt(e4, 'dsk', [128, 4])
            bglu = sbt(e4, 'bglu', [128, 4])
            T.dma('sp', dsk[:], di['dsk'], w=['dsk'], slot='c0')
            T.dma('sp', bglu[:], di['bglu'], w=['bglu'], slot='c1')
            Wu = sbt(e4, 'Wu', [128, 8, 512], BF16)
            Wgl = sbt(e4, 'Wgl', [128, 4, 512], BF16)
            kU = load_w(Wu, di['w_in'], D, 512, 'Wu', col0=0, gt=g1t)
            kGl = load_w(Wgl, di['w_glu'], 512, 512, 'Wgl')

            def sincos(st, phi, n, s_out, c_out, nm):
                u = sbt(st, nm + 'u', [128, n])
                ki = sbt(st, nm + 'ki', [128, n], I32)
                kf = sbt(st, nm + 'kf', [128, n])
                fx = sbt(st, nm + 'fx', [128, n])
                r_ = sbt(st, nm + 'r', [128, n])
                for shift, dst in ((0.0, s_out), (np.pi / 2, c_out)):
                    T.op('dve', lambda e: e.tensor_scalar(out=u[:], in0=phi, scalar1=shift, scalar2=1.0 / (2 * np.pi), op0=ALU.add, op1=ALU.mult),
                         r=[nm + 'phi'], w=[nm + 'u'])
                    T.op('dve', lambda e: e.tensor_copy(out=ki[:], in_=u[:]), r=[nm + 'u'], w=[nm + 'ki'])
                    T.op('dve', lambda e: e.tensor_copy(out=kf[:], in_=ki[:]), r=[nm + 'ki'], w=[nm + 'kf'])
                    T.op('dve', lambda e: e.tensor_scalar(out=r_[:], in0=phi, scalar1=shift, scalar2=None, op0=ALU.add), r=[nm + 'phi'], w=[nm + 'r'])
                    T.op('dve', lambda e: e.scalar_tensor_tensor(out=r_[:], in0=kf[:], scalar=-2 * np.pi, in1=r_[:], op0=ALU.mult, op1=ALU.add),
                         r=[nm + 'kf', nm + 'r'], w=[nm + 'r'])
                    T.op('dve', lambda e: e.tensor_scalar(out=fx[:], in0=r_[:], scalar1=np.pi, scalar2=-2 * np.pi, op0=ALU.is_gt, op1=ALU.mult),
                         r=[nm + 'r'], w=[nm + 'fx'])
                    T.op('dve', lambda e: e.tensor_tensor(out=r_[:], in0=r_[:], in1=fx[:], op=ALU.add), r=[nm + 'r', nm + 'fx'], w=[nm + 'r'])
                    T.op('dve', lambda e: e.tensor_scalar(out=fx[:], in0=r_[:], scalar1=-np.pi, scalar2=2 * np.pi, op0=ALU.is_lt, op1=ALU.mult),
                         r=[nm + 'r'], w=[nm + 'fx'])
                    T.op('dve', lambda e: e.tensor_tensor(out=r_[:], in0=r_[:], in1=fx[:], op=ALU.add), r=[nm + 'r', nm + 'fx'], w=[nm + 'r'])
                    T.op('dve', lambda e: e.tensor_scalar(out=r_[:], in0=r_[:], scalar1=3.14159, scalar2=-3.14159, op0=ALU.min, op1=ALU.max),
                         r=[nm + 'r'], w=[nm + 'r'])
                    T.op('act', lambda e: e.activation(out=dst, in_=r_[:], func=AF.Sin), r=[nm + 'r'], w=[nm + 'out'])

            with ExitStack() as e0:
                sm = lambda n, s=[128, 16]: sbt(e0, n, s)
                are, aim, ldt = sm('are'), sm('aim'), sm('ldt')
                T.dma('sp', are[:], di['are'], w=['are'], slot='c2')
                T.dma('sp', aim[:], di['aim'], w=['aim'], slot='c3')
                T.dma('sp', ldt[:], di['ldt'], w=['ldt'], slot='c4')
                dt_, adr, adi, cs_, sn_ = sm('dt_'), sm('adr'), sm('adi'), sm('cs_'), sm('sn_')
                abr, abi, nr, den, fr, fi, t0, t1 = sm('abr'), sm('abi'), sm('nr'), sm('den'), sm('fr'), sm('fi'), sm('t0'), sm('t1')
                a128 = sm('a128')
                V = lambda fn, r, w: T.op('dve', fn, r=r, w=w)
                T.op('act', lambda e: e.activation(out=dt_[:], in_=ldt[:], func=AF.Exp), r=['ldt'], w=['dt_'])
                V(lambda e: e.tensor_tensor(out=adr[:], in0=are[:], in1=dt_[:], op=ALU.mult), ['are', 'dt_'], ['adr'])
                V(lambda e: e.tensor_tensor(out=adi[:], in0=aim[:], in1=dt_[:], op=ALU.mult), ['aim', 'dt_'], ['s0phi'])
                T.op('act', lambda e: e.activation(out=mag[:], in_=adr[:], func=AF.Exp), r=['adr'], w=['mag'])
                sincos(e0, adi[:], 16, sn_[:], cs_[:], 's0')
                V(lambda e: e.tensor_tensor(out=abr[:], in0=mag[:], in1=cs_[:], op=ALU.mult), ['mag', 's0out'], ['abr'])
                V(lambda e: e.tensor_tensor(out=abi[:], in0=mag[:], in1=sn_[:], op=ALU.mult), ['mag', 's0out'], ['abi'])
                V(lambda e: e.tensor_scalar(out=nr[:], in0=abr[:], scalar1=-1.0, scalar2=None, op0=ALU.add), ['abr'], ['nr'])
                V(lambda e: e.tensor_tensor(out=den[:], in0=are[:], in1=are[:], op=ALU.mult), ['are'], ['den'])
                V(lambda e: e.tensor_tensor(out=t0[:], in0=aim[:], in1=aim[:], op=ALU.mult), ['aim'], ['t0'])
                V(lambda e: e.tensor_tensor(out=den[:], in0=den[:], in1=t0[:], op=ALU.add), ['den', 't0'], ['den'])
                V(lambda e: e.reciprocal(out=den[:], in_=den[:]), ['den'], ['den'])
                V(lambda e: e.tensor_tensor(out=t0[:], in0=nr[:], in1=are[:], op=ALU.mult), ['nr', 'are'], ['t0'])
                V(lambda e: e.tensor_tensor(out=t1[:], in0=abi[:], in1=aim[:], op=ALU.mult), ['abi', 'aim'], ['t1'])
                V(lambda e: e.tensor_tensor(out=t0[:], in0=t0[:], in1=t1[:], op=ALU.add), ['t0', 't1'], ['t0'])
                V(lambda e: e.tensor_tensor(out=fr[:], in0=t0[:], in1=den[:], op=ALU.mult), ['t0', 'den'], ['fr'])
                V(lambda e: e.tensor_tensor(out=t0[:], in0=abi[:], in1=are[:], op=ALU.mult), ['abi', 'are'], ['t0'])
                V(lambda e: e.tensor_tensor(out=t1[:], in0=nr[:], in1=aim[:], op=ALU.mult), ['nr', 'aim'], ['t1'])
                V(lambda e: e.tensor_tensor(out=t0[:], in0=t0[:], in1=t1[:], op=ALU.subtract), ['t0', 't1'], ['t0'])
                V(lambda e: e.tensor_tensor(out=fi[:], in0=t0[:], in1=den[:], op=ALU.mult), ['t0', 'den'], ['fi'])
                V(lambda e: e.tensor_scalar(out=a128[:], in0=adi[:], scalar1=128.0, scalar2=None, op0=ALU.mult), ['s0phi'], ['s1phi'])
                sincos(e0, a128[:], 16, rots[:], rotc[:], 's1')
                tp1 = sbt(e0, 'tp1', [128, 128])
                T.dma('sp', tp1[:], di['c_tp1'], w=['tp1'], slot='c0')
                phi = sbt(e0, 'phi', [128, 2048])
                V(lambda e: e.tensor_tensor(out=phi[:].rearrange("p (j t) -> p j t", j=16),
                                            in0=adi[:].unsqueeze(2).broadcast_to([128, 16, 128]),
                                            in1=tp1[:].unsqueeze(1).broadcast_to([128, 16, 128]), op=ALU.mult), ['s0phi', 'tp1'], ['s2phi'])
                sincos(e0, phi[:], 2048, stab[:], ctab[:], 's2')
                bre = sbt(e0, 'bre', [128, 2048])
                bim = sbt(e0, 'bim', [128, 2048])
                u1 = sbt(e0, 'u1', [128, 2048])
                u2 = sbt(e0, 'u2', [128, 2048])
                bb = [sbt(e0, f'bb{k}', [128, 2048]) for k in range(2)]
                T.dma('sp', bre[:], di['bre'], w=['bre'], slot='c1')
                T.dma('sp', bim[:], di['bim'], w=['bim'], slot='c2')
                v3 = lambda t_: t_[:].rearrange("p (j c) -> p j c", j=16)
                bc = lambda t_: t_[:].unsqueeze(2).broadcast_to([128, 16, 128])
                V(lambda e: e.tensor_tensor(out=v3(u1), in0=v3(bre), in1=bc(fr), op=ALU.mult), ['bre', 'fr'], ['u1'])
                V(lambda e: e.tensor_tensor(out=v3(u2), in0=v3(bim), in1=bc(fi), op=ALU.mult), ['bim', 'fi'], ['u2'])
                V(lambda e: e.tensor_tensor(out=bb[0][:], in0=u1[:], in1=u2[:], op=ALU.subtract), ['u1', 'u2'], ['bb0'])
                V(lambda e: e.tensor_tensor(out=v3(u1), in0=v3(bim), in1=bc(fr), op=ALU.mult), ['bim', 'fr', 'bb0'], ['u1'])
                V(lambda e: e.tensor_tensor(out=v3(u2), in0=v3(bre), in1=bc(fi), op=ALU.mult), ['bre', 'fi', 'bb0'], ['u2'])
                V(lambda e: e.tensor_tensor(out=bb[1][:], in0=u1[:], in1=u2[:], op=ALU.add), ['u1', 'u2'], ['bb1'])
                trp = pst(e0, 'trp', [128, 512])
                for k in range(2):
                    for jg in range(4):
                        for jj in range(4):
                            j = jg * 4 + jj
                            T.op('pe', lambda e: e.transpose(trp[:, jj * 128:(jj + 1) * 128], bb[k][:, j * 128:(j + 1) * 128], identf[:]),
                                 r=[f'bb{k}', 'identf'], w=['trp'])
                        T.op('dve', lambda e: e.tensor_copy(out=BbT[k][:, jg * 512:(jg + 1) * 512], in_=trp[:]), r=['trp'], w=[f'BbT{k}'])
                T.dma('sp', bre[:], di['cre'], r=['u1', 'u2'], w=['bre'], slot='c1')
                T.dma('sp', bim[:], di['cim'], r=['u1', 'u2'], w=['bim'], slot='c2')
                T.op('act', lambda e: e.activation(out=Cb[0][:], in_=bre[:], func=AF.Copy), r=['bre'], w=['Cb0'])
                T.op('act', lambda e: e.activation(out=Cb[1][:], in_=bim[:], func=AF.Copy, scale=-1.0), r=['bim'], w=['Cb1'])

            T.barrier()
            with ExitStack() as e5:
                bankU = pst(e5, 'bankU', [128, 512])
                bu = [[pst(e5, f'bu{a}{b}', [128, 512]) for b in range(2)] for a in range(2)]
                yTp = pst(e5, 'yTp', [128, 512])
                zp = pst(e5, 'zp', [128, 512])
                ub = sbt(e5, 'ub', [128, 512], BF16)
                nm12 = ('t1_', 't2_', 't3_', 't4_', 'vr', 'vi', 'wr', 'wi', 'a1', 'a2', 'a3', 'a4')
                f4b = {n: [sbt(e5, f'{n}{b}', [128, 512]) for b in range(2)] for n in nm12}
                xrb = [sbt(e5, f'xr{b}', [128, 512], BF16) for b in range(2)]
                xib = [sbt(e5, f'xi{b}', [128, 512], BF16) for b in range(2)]
                w0 = [[sbt(e5, f'w0{p}{k}', [128, 16]) for k in range(2)] for p in range(2)]
                q1b = [sbt(e5, f'q1{b}', [128, 4]) for b in range(2)]
                q2b = [sbt(e5, f'q2{b}', [128, 4]) for b in range(2)]
                ysb = sbt(e5, 'ysb', [128, 512])
                ygb = sbt(e5, 'ygb', [128, 512], BF16)
                sgb = sbt(e5, 'sgb', [128, 512])
                yso = [sbt(e5, f'yso{k}', [128, 512], BF16) for k in range(2)]
                for k in range(2):
                    T.op('dve', lambda e: e.memset(w0[0][k][:], 0.0), w=[f'w00{k}'])
                for i in range(NT if 'S' in PH else 0):
                    xt, hT, s = rms_hT(i, x_d)
                    p, pn = i % 2, (i + 1) % 2
                    for sg in range(4):
                        for c in range(8):
                            T.op('pe', lambda e: e.matmul(bankU[:, sg * 128:(sg + 1) * 128], lhsT=Wu[:, c, sg * 128:(sg + 1) * 128],
                                                          rhs=hT[:, c * 128:(c + 1) * 128], start=(c == 0), stop=(c == 7)),
                                 r=[f'hT{s}'] + kU, w=['bankU'])
                    T.op('act', lambda e: e.activation(out=ub[:], in_=bankU[:], func=AF.Copy), r=['bankU'], w=['ub'])
                    for sg in range(4):
                        B_ = sg % 2
                        t1_, t2_, t3_, t4_, vr, vi, wr, wi, a1, a2, a3, a4 = [f4b[n][B_] for n in nm12]
                        xr, xi, q1, q2 = xrb[B_], xib[B_], q1b[B_], q2b[B_]
                        bur, bui = bu[sg % 2]
                        kr, ki_ = f'bu{sg % 2}0', f'bu{sg % 2}1'
                        cs = ctab[:, sg * 512:(sg + 1) * 512]
                        ss = stab[:, sg * 512:(sg + 1) * 512]
                        for jj in range(4):
                            j = sg * 4 + jj
                            for k, (bt, kk) in enumerate(((bur, kr), (bui, ki_))):
                                T.op('pe', lambda e: e.matmul(bt[:, jj * 128:(jj + 1) * 128], lhsT=BbT[k][:, j * 128:(j + 1) * 128],
                                                              rhs=ub[:, sg * 128:(sg + 1) * 128], start=True, stop=True),
                                     r=['ub', f'BbT{k}'], w=[kk])
                        T.op('dve', lambda e: e.tensor_tensor(out=t1_[:], in0=bur[:], in1=cs, op=ALU.mult), r=[kr, 's2out'], w=['t1_' + str(B_)])
                        T.op('dve', lambda e: e.tensor_tensor(out=t2_[:], in0=bui[:], in1=ss, op=ALU.mult), r=[ki_, 's2out'], w=['t2_' + str(B_)])
                        T.op('pool', lambda e: e.tensor_tensor(out=vr[:], in0=t1_[:], in1=t2_[:], op=ALU.add), r=['t1_' + str(B_), 't2_' + str(B_)], w=['vr' + str(B_)])
                        T.op('dve', lambda e: e.tensor_tensor(out=t3_[:], in0=bui[:], in1=cs, op=ALU.mult), r=[ki_, 's2out'], w=['t3_' + str(B_)])
                        T.op('dve', lambda e: e.tensor_tensor(out=t4_[:], in0=bur[:], in1=ss, op=ALU.mult), r=[kr, 's2out'], w=['t4_' + str(B_)])
                        T.op('pool', lambda e: e.tensor_tensor(out=vi[:], in0=t3_[:], in1=t4_[:], op=ALU.subtract), r=['t3_' + str(B_), 't4_' + str(B_)], w=['vi' + str(B_)])
                        for jj in range(4):
                            j = sg * 4 + jj
                            sl = slice(jj * 128, (jj + 1) * 128)
                            for k, (vv, ww, nm) in enumerate(((vr, wr, 'wr' + str(B_)), (vi, wi, 'wi' + str(B_)))):
                                T.op('dve', lambda e: e.tensor_tensor_scan(out=ww[:, sl], data0=mag[:, j:j + 1].broadcast_to([128, 128]),
                                                                           data1=vv[:, sl], initial=w0[p][k][:, j:j + 1],
                                                                           op0=ALU.mult, op1=ALU.add),
                                     r=['mag', 'vr' + str(B_) if k == 0 else 'vi' + str(B_), f'w0{p}{k}'], w=[nm])
                        T.op('pool', lambda e: e.tensor_tensor(out=a1[:], in0=wr[:], in1=cs, op=ALU.mult), r=['wr' + str(B_), 's2out'], w=['a1' + str(B_)])
                        T.op('pool', lambda e: e.tensor_tensor(out=a2[:], in0=wi[:], in1=ss, op=ALU.mult), r=['wi' + str(B_), 's2out'], w=['a2' + str(B_)])
                        T.op('pool', lambda e: e.tensor_tensor(out=xr[:], in0=a1[:], in1=a2[:], op=ALU.subtract), r=['a1' + str(B_), 'a2' + str(B_)], w=['xr' + str(B_)])
                        T.op('pool', lambda e: e.tensor_tensor(out=a3[:], in0=wi[:], in1=cs, op=ALU.mult), r=['wi' + str(B_), 's2out'], w=['a3' + str(B_)])
                        T.op('pool', lambda e: e.tensor_tensor(out=a4[:], in0=wr[:], in1=ss, op=ALU.mult), r=['wr' + str(B_), 's2out'], w=['a4' + str(B_)])
                        T.op('pool', lambda e: e.tensor_tensor(out=xi[:], in0=a3[:], in1=a4[:], op=ALU.add), r=['a3' + str(B_), 'a4' + str(B_)], w=['xi' + str(B_)])
                        wl_r = wr[:].rearrange("p (j t) -> p j t", j=4)[:, :, 127:128]
                        wl_i = wi[:].rearrange("p (j t) -> p j t", j=4)[:, :, 127:128]
                        rc = rotc[:, sg * 4:(sg + 1) * 4].unsqueeze(2)
                        rs = rots[:, sg * 4:(sg + 1) * 4].unsqueeze(2)
                        n_r = w0[pn][0][:, sg * 4:(sg + 1) * 4].unsqueeze(2)
                        n_i = w0[pn][1][:, sg * 4:(sg + 1) * 4].unsqueeze(2)
                        T.op('dve', lambda e: e.tensor_tensor(out=q1[:].unsqueeze(2), in0=wl_r, in1=rc, op=ALU.mult), r=['wr' + str(B_), 's1out'], w=['q1' + str(B_)])
                        T.op('dve', lambda e: e.tensor_tensor(out=q2[:].unsqueeze(2), in0=wl_i, in1=rs, op=ALU.mult), r=['wi' + str(B_), 's1out'], w=['q2' + str(B_)])
                        T.op('dve', lambda e: e.tensor_tensor(out=n_r, in0=q1[:].unsqueeze(2), in1=q2[:].unsqueeze(2), op=ALU.subtract),
                             r=['q1' + str(B_), 'q2' + str(B_)], w=[f'w0{pn}0'])
                        T.op('dve', lambda e: e.tensor_tensor(out=q1[:].unsqueeze(2), in0=wl_i, in1=rc, op=ALU.mult), r=['wi' + str(B_), 's1out', f'w0{pn}0'], w=['q1' + str(B_)])
                        T.op('dve', lambda e: e.tensor_tensor(out=q2[:].unsqueeze(2), in0=wl_r, in1=rs, op=ALU.mult), r=['wr' + str(B_), 's1out', f'w0{pn}0'], w=['q2' + str(B_)])
                        T.op('dve', lambda e: e.tensor_tensor(out=n_i, in0=q1[:].unsqueeze(2), in1=q2[:].unsqueeze(2), op=ALU.add),
                             r=['q1' + str(B_), 'q2' + str(B_)], w=[f'w0{pn}1'])
                        for jj in range(4):
                            j = sg * 4 + jj
                            sl = slice(jj * 128, (jj + 1) * 128)
                            T.op('pe', lambda e: e.matmul(yTp[:, sg * 128:(sg + 1) * 128], lhsT=Cb[0][:, j * 128:(j + 1) * 128], rhs=xr[:, sl],
                                                          start=(jj == 0), stop=False), r=['Cb0', 'xr' + str(B_)], w=['yTp'])
                            T.op('pe', lambda e: e.matmul(yTp[:, sg * 128:(sg + 1) * 128], lhsT=Cb[1][:, j * 128:(j + 1) * 128], rhs=xi[:, sl],
                                                          start=False, stop=(jj == 3)), r=['Cb1', 'xi' + str(B_)], w=['yTp'])
                    for sg in range(4):
                        sl = slice(sg * 128, (sg + 1) * 128)
                        T.op('dve', lambda e: e.scalar_tensor_tensor(out=ysb[:, sl], in0=ub[:, sl], scalar=dsk[:, sg:sg + 1], in1=yTp[:, sl],
                                                                     op0=ALU.mult, op1=ALU.add), r=['ub', 'dsk', 'yTp'], w=['ysb'])
                    T.op('act', lambda e: e.activation(out=ygb[:], in_=ysb[:], func=AF.Gelu_apprx_tanh), r=['ysb'], w=['ygb'])
                    for co in range(4):
                        for ci in range(4):
                            T.op('pe', lambda e: e.matmul(zp[:, co * 128:(co + 1) * 128], lhsT=Wgl[:, ci, co * 128:(co + 1) * 128],
                                                          rhs=ygb[:, ci * 128:(ci + 1) * 128], start=(ci == 0), stop=(ci == 3)),
                                 r=['ygb'] + kGl, w=['zp'])
                    for co in range(4):
                        sl = slice(co * 128, (co + 1) * 128)
                        T.op('act', lambda e: e.activation(out=sgb[:, sl], in_=zp[:, sl], func=AF.Sigmoid, bias=bglu[:, co:co + 1]),
                             r=['zp', 'bglu'], w=['sgb'])
                    yo = yso[i % 2]
                    T.op('pool', lambda e: e.tensor_tensor(out=yo[:], in0=ygb[:], in1=sgb[:], op=ALU.mult), r=['ygb', 'sgb'], w=[f'yso{i % 2}'])
                    T.dma('sp', ys_d[i], yo[:], r=[f'yso{i % 2}'], w=[f'ys_{i}'], slot=f'yso{i % 2}')

        T.barrier()
        with ExitStack() as e6:
            Wmg = sbt(e6, 'Wmg', [128, 8, 2048], BF16)
            Wps = sbt(e6, 'Wps', [128, 4, D], BF16)
            Wpn = sbt(e6, 'Wpn', [128, 4, D], BF16)
            Wo = sbt(e6, 'Wo', [128, 8, D], BF16)
            kMg = load_w(Wmg, di['w_in'], D, 2048, 'Wmg', col0=O4, gt=g1t)
            kPs = load_w(Wps, di['w_ps'], 512, D, 'Wps')
            kPn = load_w(Wpn, di['w_pn'], 512, D, 'Wpn')
            kWo = load_w(Wo, di['w_out'], D, D, 'Wo')
            Gp = [pst(e6, f'Gp{k}', [128, 512]) for k in range(2)]
            Pp = [pst(e6, f'Pp{k}', [128, 512]) for k in range(2)]
            Op = [pst(e6, f'Op{k}', [128, 512]) for k in range(2)]
            gs = [sbt(e6, f'gs{k}', [128, 512]) for k in range(2)]
            mt = [sbt(e6, f'mt{k}', [128, 512]) for k in range(2)]
            mrg = sbt(e6, 'mrg', [128, D], BF16)
            mT = sbt(e6, 'mT', [128, D], BF16)
            ysl_ = [sbt(e6, f'ysl{k}', [128, 512], BF16) for k in range(2)]
            ynl_ = [sbt(e6, f'ynl{k}', [128, 512], BF16) for k in range(2)]
            x1s = [sbt(e6, f'x1s{k}', [128, D]) for k in range(2)]
            for i in range(NT if '4' in PH else 0):
                xt, hT, s = rms_hT(i, x_d)
                q = i % 2
                T.dma('sp', ysl_[q][:], ys_d[i], r=[f'ys_{i}'], w=[f'ysl{q}'], slot=f'ysl{q}')
                T.dma('sp', ynl_[q][:], yn_d[i], r=[f'yn_{i}'], w=[f'ynl{q}'], slot=f'ynl{q}')
                for hf in range(2):
                    hs = slice(hf * 512, (hf + 1) * 512)
                    for m_ in range(2):
                        for c in range(8):
                            T.op('pe', lambda e: e.matmul(Gp[m_][:], lhsT=hT[:, c * 128:(c + 1) * 128],
                                                          rhs=Wmg[:, c, m_ * 1024 + hf * 512:m_ * 1024 + (hf + 1) * 512],
                                                          start=(c == 0), stop=(c == 7)), r=[f'hT{s}'] + kMg, w=[f'Gp{m_}'])
                        T.op('act', lambda e: e.activation(out=gs[m_][:], in_=Gp[m_][:], func=AF.Sigmoid), r=[f'Gp{m_}'], w=[f'gs{m_}'])
                        yl, Wp, kk, kn_ = ((ysl_[q], Wps, kPs, f'ysl{q}'), (ynl_[q], Wpn, kPn, f'ynl{q}'))[m_]
                        for ci in range(4):
                            T.op('pe', lambda e: e.matmul(Pp[m_][:], lhsT=yl[:, ci * 128:(ci + 1) * 128], rhs=Wp[:, ci, hs],
                                                          start=(ci == 0), stop=(ci == 3)), r=[kn_] + kk, w=[f'Pp{m_}'])
                        T.op('dve', lambda e: e.tensor_tensor(out=mt[m_][:], in0=Pp[m_][:], in1=gs[m_][:], op=ALU.mult),
                             r=[f'Pp{m_}', f'gs{m_}'], w=[f'mt{m_}'])
                    T.op('pool', lambda e: e.tensor_tensor(out=mrg[:, hs], in0=mt[0][:], in1=mt[1][:], op=ALU.add), r=['mt0', 'mt1'], w=['mrg'])
                for c in range(8):
                    T.op('pe', lambda e: e.transpose(tp[:, c * 128:(c + 1) * 128], mrg[:, c * 128:(c + 1) * 128], identb[:]),
                         r=['mrg', 'identb'], w=['tp'])
                T.op('dve', lambda e: e.tensor_copy(out=mT[:], in_=tp[:]), r=['tp'], w=['mT'])
                for hf in range(2):
                    hs = slice(hf * 512, (hf + 1) * 512)
                    for c in range(8):
                        T.op('pe', lambda e: e.matmul(Op[hf][:], lhsT=mT[:, c * 128:(c + 1) * 128], rhs=Wo[:, c, hs],
                                                      start=(c == 0), stop=(c == 7)), r=['mT'] + kWo, w=[f'Op{hf}'])
                    T.op('dve', lambda e: e.tensor_tensor(out=x1s[q][:, hs], in0=Op[hf][:], in1=xt[:, hs], op=ALU.add),
                         r=[f'Op{hf}', f'xt{s}'], w=[f'x1s{q}'])
                T.dma('sp', x1_d[i * 128:(i + 1) * 128, :], x1s[q][:], r=[f'x1s{q}'], w=[f'x1_{i}', f'ys_{i}', f'yn_{i}'], slot=f'x1s{q}')

        T.barrier()
        with ExitStack() as e7:
            Wup = sbt(e7, 'Wup', [128, 8, 4096], BF16)
            Wdn = sbt(e7, 'Wdn', [128, 32, D], BF16)
            kUp = load_w(Wup, di['w_up'], D, 4096, 'Wup', gt=g2t)
            kDn = load_w(Wdn, di['w_down'], 4096, D, 'Wdn')
            Up = [pst(e7, f'Up{k}', [128, 512]) for k in range(2)]
            Dn = [pst(e7, f'Dn{k}', [128, 512]) for k in range(2)]
            rl = [sbt(e7, f'rl{k}', [128, 512]) for k in range(2)]
            acT = [sbt(e7, f'acT{k}', [128, 512], BF16) for k in range(2)]
            os_ = [sbt(e7, f'os{k}', [128, D]) for k in range(2)]
            for i in range(NT if '5' in PH else 0):
                xt, hT, s = rms_hT(i, x1_d, keysrc=[f'x1_{i}'])
                q = i % 2
                def down(fg):
                    u = fg % 2
                    for q4 in range(4):
                        fc = fg * 4 + q4
                        for hf in range(2):
                            T.op('pe', lambda e: e.matmul(Dn[hf][:], lhsT=acT[u][:, q4 * 128:(q4 + 1) * 128],
                                                          rhs=Wdn[:, fc, hf * 512:(hf + 1) * 512], start=(fc == 0), stop=(fc == 31)),
                                 r=[f'acT{u}'] + kDn, w=[f'Dn{hf}'])

                for fg in range(8):
                    u = fg % 2
                    for q4 in range(4):
                        fc = fg * 4 + q4
                        for c in range(8):
                            T.op('pe', lambda e: e.matmul(Up[u][:, q4 * 128:(q4 + 1) * 128], lhsT=Wup[:, c, fc * 128:(fc + 1) * 128],
                                                          rhs=hT[:, c * 128:(c + 1) * 128], start=(c == 0), stop=(c == 7)),
                                 r=[f'hT{s}'] + kUp, w=[f'Up{u}'])
                    T.op('act', lambda e: e.activation(out=rl[u][:], in_=Up[u][:], func=AF.Relu), r=[f'Up{u}'], w=[f'rl{u}'])
                    T.op('dve', lambda e: e.tensor_tensor(out=acT[u][:], in0=rl[u][:], in1=rl[u][:], op=ALU.mult), r=[f'rl{u}'], w=[f'acT{u}'])
                    if fg > 0:
                        down(fg - 1)
                down(7)
                for hf in range(2):
                    hs = slice(hf * 512, (hf + 1) * 512)
                    T.op('dve', lambda e: e.tensor_tensor(out=os_[q][:, hs], in0=Dn[hf][:], in1=xt[:, hs], op=ALU.add),
                         r=[f'Dn{hf}', f'xt{s}'], w=[f'os{q}'])
                T.dma('sp', out_d[i * 128:(i + 1) * 128, :], os_[q][:], r=[f'os{q}'], w=[f'out_{i}', f'x1_{i}'], slot=f'os{q}')
        T.finish()
        print("instructions:", T.nins)
    return nc, hc


_CACHE = {}


def run(inputs, SEQ, n_cores, dbg=False):
    key = (SEQ, dbg)
    if key not in _CACHE:
        _CACHE[key] = build(SEQ, dbg)
    nc, hc = _CACHE[key]
    L = host_layouts(inputs)
    x = np.asarray(inputs['x'], dtype=np.float32)
    B = x.shape[0]
    in_maps = []
    for core in range(n_cores):
        m = {'x': np.ascontiguousarray(x[core % B])}
        for k, v in hc.items():
            m['c_' + k] = v
        m.update(L)
        in_maps.append(m)
    res = run_bass_kernel_spmd(nc, in_maps, core_ids=list(range(n_cores)))
    return res


def kernel(**inputs):
    x = np.asarray(inputs['x'])
    B, SEQ, _ = x.shape
    res = run(inputs, SEQ, B)
    return np.stack([np.asarray(res.results[b]['out'], dtype=np.float32) for b in range(B)], 0)
```

```python
import os
import numpy as np
import ml_dtypes
from contextlib import ExitStack
import concourse.bass as bass
import concourse.mybir as mybir
from concourse.bass_utils import run_bass_kernel_spmd

F32 = mybir.dt.float32
BF16 = mybir.dt.bfloat16
I32 = mybir.dt.int32
AF = mybir.ActivationFunctionType
ALU = mybir.AluOpType
AX = mybir.AxisListType
SAME_ENGINE_SYNC = True
NEG = -30000.0
EPS = 1e-6
D = 1024
IN_W = 3864
O1, O2, O3, O4 = 512, 1024, 1792, 1816
BF = ml_dtypes.bfloat16


class TS:
    def __init__(self, nc, es):
        self.nc = nc
        self.es = es
        self.E = dict(pe=nc.tensor, act=nc.scalar, dve=nc.vector, pool=nc.gpsimd, sp=nc.sync)
        self.sem = {k: es.enter_context(nc.semaphore("s_" + k)) for k in self.E}
        self.cnt = {k: 0 for k in self.E}
        self.waited = {k: {} for k in self.E}
        self.lastw = {}
        self.readers = {}
        self.nins = 0

    def _need(self, e, reads, writes):
        deps = {}
        for k in reads:
            t = self.lastw.get(k)
            if t is not None and deps.get(t[0], 0) < t[1]:
                deps[t[0]] = t[1]
        for k in writes:
            t = self.lastw.get(k)
            if t is not None and deps.get(t[0], 0) < t[1]:
                deps[t[0]] = t[1]
            for s, v in self.readers.get(k, {}).items():
                if deps.get(s, 0) < v:
                    deps[s] = v
        for s, v in deps.items():
            if s == e and (e == 'pe' or not SAME_ENGINE_SYNC):
                continue
            if self.waited[e].get(s, 0) >= v:
                continue
            self.E[e].wait_ge(self.sem[s], v)
            self.waited[e][s] = v

    def _record(self, tok, reads, writes):
        s, v = tok
        for k in writes:
            self.lastw[k] = tok
            self.readers[k] = {}
        for k in reads:
            d = self.readers.setdefault(k, {})
            if d.get(s, 0) < v:
                d[s] = v

    def op(self, e, fn, r=(), w=()):
        self._need(e, r, w)
        ins = fn(self.E[e])
        self.cnt[e] += 1
        ins.then_inc(self.sem[e], 1)
        self._record((e, self.cnt[e]), r, w)
        self.nins += 1

    def dma(self, q, out, in_, r=(), w=(), slot=None):
        self._need(q, r, w)
        name = 'd_' + slot
        if name not in self.sem:
            self.sem[name] = self.es.enter_context(self.nc.semaphore("s_" + name))
            self.cnt[name] = 0
        if self.cnt[name] > 0 and self.waited[q].get(name, 0) < self.cnt[name]:
            self.E[q].wait_ge(self.sem[name], self.cnt[name])
            self.waited[q][name] = self.cnt[name]
        ins = self.E[q].dma_start(out=out, in_=in_)
        self.cnt[name] += 16
        ins.then_inc(self.sem[name], 16)
        self._record((name, self.cnt[name]), r, w)
        self.nins += 1

    def barrier(self):
        for e in self.E:
            for s, h in self.sem.items():
                if self.cnt[s] > 0 and self.waited[e].get(s, 0) < self.cnt[s]:
                    self.E[e].wait_ge(h, self.cnt[s])
                    self.waited[e][s] = self.cnt[s]

    def finish(self):
        for s, h in self.sem.items():
            if self.cnt[s] > 0 and self.waited['sp'].get(s, 0) < self.cnt[s]:
                self.E['sp'].wait_ge(h, self.cnt[s])


def host_consts(SEQ):
    NT = SEQ // 128
    nblk = SEQ // 16 - 1
    NCT = (nblk + 127) // 128
    n_sel = SEQ // 64
    c = {}
    c['identf'] = np.eye(128, dtype=np.float32)
    i4 = np.tile(np.eye(128, dtype=np.float32), (1, 4))
    c['i4'] = i4.astype(BF)
    qi = np.arange(128)[:, None]
    ki = np.arange(128)[None, :]
    c['negtri'] = np.where(ki > qi, NEG, 0.0).astype(BF)
    c['negtric'] = np.where(ki <= qi, NEG, 0.0).astype(BF)
    keys = np.arange(SEQ)
    c['eall'] = (keys[None, :] // 64 == np.arange(128)[:, None]).astype(np.float32).astype(BF)
    A = np.zeros((128, NCT, 128), np.float32)
    for cc in range(NCT * 128):
        for n in ((cc + 1) // 4, cc // 4):
            if 4 * n - 1 <= cc <= 4 * n + 3 and n < n_sel and n != 127:
                A[cc % 128, cc // 128, n] = 1.0
    A[:, :, 127] = 1.0
    c['aaug'] = A.reshape(128, NCT * 128).astype(BF)
    m = np.arange(-127, 145)[None, :]
    f = ((np.arange(128) + 1) // 16)[:, None]
    c['wc'] = np.where(m > f, NEG, 0.0).astype(BF)
    t = np.arange(SEQ)
    cur = t // 64
    n = np.arange(128)[None, :]
    fb = np.zeros((SEQ, 128), np.float32)
    forced = (n == 0) | (n == cur[:, None]) | (n == cur[:, None] - 1)
    fb[forced] = 1e4
    fb[(n > cur[:, None]) | (n >= n_sel)] = -1e30
    c['fb'] = fb
    kpos = np.stack([64 * (t // 64), t % 64, np.ones_like(t), np.ones_like(t)]).astype(np.float32)
    c['kpos'] = kpos.astype(BF)
    ce = 16 * np.arange(NCT * 128) + 31
    c['cpos'] = np.stack([64 * (ce // 64), ce % 64, np.ones_like(ce), np.ones_like(ce)]).astype(np.float32).astype(BF)
    slopes = 2.0 ** -(np.arange(8) + 1.0)
    qpos = np.zeros((4, 8, SEQ), np.float32)
    qpos[0] = slopes[:, None]
    qpos[1] = slopes[:, None]
    qpos[2] = -slopes[:, None] * (64 * (t // 64))[None, :]
    qpos[3] = -slopes[:, None] * (t % 64)[None, :]
    c['qpos'] = qpos.astype(BF)
    c['tp1'] = np.tile(np.arange(1, 129, dtype=np.float32)[None, :], (128, 1))
    c['ones64'] = np.full((64, 64), 1.0 / 64, np.float32)
    return c


def host_layouts(inp):
    L = {}
    f = lambda a: np.ascontiguousarray(np.asarray(a, dtype=np.float32))
    L['g1'] = f(inp['norm1_g'][0].reshape(8, 128).T)
    L['g2'] = f(inp['norm2_g'][0].reshape(8, 128).T)
    L['w_in'] = f(inp['w_in'][0])
    L['w_glu'] = f(inp['ssm_w_glu'][0])
    L['w_ps'] = f(inp['w_proj_ssm'][0])
    L['w_pn'] = f(inp['w_proj_nsa'][0])
    L['w_out'] = f(inp['w_out'][0])
    L['w_up'] = f(inp['w_up'][0])
    L['w_down'] = f(inp['w_down'][0])
    L['gq'] = f(inp['q_norm_g'][0].reshape(64, 1))
    L['gk'] = f(inp['k_norm_g'][0].T)
    for nm, key in (('w1k', 'cmp_wk1'), ('w1v', 'cmp_wv1')):
        w = np.asarray(inp[key][0]).reshape(32, 64, 128).transpose(1, 0, 2)
        L[nm] = f(np.concatenate([w, w], 0).reshape(128, 32 * 128))
    L['w2k'] = f(inp['cmp_wk2'][0])
    L['w2v'] = f(inp['cmp_wv2'][0])
    L['pek'] = f(np.asarray(inp['cmp_pe_k'][0]).T)
    L['pev'] = f(np.asarray(inp['cmp_pe_v'][0]).T)

    def sm(a):
        a = np.asarray(a)
        return f(a.reshape(16, 2, 64).transpose(1, 2, 0).reshape(128, 16))
    L['are'] = sm(inp['ssm_a_re'][0])
    L['aim'] = sm(inp['ssm_a_im'][0])
    L['ldt'] = sm(np.repeat(np.asarray(inp['ssm_log_dt'][0])[:, None], 64, 1))

    def pad_b(b):
        b = np.asarray(b)
        o = np.zeros((128, 16, 128), np.float32)
        for g in range(32):
            j, gl = g // 2, g % 2
            o[gl * 64:(gl + 1) * 64, j, (j % 4) * 32 + gl * 16:(j % 4) * 32 + gl * 16 + 16] = b[g]
        return o.reshape(128, 2048)
    L['bre'] = pad_b(inp['ssm_b_re'][0])
    L['bim'] = pad_b(inp['ssm_b_im'][0])
    L['cre'] = pad_b(np.asarray(inp['ssm_c_re'][0]).transpose(0, 2, 1))
    L['cim'] = pad_b(np.asarray(inp['ssm_c_im'][0]).transpose(0, 2, 1))
    L['dsk'] = f(inp['ssm_d'][0].reshape(4, 128).T)
    L['bglu'] = f(inp['ssm_b_glu'][0].reshape(4, 128).T)
    return L


def build(SEQ, dbg=False):
    PH = os.environ.get('PHASES', '123S45')
    NT = SEQ // 128
    nblk = SEQ // 16 - 1
    NCT = (nblk + 127) // 128
    nc = bass.Bass("TRN2", target_bir_lowering=False)
    hc = host_consts(SEQ)
    di = {}

    def din(name, shape, dt=F32):
        di[name] = nc.dram_tensor(name, list(shape), dt, kind="ExternalInput").ap()
        return di[name]

    x_d = din('x', [SEQ, D])
    for k, v in hc.items():
        din('c_' + k, v.shape, BF16 if v.dtype == BF else F32)
    shapes = dict(g1=[128, 8], g2=[128, 8], w_in=[D, IN_W], w_glu=[512, 512], w_ps=[512, D], w_pn=[512, D],
                  w_out=[D, D], w_up=[D, 4096], w_down=[4096, D], gq=[64, 1], gk=[64, 3], w1k=[128, 4096],
                  w1v=[128, 4096], w2k=[128, 64], w2v=[128, 64], pek=[64, 32], pev=[64, 32], are=[128, 16],
                  aim=[128, 16], ldt=[128, 16], bre=[128, 2048], bim=[128, 2048], cre=[128, 2048],
                  cim=[128, 2048], dsk=[128, 4], bglu=[128, 4])
    for k, s in shapes.items():
        din(k, s)
    out_d = nc.dram_tensor('out', [SEQ, D], F32, kind="ExternalOutput").ap()
    if dbg:
        yn_t = nc.dram_tensor('yn_scr', [NT, 128, 512], BF16, kind="ExternalOutput").ap()
        ys_t = nc.dram_tensor('ys_scr', [NT, 128, 512], BF16, kind="ExternalOutput").ap()
        x1_d = nc.dram_tensor('x1_scr', [SEQ, D], F32, kind="ExternalOutput").ap()
        yn_d = [yn_t[i] for i in range(NT)]
        ys_d = [ys_t[i] for i in range(NT)]
    else:
        out_bf = out_d.bitcast(BF16)
        yn_d = [out_bf[i * 128:(i + 1) * 128, 0:512] for i in range(NT)]
        ys_d = [out_bf[i * 128:(i + 1) * 128, 512:1024] for i in range(NT)]
        x1_d = out_d

    with ExitStack() as es:
        T = TS(nc, es)

        def sbt(st, n, s, d=F32):
            return st.enter_context(nc.sbuf_tensor('sb_' + n, list(s), d))

        def pst(st, n, s, d=F32):
            return st.enter_context(nc.psum_tensor('ps_' + n, list(s), d))

        identf = sbt(es, 'identf', [128, 128])
        identb = sbt(es, 'identb', [128, 128], BF16)
        i4 = sbt(es, 'i4', [128, 512], BF16)
        negtri = sbt(es, 'negtri', [128, 128], BF16)
        negtric = sbt(es, 'negtric', [128, 128], BF16)
        epsb = sbt(es, 'epsb', [128, 1])
        g1t = sbt(es, 'g1t', [128, 8])
        g2t = sbt(es, 'g2t', [128, 8])
        xts = [sbt(es, f'xt{s}', [128, D]) for s in range(2)]
        hbs = [sbt(es, f'hb{s}', [128, D], BF16) for s in range(2)]
        hTs = [sbt(es, f'hT{s}', [128, D], BF16) for s in range(2)]
        junk = sbt(es, 'junk', [128, D], BF16)
        ssqs = [sbt(es, f'ssq{s}', [128, 1]) for s in range(2)]
        rstds = [sbt(es, f'rstd{s}', [128, 1]) for s in range(2)]
        stg = [sbt(es, f'stg{s}', [128, 1024]) for s in range(2)]
        tp = pst(es, 'tp', [128, 1024], BF16)
        wstate = {'slot': 0}

        T.dma('sp', identf[:], di['c_identf'], w=['identf'], slot='c0')
        T.dma('sp', i4[:], di['c_i4'], w=['i4'], slot='c1')
        T.dma('sp', negtri[:], di['c_negtri'], w=['negtri'], slot='c2')
        T.dma('sp', negtric[:], di['c_negtric'], w=['negtric'], slot='c3')
        T.dma('sp', g1t[:], di['g1'], w=['g1t'], slot='c4')
        T.dma('sp', g2t[:], di['g2'], w=['g2t'], slot='c5')
        T.op('dve', lambda e: e.memset(epsb[:], EPS), w=['epsb'])
        T.op('dve', lambda e: e.tensor_copy(out=identb[:], in_=identf[:]), r=['identf'], w=['identb'])

        def load_cast(dst_ap, src_ap, n, key, scale=None, parts=128):
            s = wstate['slot']
            wstate['slot'] ^= 1
            T.dma('sp', stg[s][0:parts, 0:n], src_ap, w=[f'stg{s}'], slot=f'stg{s}')
            if scale is None:
                T.op('act', lambda e: e.activation(out=dst_ap, in_=stg[s][0:parts, 0:n], func=AF.Copy), r=[f'stg{s}'], w=[key])
            else:
                T.op('act', lambda e: e.activation(out=dst_ap, in_=stg[s][0:parts, 0:n], func=AF.Copy, scale=scale),
                     r=[f'stg{s}', 'g1t', 'g2t'], w=[key])

        def load_w(dst, src, rows, cols, name, col0=0, gt=None):
            keys = []
            for c in range(rows // 128):
                for a in range(0, cols, 1024):
                    n = min(1024, cols - a)
                    k = f'{name}_{c}_{a}'
                    load_cast(dst[:, c, a:a + n], src[c * 128:(c + 1) * 128, col0 + a:col0 + a + n], n, k,
                              scale=None if gt is None else gt[:, c:c + 1])
                    keys.append(k)
            return keys

        def rms_hT(i, src_d, keysrc=()):
            s = i % 2
            xt, hb, hT = xts[s], hbs[s], hTs[s]
            T.dma('sp', xt[:], src_d[i * 128:(i + 1) * 128, :], r=list(keysrc), w=[f'xt{s}'], slot=f'xt{s}')
            T.op('act', lambda e: e.activation(out=junk[:], in_=xt[:], func=AF.Square, accum_out=ssqs[s][:]),
                 r=[f'xt{s}'], w=['junk', f'ssq{s}'])
            T.op('act', lambda e: e.activation(out=rstds[s][:], in_=ssqs[s][:], func=AF.Sqrt, bias=epsb[:], scale=1.0 / D),
                 r=[f'ssq{s}', 'epsb'], w=[f'rstd{s}'])
            T.op('dve', lambda e: e.reciprocal(out=rstds[s][:], in_=rstds[s][:]), r=[f'rstd{s}'], w=[f'rstd{s}'])
            T.op('act', lambda e: e.activation(out=hb[:], in_=xt[:], func=AF.Copy, scale=rstds[s][:]),
                 r=[f'xt{s}', f'rstd{s}'], w=[f'hb{s}'])
            for c in range(8):
                T.op('pe', lambda e: e.transpose(tp[:, c * 128:(c + 1) * 128], hb[:, c * 128:(c + 1) * 128], identb[:]),
                     r=[f'hb{s}', 'identb'], w=['tp'])
            T.op('dve', lambda e: e.tensor_copy(out=hT[:], in_=tp[:]), r=['tp'], w=[f'hT{s}'])
            return xt, hT, s

        with ExitStack() as ea:
            KselT = [sbt(ea, f'KselT{g}', [68, SEQ], BF16) for g in range(2)]
            KwinT = [sbt(ea, f'KwinT{g}', [68, SEQ], BF16) for g in range(2)]
            Vsel = sbt(ea, 'Vsel', [128, NT * 130], BF16)
            Vwin = sbt(ea, 'Vwin', [128, NT * 130], BF16)
            KcT = [sbt(ea, f'KcT{g}', [68, NCT * 128], BF16) for g in range(2)]
            Vc = sbt(ea, 'Vc', [128, NCT * 130], BF16)
            gqt = sbt(ea, 'gqt', [64, 1])
            gq8 = sbt(ea, 'gq8', [64, 1])
            gkt = sbt(ea, 'gkt', [64, 3])
            T.dma('sp', gqt[:], di['gq'], w=['gqt'], slot='c0')
            T.dma('sp', gkt[:], di['gk'], w=['gkt'], slot='c1')
            T.op('dve', lambda e: e.tensor_scalar(out=gq8[:], in0=gqt[:], scalar1=0.125, scalar2=None, op0=ALU.mult), r=['gqt'], w=['gq8'])
            for g in range(2):
                T.dma('sp', KselT[g][64:68, :], di['c_kpos'], w=[f'KselTp{g}'], slot='c2')
                T.dma('sp', KwinT[g][64:68, :], di['c_kpos'], w=[f'KwinTp{g}'], slot='c3')
                T.dma('sp', KcT[g][64:68, :], di['c_cpos'], w=[f'KcTp{g}'], slot='c4')
            T.op('pool', lambda e: e.memset(Vsel[:], 1.0), w=['Vsel_init'])
            T.op('pool', lambda e: e.memset(Vwin[:], 1.0), w=['Vwin_init'])
            T.op('pool', lambda e: e.memset(Vc[:], 1.0), w=['Vc_init'])

            with ExitStack() as e1:
                Wkv = sbt(e1, 'Wkv', [128, 8, 768], BF16)
                rawk = sbt(e1, 'rawk', [128, SEQ], BF16)
                rawv = sbt(e1, 'rawv', [128, SEQ], BF16)
                kvp = pst(e1, 'kvp', [128, 512])
                cmp_ = pst(e1, 'cmp_', [128, 512])
                tp2 = pst(e1, 'tp2', [128, 1024], BF16)
                sq = sbt(e1, 'sq', [128, 256])
                ssk = sbt(e1, 'ssk', [128, 4])
                rsk = sbt(e1, 'rsk', [128, 4])
                kn = sbt(e1, 'kn', [128, 256], BF16)
                kW = load_w(Wkv, di['w_in'], D, 768, 'Wkv', col0=O2, gt=g1t)
                for i in range(NT if '1' in PH else 0):
                    xt, hT, s = rms_hT(i, x_d)
                    cs = slice(i * 128, (i + 1) * 128)
                    for c in range(8):
                        T.op('pe', lambda e: e.matmul(kvp[:], lhsT=hT[:, c * 128:(c + 1) * 128], rhs=Wkv[:, c, 256:768],
                                                      start=(c == 0), stop=(c == 7)), r=[f'hT{s}'] + kW, w=['kvp'])
                    for half in range(2):
                        for c in range(8):
                            T.op('pe', lambda e: e.matmul(cmp_[:, half * 128:(half + 1) * 128],
                                                          lhsT=Wkv[:, c, half * 128:(half + 1) * 128],
                                                          rhs=hT[:, c * 128:(c + 1) * 128], start=(c == 0), stop=(c == 7)),
                                 r=[f'hT{s}'] + kW, w=['cmp_'])
                    T.op('act', lambda e: e.activation(out=sq[:, 0:128], in_=kvp[:, 0:128], func=AF.Square), r=['kvp'], w=['sq'])
                    T.op('act', lambda e: e.activation(out=sq[:, 128:256], in_=kvp[:, 256:384], func=AF.Square), r=['kvp'], w=['sq'])
                    T.op('dve', lambda e: e.tensor_reduce(out=ssk[:], in_=sq[:].rearrange("p (h d) -> p h d", h=4), axis=AX.X, op=ALU.add),
                         r=['sq'], w=['ssk'])
                    T.op('act', lambda e: e.activation(out=rsk[:], in_=ssk[:], func=AF.Sqrt, bias=epsb[:], scale=1.0 / 64),
                         r=['ssk', 'epsb'], w=['rsk'])
                    T.op('dve', lambda e: e.reciprocal(out=rsk[:], in_=rsk[:]), r=['rsk'], w=['rsk'])
                    for b in range(2):
                        T.op('dve', lambda e: e.tensor_tensor(
                            out=kn[:, b * 128:(b + 1) * 128].rearrange("p (h d) -> p h d", h=2),
                            in0=kvp[:, b * 256:b * 256 + 128].rearrange("p (h d) -> p h d", h=2),
                            in1=rsk[:, 2 * b:2 * b + 2].unsqueeze(2).broadcast_to([128, 2, 64]), op=ALU.mult),
                            r=['kvp', 'rsk'], w=['kn'])
                    for j in range(4):
                        T.op('pe', lambda e: e.transpose(tp2[0:64, j * 128:(j + 1) * 128], kn[:, j * 64:(j + 1) * 64], identb[:]),
                             r=['kn', 'identb'], w=['tp2'])
                    for j in range(4):
                        br, g = (1, j) if j < 2 else (2, j - 2)
                        dst = (KselT if br == 1 else KwinT)[g]
                        T.op('act', lambda e: e.activation(out=dst[0:64, cs], in_=tp2[0:64, j * 128:(j + 1) * 128], func=AF.Copy,
                                                           scale=gkt[:, br:br + 1]), r=['tp2', 'gkt'],
                             w=[f'K{br}_{g}_{i}'])
                    for b, Vt, nm in ((0, Vsel, 'Vs'), (1, Vwin, 'Vw')):
                        T.op('dve', lambda e: e.tensor_copy(
                            out=Vt[:, i * 130:(i + 1) * 130].rearrange("p (g d) -> p g d", g=2)[:, :, 0:64],
                            in_=kvp[:, b * 256 + 128:b * 256 + 256].rearrange("p (g d) -> p g d", g=2)),
                            r=['kvp', 'Vsel_init', 'Vwin_init'], w=[f'{nm}_{i}'])
                    T.op('act', lambda e: e.activation(out=rawk[:, cs], in_=cmp_[:, 0:128], func=AF.Copy), r=['cmp_'], w=['rawk'])
                    T.op('act', lambda e: e.activation(out=rawv[:, cs], in_=cmp_[:, 128:256], func=AF.Copy), r=['cmp_'], w=['rawv'])

                W1 = {'k': sbt(e1, 'W1k', [128, 4096], BF16), 'v': sbt(e1, 'W1v', [128, 4096], BF16)}
                W2 = {'k': sbt(e1, 'W2k', [128, 64], BF16), 'v': sbt(e1, 'W2v', [128, 64], BF16)}
                pe_ = {'k': sbt(e1, 'pek', [64, 32], BF16), 'v': sbt(e1, 'pev', [64, 32], BF16)}
                ones64 = sbt(e1, 'ones64', [64, 64])
                T.dma('sp', ones64[:], di['c_ones64'], w=['ones64'], slot='c0')
                for kd in 'kv':
                    for a in range(4):
                        load_cast(W1[kd][:, a * 1024:(a + 1) * 1024], di['w1' + kd][:, a * 1024:(a + 1) * 1024], 1024, 'W1' + kd)
                    load_cast(W2[kd][:], di['w2' + kd], 64, 'W2' + kd)
                    load_cast(pe_[kd][:], di['pe' + kd], 32, 'pe' + kd, parts=64)
                b1p = pst(e1, 'b1p', [128, 512])
                b1s = sbt(e1, 'b1s', [128, 1])
                hid = sbt(e1, 'hid', [128, NCT * 128], BF16)
                sqc = sbt(e1, 'sqc', [64, 512])
                rsc = sbt(e1, 'rsc', [64, 512])
                T.op('pool', lambda e: e.memset(hid[:], 0.0), w=['hid'])
                for g in range(2):
                    T.op('pool', lambda e: e.memset(KcT[g][0:64, :], 0.0), w=[f'KcT{g}'])
                for kd in ('kv' if '2' in PH else ''):
                    raw = rawk if kd == 'k' else rawv
                    for r_ in range(32):
                        T.op('pe', lambda e: e.matmul(b1p[:, 0:1], lhsT=W1[kd][0:64, r_ * 128:(r_ + 1) * 128],
                                                      rhs=pe_[kd][0:64, r_:r_ + 1], start=(r_ == 0), stop=(r_ == 31)),
                             r=['W1' + kd, 'pe' + kd], w=['b1p'])
                    T.op('dve', lambda e: e.tensor_copy(out=b1s[:], in_=b1p[:, 0:1]), r=['b1p'], w=['b1s'])
                    for g in range(2):
                        ps_ = slice(g * 64, (g + 1) * 64)
                        for r_ in range(32):
                            T.op('pe', lambda e: e.matmul(kvp[:, 0:nblk], lhsT=W1[kd][ps_, r_ * 128:(r_ + 1) * 128],
                                                          rhs=raw[ps_, r_:r_ + 16 * (nblk - 1) + 1:16],
                                                          start=(r_ == 0), stop=(r_ == 31)),
                                 r=['W1' + kd, 'rawk', 'rawv'], w=['kvp'])
                        T.op('act', lambda e: e.activation(out=hid[:, 0:nblk], in_=kvp[:, 0:nblk], func=AF.Gelu_apprx_tanh, bias=b1s[:]),
                             r=['kvp', 'b1s'], w=['hid'])
                        if kd == 'k':
                            T.op('pe', lambda e: e.matmul(cmp_[0:64, 0:nblk], lhsT=W2['k'][:, :], rhs=hid[:, 0:nblk], start=True, stop=True),
                                 r=['W2k', 'hid'], w=['cmp_'])
                            T.op('act', lambda e: e.activation(out=sqc[:, 0:nblk], in_=cmp_[0:64, 0:nblk], func=AF.Square), r=['cmp_'], w=['sqc'])
                            T.op('pe', lambda e: e.matmul(b1p[0:64, 0:nblk], lhsT=ones64[:, :], rhs=sqc[:, 0:nblk], start=True, stop=True),
                                 r=['ones64', 'sqc'], w=['b1p'])
                            T.op('act', lambda e: e.activation(out=rsc[:, 0:nblk], in_=b1p[0:64, 0:nblk], func=AF.Sqrt, bias=epsb[0:64, :]),
                                 r=['b1p', 'epsb'], w=['rsc'])
                            T.op('dve', lambda e: e.reciprocal(out=rsc[:, 0:nblk], in_=rsc[:, 0:nblk]), r=['rsc'], w=['rsc'])
                            T.op('dve', lambda e: e.scalar_tensor_tensor(out=KcT[g][0:64, 0:nblk], in0=cmp_[0:64, 0:nblk], scalar=gkt[:, 0:1],
                                                                         in1=rsc[:, 0:nblk], op0=ALU.mult, op1=ALU.mult),
                                 r=['cmp_', 'gkt', 'rsc'], w=[f'KcT{g}'])
                        else:
                            for ct in range(NCT):
                                T.op('pe', lambda e: e.matmul(cmp_[:, 0:64], lhsT=hid[:, ct * 128:(ct + 1) * 128], rhs=W2['v'][:, :],
                                                              start=True, stop=True), r=['W2v', 'hid'], w=['cmp_'])
                                T.op('dve', lambda e: e.tensor_copy(out=Vc[:, (ct * 2 + g) * 65:(ct * 2 + g) * 65 + 64], in_=cmp_[:, 0:64]),
                                     r=['cmp_', 'Vc_init'], w=[f'Vc{g}'])

            if dbg and not os.environ.get('NODUMP'):
                for g in range(2):
                    dk = nc.dram_tensor(f'dbg_kc{g}', [68, NCT * 128], BF16, kind="ExternalOutput").ap()
                    T.dma('sp', dk, KcT[g][:], r=[f'KcT{g}', f'KcTp{g}'], slot=f'dbgk{g}')
                dv = nc.dram_tensor('dbg_vc', [128, NCT * 130], BF16, kind="ExternalOutput").ap()
                T.dma('sp', dv, Vc[:], r=['Vc0', 'Vc1', 'Vc_init'], slot='dbgv')
            T.barrier()
            with ExitStack() as e3:
                Wq = sbt(e3, 'Wq', [128, 8, 536], BF16)
                eall = sbt(e3, 'eall', [128, SEQ], BF16)
                aaug = sbt(e3, 'aaug', [128, NCT * 128], BF16)
                wc = sbt(e3, 'wc', [128, 272], BF16)
                T.dma('sp', eall[:], di['c_eall'], w=['eall'], slot='c0')
                T.dma('sp', aaug[:], di['c_aaug'], w=['aaug'], slot='c1')
                T.dma('sp', wc[:], di['c_wc'], w=['wc'], slot='c2')
                kQ = load_w(Wq, di['w_in'], D, 512, 'Wq', col0=O1, gt=g1t)
                kQ += load_w(Wq[:, :, 512:536], di['w_in'], D, 24, 'Wg', col0=O3, gt=g1t)
                bankQ = pst(e3, 'bankQ', [128, 512])
                bankG = pst(e3, 'bankG', [128, 512])
                Sb = [pst(e3, f'S{k}', [128, 512]) for k in range(3)]
                Ob = [pst(e3, f'O{k}', [128, 512]) for k in range(2)]
                IMP = bankQ
                Pb = [sbt(e3, f'P{k}', [128, 512], BF16) for k in range(4)]
                QTs = [sbt(e3, f'QT{k}', [68, 1024], BF16) for k in range(2)]
                Pc = [sbt(e3, f'Pc{k}', [128, 512], BF16) for k in range(NCT)]
                sqq = sbt(e3, 'sqq', [128, 512])
                ss8 = sbt(e3, 'ss8', [128, 8])
                rq = sbt(e3, 'rq', [128, 8])
                qn = sbt(e3, 'qn', [128, 512], BF16)
                gsig = sbt(e3, 'gsig', [128, 24])
                fbs = [sbt(e3, f'fb{k}', [128, 128]) for k in range(2)]
                den4 = sbt(e3, 'den4', [128, 4])
                impa = sbt(e3, 'impa', [128, 128])
                m1 = sbt(e3, 'm1', [128, 8])
                m2 = sbt(e3, 'm2', [128, 8])
                tmpm = sbt(e3, 'tmpm', [128, 128])
                negms = [sbt(e3, f'negm{k}', [128, 128], BF16) for k in range(2)]
                nmT4 = [sbt(e3, f'nmT4{k}', [128, 512], BF16) for k in range(2)]
                osbs = [sbt(e3, f'osb{k}', [65, 512]) for k in range(4)]
                dd = sbt(e3, 'dd', [128, 4])
                rr = sbt(e3, 'rr', [128, 4])
                ynsa = sbt(e3, 'ynsa', [128, 512])
                ynb = sbt(e3, 'ynb', [128, 512], BF16)
                ynT = [sbt(e3, f'ynT{k}', [128, 512], BF16) for k in range(2)]
                cnt = {'s': 0, 'o': 0, 'c': 0}

                pend = []

                deferred = []

                def defer(n, fn):
                    deferred.append([n, fn])

                def tick():
                    for d in deferred:
                        d[0] -= 1
                    while deferred and deferred[0][0] <= 0:
                        deferred.pop(0)[1]()

                def emit_S(u):
                    k = cnt['s'] % 3
                    kp = cnt['s'] % 4
                    cnt['s'] += 1
                    u['kp'] = kp
                    S, P = Sb[k], Pb[kp]
                    masks = u['masks']
                    T.op('pe', lambda e: e.matmul(S[:], lhsT=u['KT'][0:68, u['lo']:u['lo'] + 128],
                                                  rhs=u['QT'][0:68, u['g'] * 512:(u['g'] + 1) * 512],
                                                  start=True, stop=(len(masks) == 0)), r=u['kkeys'] + [u['qk']], w=[f'S{k}'])
                    for mi, (ml, mr, mk) in enumerate(masks):
                        T.op('pe', lambda e: e.matmul(S[:], lhsT=ml, rhs=mr, start=False, stop=(mi == len(masks) - 1)),
                             r=mk, w=[f'S{k}'])
                    T.op('act', lambda e: e.activation(out=P[:], in_=S[:], func=AF.Exp), r=[f'S{k}'], w=[f'P{kp}'])

                def emit_PV(u):
                    kp = u['kp']
                    P = Pb[kp]
                    O = Ob[u['O']]
                    T.op('pe', lambda e: e.matmul(O[0:65, :], lhsT=u['Vt'][:, u['voff']:u['voff'] + 65], rhs=P[:],
                                                  start=u['first'], stop=u['last']), r=[f'P{kp}'] + u['vkeys'], w=[f"O{u['O']}"])
                    if u.get('imp_ct') is not None:
                        ic = u['imp_ct']
                        T.op('pool', lambda e: e.tensor_copy(out=Pc[ic][:], in_=P[:]), r=[f'P{kp}'], w=[f'Pc{ic}'])
                    if u.get('post') is not None:
                        u['post']()

                def push(u):
                    emit_S(u)
                    if len(pend) >= 2:
                        emit_PV(pend.pop(0))
                    pend.append(u)
                    tick()

                def flush():
                    while pend:
                        emit_PV(pend.pop(0))
                    while deferred:
                        deferred.pop(0)[1]()

                DBG_BR = int(os.environ.get('DBG_BR', '-1'))

                def combine(oi, g, br):
                    if DBG_BR >= 0 and br != DBG_BR:
                        return
                    ko = cnt['c'] % 4
                    cnt['c'] += 1
                    osb = osbs[ko]
                    Oacc = Ob[oi]
                    T.op('act', lambda e: e.activation(out=osb[:], in_=Oacc[0:65, :], func=AF.Copy), r=[f'O{oi}'], w=[f'osb{ko}'])
                    defer(2, lambda: combine2(ko, g, br))

                def combine2(ko, g, br):
                    osb = osbs[ko]
                    for hg in range(4):
                        T.op('pe', lambda e: e.transpose(bankG[:, hg * 65:(hg + 1) * 65], osb[0:65, hg * 128:(hg + 1) * 128], identf[0:65, 0:65]),
                             r=[f'osb{ko}', 'identf'], w=['bankG'])
                    ot = bankG[:, 0:260].rearrange("p (h d) -> p h d", h=4)
                    T.op('dve', lambda e: e.tensor_scalar(out=dd[:].unsqueeze(2), in0=ot[:, :, 64:65], scalar1=1e-30, scalar2=None, op0=ALU.max),
                         r=['bankG'], w=['dd'])
                    T.op('dve', lambda e: e.reciprocal(out=dd[:], in_=dd[:]), r=['dd'], w=['dd'])
                    if DBG_BR >= 0:
                        T.op('dve', lambda e: e.tensor_copy(out=rr[:], in_=dd[:]), r=['dd'], w=['rr'])
                    else:
                        T.op('dve', lambda e: e.tensor_tensor(out=rr[:], in0=dd[:], in1=gsig[:, br * 8 + g * 4:br * 8 + g * 4 + 4], op=ALU.mult),
                             r=['dd', 'gsig'], w=['rr'])
                    for hg in range(4):
                        hh = g * 4 + hg
                        ysl = ynsa[:, hh * 64:(hh + 1) * 64]
                        if br == 0 or DBG_BR >= 0:
                            T.op('dve', lambda e: e.tensor_scalar(out=ysl, in0=bankG[:, hg * 65:hg * 65 + 64], scalar1=rr[:, hg:hg + 1],
                                                                  scalar2=None, op0=ALU.mult), r=['bankG', 'rr'], w=['ynsa'])
                        else:
                            T.op('dve', lambda e: e.scalar_tensor_tensor(out=ysl, in0=bankG[:, hg * 65:hg * 65 + 64], scalar=rr[:, hg:hg + 1],
                                                                         in1=ysl, op0=ALU.mult, op1=ALU.add), r=['bankG', 'rr', 'ynsa'], w=['ynsa'])

                for i in range(NT if '3' in PH else 0):
                    xt, hT, s = rms_hT(i, x_d)
                    QT = QTs[i % 2]
                    qk = f'QT{i % 2}'
                    fb = fbs[i % 2]
                    for c in range(8):
                        T.op('pe', lambda e: e.matmul(bankQ[:], lhsT=hT[:, c * 128:(c + 1) * 128], rhs=Wq[:, c, 0:512],
                                                      start=(c == 0), stop=(c == 7)), r=[f'hT{s}'] + kQ, w=['bankQ'])
                    for c in range(8):
                        T.op('pe', lambda e: e.matmul(bankG[:, 0:24], lhsT=hT[:, c * 128:(c + 1) * 128], rhs=Wq[:, c, 512:536],
                                                      start=(c == 0), stop=(c == 7)), r=[f'hT{s}'] + kQ, w=['bankG'])
                    T.op('act', lambda e: e.activation(out=sqq[:], in_=bankQ[:], func=AF.Square), r=['bankQ'], w=['sqq'])
                    T.op('dve', lambda e: e.tensor_reduce(out=ss8[:], in_=sqq[:].rearrange("p (h d) -> p h d", h=8), axis=AX.X, op=ALU.add),
                         r=['sqq'], w=['ss8'])
                    T.op('act', lambda e: e.activation(out=rq[:], in_=ss8[:], func=AF.Sqrt, bias=epsb[:], scale=1.0 / 64), r=['ss8', 'epsb'], w=['rq'])
                    T.op('dve', lambda e: e.reciprocal(out=rq[:], in_=rq[:]), r=['rq'], w=['rq'])
                    T.op('dve', lambda e: e.tensor_tensor(out=qn[:].rearrange("p (h d) -> p h d", h=8),
                                                          in0=bankQ[:].rearrange("p (h d) -> p h d", h=8),
                                                          in1=rq[:].unsqueeze(2).broadcast_to([128, 8, 64]), op=ALU.mult),
                         r=['bankQ', 'rq'], w=['qn'])
                    for h in range(8):
                        T.op('pe', lambda e: e.transpose(tp[0:64, h * 128:(h + 1) * 128], qn[:, h * 64:(h + 1) * 64], identb[:]),
                             r=['qn', 'identb'], w=['tp'])
                    T.op('act', lambda e: e.activation(out=QT[0:64, :], in_=tp[0:64, :], func=AF.Copy, scale=gq8[:]), r=['tp', 'gq8'], w=[qk])
                    T.dma('sp', QT[64:68, :].rearrange("p (h q) -> p h q", h=8), di['c_qpos'][:, :, i * 128:(i + 1) * 128],
                          w=[qk + 'p'], slot=qk + 'p')
                    qk2 = [qk, qk + 'p']
                    T.op('act', lambda e: e.activation(out=gsig[:], in_=bankG[:, 0:24], func=AF.Sigmoid), r=['bankG'], w=['gsig'])
                    T.dma('sp', fb[:], di['c_fb'][i * 128:(i + 1) * 128, :], w=[f'fb{i % 2}'], slot=f'fb{i % 2}')
                    if dbg and not os.environ.get('NODUMP'):
                        if i == 0:
                            dgs = nc.dram_tensor('dbg_gsig', [NT * 128, 24], F32, kind="ExternalOutput").ap()
                        T.dma('sp', dgs[i * 128:(i + 1) * 128, :], gsig[:], r=['gsig'], slot='dbggs')
                    n_ct = min(NCT, (8 * i + 6) // 128 + 1)

                    def imp_a(g, n_ct=n_ct, fb=fb, i=i):
                        for h in range(4):
                            for ct in range(n_ct):
                                T.op('pe', lambda e: e.matmul(IMP[:, h * 128:(h + 1) * 128], lhsT=Pc[ct][:, h * 128:(h + 1) * 128],
                                                              rhs=aaug[:, ct * 128:(ct + 1) * 128], start=(ct == 0), stop=(ct == n_ct - 1)),
                                     r=[f'Pc{ct}', 'aaug'], w=['bankQ'])
                        iv = IMP[:].rearrange("p (h n) -> p h n", h=4)
                        T.op('dve', lambda e: e.tensor_scalar(out=den4[:].unsqueeze(2), in0=iv[:, :, 127:128], scalar1=1e-30, scalar2=None, op0=ALU.max),
                             r=['bankQ'], w=['den4'])
                        T.op('dve', lambda e: e.reciprocal(out=den4[:], in_=den4[:]), r=['den4'], w=['den4'])
                        T.op('dve', lambda e: e.tensor_scalar(out=impa[:], in0=IMP[:, 0:128], scalar1=den4[:, 0:1], scalar2=None, op0=ALU.mult),
                             r=['bankQ', 'den4'], w=['impa'])
                        for h in range(1, 4):
                            T.op('dve', lambda e: e.scalar_tensor_tensor(out=impa[:], in0=IMP[:, h * 128:(h + 1) * 128], scalar=den4[:, h:h + 1],
                                                                         in1=impa[:], op0=ALU.mult, op1=ALU.add), r=['bankQ', 'den4', 'impa'], w=['impa'])
                        T.op('dve', lambda e: e.tensor_tensor(out=impa[:], in0=impa[:], in1=fb[:], op=ALU.add), r=['impa', f'fb{i % 2}'], w=['impa'])
                        T.op('dve', lambda e: e.max(out=m1[:], in_=impa[:]), r=['impa'], w=['m1'])
                        T.op('dve', lambda e: e.match_replace(out=tmpm[:], in_to_replace=m1[:], in_values=impa[:], imm_value=-3e38),
                             r=['impa', 'm1'], w=['tmpm'])
                        T.op('dve', lambda e: e.max(out=m2[:], in_=tmpm[:]), r=['tmpm'], w=['m2'])
                        T.op('dve', lambda e: e.tensor_scalar(out=negms[g][:], in0=impa[:], scalar1=m2[:, 7:8], scalar2=NEG, op0=ALU.is_lt, op1=ALU.mult),
                             r=['impa', 'm2'], w=[f'negm{g}'])
                        defer(3, lambda: imp_b(g))

                    def imp_b(g):
                        T.op('pe', lambda e: e.transpose(tp[:, 0:128], negms[g][:], identb[:]), r=[f'negm{g}', 'identb'], w=['tp'])
                        T.op('dve', lambda e: e.tensor_copy(out=nmT4[g][:].rearrange("p (h q) -> p h q", h=4),
                                                            in_=tp[:, 0:128].unsqueeze(1).broadcast_to([128, 4, 128])), r=['tp'], w=[f'nmT4{g}'])

                    def post_cmp(oi, g):
                        combine(oi, g, 0)
                        defer(2, lambda: imp_a(g))

                    obank = {}
                    for g in range(2):
                        oi = cnt['o'] % 2
                        cnt['o'] += 1
                        for ct in range(n_ct):
                            off = 128 * ct - 8 * i + 2
                            masks = []
                            if off > -127:
                                masks.append((wc[:, 127 + off:127 + off + 128], i4[:], ['wc', 'i4']))
                            push(dict(KT=KcT[g], QT=QT, g=g, qk=qk, lo=ct * 128, masks=masks, Vt=Vc, voff=(ct * 2 + g) * 65, O=oi,
                                      first=(ct == 0), last=(ct == n_ct - 1), kkeys=[f'KcT{g}', f'KcTp{g}', qk + 'p'],
                                      vkeys=[f'Vc{g}', 'Vc_init'], imp_ct=ct,
                                      post=(lambda oi=oi, g=g: post_cmp(oi, g)) if ct == n_ct - 1 else None))
                        if g == 0:
                            flush()
                    for g in range(2):
                        oi = cnt['o'] % 2
                        cnt['o'] += 1
                        k0 = max(0, i - 4)
                        for kt in range(k0, i + 1):
                            masks = []
                            if kt == i:
                                masks.append((negtri[:], i4[:], ['negtri', 'i4']))
                            if kt == i - 4:
                                masks.append((negtric[:], i4[:], ['negtric', 'i4']))
                            push(dict(KT=KwinT[g], QT=QT, g=g, qk=qk, lo=kt * 128, masks=masks, Vt=Vwin, voff=kt * 130 + g * 65, O=oi,
                                      first=(kt == k0), last=(kt == i), kkeys=[f'K2_{g}_{kt}', f'KwinTp{g}', qk + 'p'], vkeys=[f'Vw_{kt}'],
                                      post=(lambda oi=oi, g=g: combine(oi, g, 2)) if kt == i else None))
                    flush()
                    for g in range(2):
                        oi = cnt['o'] % 2
                        cnt['o'] += 1
                        for kt in range(i + 1):
                            masks = [(eall[:, kt * 128:(kt + 1) * 128], nmT4[g][:], ['eall', f'nmT4{g}'])]
                            if kt == i:
                                masks.append((negtri[:], i4[:], ['negtri', 'i4']))
                            push(dict(KT=KselT[g], QT=QT, g=g, qk=qk, lo=kt * 128, masks=masks, Vt=Vsel, voff=kt * 130 + g * 65, O=oi,
                                      first=(kt == 0), last=(kt == i), kkeys=[f'K1_{g}_{kt}', f'KselTp{g}', qk + 'p'], vkeys=[f'Vs_{kt}'],
                                      post=(lambda oi=oi, g=g: combine(oi, g, 1)) if kt == i else None))
                    flush()
                    yT = ynT[i % 2]
                    T.op('act', lambda e: e.activation(out=ynb[:], in_=ynsa[:], func=AF.Copy), r=['ynsa'], w=['ynb'])
                    for c in range(4):
                        T.op('pe', lambda e: e.transpose(tp[:, c * 128:(c + 1) * 128], ynb[:, c * 128:(c + 1) * 128], identb[:]),
                             r=['ynb', 'identb'], w=['tp'])
                    T.op('dve', lambda e: e.tensor_copy(out=yT[:], in_=tp[:, 0:512]), r=['tp'], w=[f'ynT{i % 2}'])
                    T.dma('sp', yn_d[i], yT[:], r=[f'ynT{i % 2}'], w=[f'yn_{i}'], slot=f'ynT{i % 2}')

        T.barrier()
        with ExitStack() as e4:
            ctab = sbt(e4, 'ctab', [128, 2048])
            stab = sbt(e4, 'stab', [128, 2048])
            BbT = [sbt(e4, f'BbT{k}', [128, 2048], BF16) for k in range(2)]
            Cb = [sbt(e4, f'Cb{k}', [128, 2048], BF16) for k in range(2)]
            mag = sbt(e4, 'mag', [128, 16])
            rotc = sbt(e4, 'rotc', [128, 16])
            rots = sbt(e4, 'rots', [128, 16])
            dsk = sbt(e4, 'dsk', [128, 4])
            bglu = sbt(e4, 'bglu', [128, 4])
            T.dma('sp', dsk[:], di['dsk'], w=['dsk'], slot='c0')
            T.dma('sp', bglu[:], di['bglu'], w=['bglu'], slot='c1')
            ctb = sbt(e4, 'ctb', [128, 2048], BF16)
            stb = sbt(e4, 'stb', [128, 2048], BF16)
            Wu = sbt(e4, 'Wu', [128, 8, 512], BF16)
            Wgl = sbt(e4, 'Wgl', [128, 4, 512], BF16)
            kU = load_w(Wu, di['w_in'], D, 512, 'Wu', col0=0, gt=g1t)
            kGl = load_w(Wgl, di['w_glu'], 512, 512, 'Wgl')

            def sincos(st, phi, n, s_out, c_out, nm):
                u = sbt(st, nm + 'u', [128, n])
                ki = sbt(st, nm + 'ki', [128, n], I32)
                kf = sbt(st, nm + 'kf', [128, n])
                fx = sbt(st, nm + 'fx', [128, n])
                r_ = sbt(st, nm + 'r', [128, n])
                for shift, dst in ((0.0, s_out), (np.pi / 2, c_out)):
                    T.op('dve', lambda e: e.tensor_scalar(out=u[:], in0=phi, scalar1=shift, scalar2=1.0 / (2 * np.pi), op0=ALU.add, op1=ALU.mult),
                         r=[nm + 'phi'], w=[nm + 'u'])
                    T.op('dve', lambda e: e.tensor_copy(out=ki[:], in_=u[:]), r=[nm + 'u'], w=[nm + 'ki'])
                    T.op('dve', lambda e: e.tensor_copy(out=kf[:], in_=ki[:]), r=[nm + 'ki'], w=[nm + 'kf'])
                    T.op('dve', lambda e: e.tensor_scalar(out=r_[:], in0=phi, scalar1=shift, scalar2=None, op0=ALU.add), r=[nm + 'phi'], w=[nm + 'r'])
                    T.op('dve', lambda e: e.scalar_tensor_tensor(out=r_[:], in0=kf[:], scalar=-2 * np.pi, in1=r_[:], op0=ALU.mult, op1=ALU.add),
                         r=[nm + 'kf', nm + 'r'], w=[nm + 'r'])
                    T.op('dve', lambda e: e.tensor_scalar(out=fx[:], in0=r_[:], scalar1=np.pi, scalar2=-2 * np.pi, op0=ALU.is_gt, op1=ALU.mult),
                         r=[nm + 'r'], w=[nm + 'fx'])
                    T.op('dve', lambda e: e.tensor_tensor(out=r_[:], in0=r_[:], in1=fx[:], op=ALU.add), r=[nm + 'r', nm + 'fx'], w=[nm + 'r'])
                    T.op('dve', lambda e: e.tensor_scalar(out=fx[:], in0=r_[:], scalar1=-np.pi, scalar2=2 * np.pi, op0=ALU.is_lt, op1=ALU.mult),
                         r=[nm + 'r'], w=[nm + 'fx'])
                    T.op('dve', lambda e: e.tensor_tensor(out=r_[:], in0=r_[:], in1=fx[:], op=ALU.add), r=[nm + 'r', nm + 'fx'], w=[nm + 'r'])
                    T.op('dve', lambda e: e.tensor_scalar(out=r_[:], in0=r_[:], scalar1=3.14159, scalar2=-3.14159, op0=ALU.min, op1=ALU.max),
                         r=[nm + 'r'], w=[nm + 'r'])
                    T.op('act', lambda e: e.activation(out=dst, in_=r_[:], func=AF.Sin), r=[nm + 'r'], w=[nm + 'out'])

            with ExitStack() as e0:
                sm = lambda n, s=[128, 16]: sbt(e0, n, s)
                are, aim, ldt = sm('are'), sm('aim'), sm('ldt')
                T.dma('sp', are[:], di['are'], w=['are'], slot='c2')
                T.dma('sp', aim[:], di['aim'], w=['aim'], slot='c3')
                T.dma('sp', ldt[:], di['ldt'], w=['ldt'], slot='c4')
                dt_, adr, adi, cs_, sn_ = sm('dt_'), sm('adr'), sm('adi'), sm('cs_'), sm('sn_')
                abr, abi, nr, den, fr, fi, t0, t1 = sm('abr'), sm('abi'), sm('nr'), sm('den'), sm('fr'), sm('fi'), sm('t0'), sm('t1')
                a128 = sm('a128')
                V = lambda fn, r, w: T.op('dve', fn, r=r, w=w)
                T.op('act', lambda e: e.activation(out=dt_[:], in_=ldt[:], func=AF.Exp), r=['ldt'], w=['dt_'])
                V(lambda e: e.tensor_tensor(out=adr[:], in0=are[:], in1=dt_[:], op=ALU.mult), ['are', 'dt_'], ['adr'])
                V(lambda e: e.tensor_tensor(out=adi[:], in0=aim[:], in1=dt_[:], op=ALU.mult), ['aim', 'dt_'], ['s0phi'])
                T.op('act', lambda e: e.activation(out=mag[:], in_=adr[:], func=AF.Exp), r=['adr'], w=['mag'])
                sincos(e0, adi[:], 16, sn_[:], cs_[:], 's0')
                V(lambda e: e.tensor_tensor(out=abr[:], in0=mag[:], in1=cs_[:], op=ALU.mult), ['mag', 's0out'], ['abr'])
                V(lambda e: e.tensor_tensor(out=abi[:], in0=mag[:], in1=sn_[:], op=ALU.mult), ['mag', 's0out'], ['abi'])
                V(lambda e: e.tensor_scalar(out=nr[:], in0=abr[:], scalar1=-1.0, scalar2=None, op0=ALU.add), ['abr'], ['nr'])
                V(lambda e: e.tensor_tensor(out=den[:], in0=are[:], in1=are[:], op=ALU.mult), ['are'], ['den'])
                V(lambda e: e.tensor_tensor(out=t0[:], in0=aim[:], in1=aim[:], op=ALU.mult), ['aim'], ['t0'])
                V(lambda e: e.tensor_tensor(out=den[:], in0=den[:], in1=t0[:], op=ALU.add), ['den', 't0'], ['den'])
                V(lambda e: e.reciprocal(out=den[:], in_=den[:]), ['den'], ['den'])
                V(lambda e: e.tensor_tensor(out=t0[:], in0=nr[:], in1=are[:], op=ALU.mult), ['nr', 'are'], ['t0'])
                V(lambda e: e.tensor_tensor(out=t1[:], in0=abi[:], in1=aim[:], op=ALU.mult), ['abi', 'aim'], ['t1'])
                V(lambda e: e.tensor_tensor(out=t0[:], in0=t0[:], in1=t1[:], op=ALU.add), ['t0', 't1'], ['t0'])
                V(lambda e: e.tensor_tensor(out=fr[:], in0=t0[:], in1=den[:], op=ALU.mult), ['t0', 'den'], ['fr'])
                V(lambda e: e.tensor_tensor(out=t0[:], in0=abi[:], in1=are[:], op=ALU.mult), ['abi', 'are'], ['t0'])
                V(lambda e: e.tensor_tensor(out=t1[:], in0=nr[:], in1=aim[:], op=ALU.mult), ['nr', 'aim'], ['t1'])
                V(lambda e: e.tensor_tensor(out=t0[:], in0=t0[:], in1=t1[:], op=ALU.subtract), ['t0', 't1'], ['t0'])
                V(lambda e: e.tensor_tensor(out=fi[:], in0=t0[:], in1=den[:], op=ALU.mult), ['t0', 'den'], ['fi'])
                V(lambda e: e.tensor_scalar(out=a128[:], in0=adi[:], scalar1=128.0, scalar2=None, op0=ALU.mult), ['s0phi'], ['s1phi'])
                sincos(e0, a128[:], 16, rots[:], rotc[:], 's1')
                tp1 = sbt(e0, 'tp1', [128, 128])
                T.dma('sp', tp1[:], di['c_tp1'], w=['tp1'], slot='c0')
                phi = sbt(e0, 'phi', [128, 2048])
                V(lambda e: e.tensor_tensor(out=phi[:].rearrange("p (j t) -> p j t", j=16),
                                            in0=adi[:].unsqueeze(2).broadcast_to([128, 16, 128]),
                                            in1=tp1[:].unsqueeze(1).broadcast_to([128, 16, 128]), op=ALU.mult), ['s0phi', 'tp1'], ['s2phi'])
                sincos(e0, phi[:], 2048, stab[:], ctab[:], 's2')
                T.op('act', lambda e: e.activation(out=ctb[:], in_=ctab[:], func=AF.Copy), r=['s2out'], w=['tbb'])
                T.op('act', lambda e: e.activation(out=stb[:], in_=stab[:], func=AF.Copy), r=['s2out'], w=['tbb'])
                bre = sbt(e0, 'bre', [128, 2048])
                bim = sbt(e0, 'bim', [128, 2048])
                u1 = sbt(e0, 'u1', [128, 2048])
                u2 = sbt(e0, 'u2', [128, 2048])
                bb = [sbt(e0, f'bb{k}', [128, 2048]) for k in range(2)]
                T.dma('sp', bre[:], di['bre'], w=['bre'], slot='c1')
                T.dma('sp', bim[:], di['bim'], w=['bim'], slot='c2')
                v3 = lambda t_: t_[:].rearrange("p (j c) -> p j c", j=16)
                bc = lambda t_: t_[:].unsqueeze(2).broadcast_to([128, 16, 128])
                V(lambda e: e.tensor_tensor(out=v3(u1), in0=v3(bre), in1=bc(fr), op=ALU.mult), ['bre', 'fr'], ['u1'])
                V(lambda e: e.tensor_tensor(out=v3(u2), in0=v3(bim), in1=bc(fi), op=ALU.mult), ['bim', 'fi'], ['u2'])
                V(lambda e: e.tensor_tensor(out=bb[0][:], in0=u1[:], in1=u2[:], op=ALU.subtract), ['u1', 'u2'], ['bb0'])
                V(lambda e: e.tensor_tensor(out=v3(u1), in0=v3(bim), in1=bc(fr), op=ALU.mult), ['bim', 'fr', 'bb0'], ['u1'])
                V(lambda e: e.tensor_tensor(out=v3(u2), in0=v3(bre), in1=bc(fi), op=ALU.mult), ['bre', 'fi', 'bb0'], ['u2'])
                V(lambda e: e.tensor_tensor(out=bb[1][:], in0=u1[:], in1=u2[:], op=ALU.add), ['u1', 'u2'], ['bb1'])
                trp = pst(e0, 'trp', [128, 512])
                for k in range(2):
                    for jg in range(4):
                        for jj in range(4):
                            j = jg * 4 + jj
                            T.op('pe', lambda e: e.transpose(trp[:, jj * 128:(jj + 1) * 128], bb[k][:, j * 128:(j + 1) * 128], identf[:]),
                                 r=[f'bb{k}', 'identf'], w=['trp'])
                        T.op('dve', lambda e: e.tensor_copy(out=BbT[k][:, jg * 512:(jg + 1) * 512], in_=trp[:]), r=['trp'], w=[f'BbT{k}'])
                T.dma('sp', bre[:], di['cre'], r=['u1', 'u2'], w=['bre'], slot='c1')
                T.dma('sp', bim[:], di['cim'], r=['u1', 'u2'], w=['bim'], slot='c2')
                T.op('act', lambda e: e.activation(out=Cb[0][:], in_=bre[:], func=AF.Copy), r=['bre'], w=['Cb0'])
                T.op('act', lambda e: e.activation(out=Cb[1][:], in_=bim[:], func=AF.Copy, scale=-1.0), r=['bim'], w=['Cb1'])

            T.barrier()
            with ExitStack() as e5:
                bankU = pst(e5, 'bankU', [128, 512])
                bu = [[pst(e5, f'bu{a}{b}', [128, 512]) for b in range(2)] for a in range(2)]
                yTp = pst(e5, 'yTp', [128, 512])
                zp = pst(e5, 'zp', [128, 512])
                ubs = [sbt(e5, f'ub{b}', [128, 512], BF16) for b in range(2)]
                nm12 = ('t1_', 't2_', 't3_', 't4_', 'vr', 'vi', 'wr', 'wi')
                f4b = {n: [sbt(e5, f'{n}{b}', [128, 512]) for b in range(2)] for n in nm12}
                hb4 = {n: [sbt(e5, f'{n}{b}', [128, 512], BF16) for b in range(2)] for n in ('wrb', 'wib', 'a1b', 'a2b', 'a3b', 'a4b')}
                xrb = [sbt(e5, f'xr{b}', [128, 512], BF16) for b in range(2)]
                xib = [sbt(e5, f'xi{b}', [128, 512], BF16) for b in range(2)]
                w0 = [[sbt(e5, f'w0{p}{k}', [128, 16]) for k in range(2)] for p in range(2)]
                q1b = [sbt(e5, f'q1{b}', [128, 4]) for b in range(2)]
                q2b = [sbt(e5, f'q2{b}', [128, 4]) for b in range(2)]
                ysbs = [sbt(e5, f'ysb{b}', [128, 512]) for b in range(2)]
                ygbs = [sbt(e5, f'ygb{b}', [128, 512], BF16) for b in range(2)]
                sgbs = [sbt(e5, f'sgb{b}', [128, 512]) for b in range(2)]
                yso = [sbt(e5, f'yso{k}', [128, 512], BF16) for k in range(2)]
                for k in range(2):
                    T.op('dve', lambda e: e.memset(w0[0][k][:], 0.0), w=[f'w00{k}'])
                NTS = NT if 'S' in PH else 0

                def s_pre(i):
                    xt, hT, s = rms_hT(i, x_d)
                    p = i % 2
                    ub = ubs[p]
                    for sg in range(4):
                        for c in range(8):
                            T.op('pe', lambda e: e.matmul(bankU[:, sg * 128:(sg + 1) * 128], lhsT=Wu[:, c, sg * 128:(sg + 1) * 128],
                                                          rhs=hT[:, c * 128:(c + 1) * 128], start=(c == 0), stop=(c == 7)),
                                 r=[f'hT{s}'] + kU, w=['bankU'])
                    T.op('act', lambda e: e.activation(out=ub[:], in_=bankU[:], func=AF.Copy), r=['bankU'], w=['ub' + str(p)])

                def s_T(n):
                    i, sg = divmod(n, 4)
                    p, B_ = i % 2, n % 2
                    ub = ubs[p]
                    t1_, t2_, t3_, t4_, vr, vi, wr, wi = [f4b[x][B_] for x in nm12]
                    bur, bui = bu[B_]
                    kr, ki_ = f'bu{B_}0', f'bu{B_}1'
                    cs = ctab[:, sg * 512:(sg + 1) * 512]
                    ss = stab[:, sg * 512:(sg + 1) * 512]
                    for jj in range(4):
                        j = sg * 4 + jj
                        for k, (bt, kk) in enumerate(((bur, kr), (bui, ki_))):
                            T.op('pe', lambda e: e.matmul(bt[:, jj * 128:(jj + 1) * 128], lhsT=BbT[k][:, j * 128:(j + 1) * 128],
                                                          rhs=ub[:, sg * 128:(sg + 1) * 128], start=True, stop=True),
                                 r=['ub' + str(p), f'BbT{k}'], w=[kk])
                    T.op('dve', lambda e: e.tensor_tensor(out=t1_[:], in0=bur[:], in1=cs, op=ALU.mult), r=[kr, 's2out'], w=['t1_' + str(B_)])
                    T.op('dve', lambda e: e.tensor_tensor(out=t2_[:], in0=bui[:], in1=ss, op=ALU.mult), r=[ki_, 's2out'], w=['t2_' + str(B_)])
                    T.op('pool', lambda e: e.tensor_tensor(out=vr[:], in0=t1_[:], in1=t2_[:], op=ALU.add), r=['t1_' + str(B_), 't2_' + str(B_)], w=['vr' + str(B_)])
                    T.op('dve', lambda e: e.tensor_tensor(out=t3_[:], in0=bui[:], in1=cs, op=ALU.mult), r=[ki_, 's2out'], w=['t3_' + str(B_)])
                    T.op('dve', lambda e: e.tensor_tensor(out=t4_[:], in0=bur[:], in1=ss, op=ALU.mult), r=[kr, 's2out'], w=['t4_' + str(B_)])
                    T.op('pool', lambda e: e.tensor_tensor(out=vi[:], in0=t3_[:], in1=t4_[:], op=ALU.subtract), r=['t3_' + str(B_), 't4_' + str(B_)], w=['vi' + str(B_)])

                def s_SC(n):
                    i, sg = divmod(n, 4)
                    p, pn, B_ = i % 2, (i + 1) % 2, n % 2
                    t1_, t2_, t3_, t4_, vr, vi, wr, wi = [f4b[x][B_] for x in nm12]
                    wrb, wib = hb4['wrb'][B_], hb4['wib'][B_]
                    q1, q2 = q1b[B_], q2b[B_]
                    for jj in range(4):
                        j = sg * 4 + jj
                        sl = slice(jj * 128, (jj + 1) * 128)
                        for k, (vv, ww, nm) in enumerate(((vr, wr, 'wr' + str(B_)), (vi, wi, 'wi' + str(B_)))):
                            T.op('dve', lambda e: e.tensor_tensor_scan(out=ww[:, sl], data0=mag[:, j:j + 1].broadcast_to([128, 128]),
                                                                       data1=vv[:, sl], initial=w0[p][k][:, j:j + 1],
                                                                       op0=ALU.mult, op1=ALU.add),
                                 r=['mag', 'vr' + str(B_) if k == 0 else 'vi' + str(B_), f'w0{p}{k}'], w=[nm])
                    T.op('act', lambda e: e.activation(out=wrb[:], in_=wr[:], func=AF.Copy), r=['wr' + str(B_)], w=['wrb' + str(B_)])
                    T.op('act', lambda e: e.activation(out=wib[:], in_=wi[:], func=AF.Copy), r=['wi' + str(B_)], w=['wib' + str(B_)])
                    wl_r = wr[:].rearrange("p (j t) -> p j t", j=4)[:, :, 127:128]
                    wl_i = wi[:].rearrange("p (j t) -> p j t", j=4)[:, :, 127:128]
                    rc = rotc[:, sg * 4:(sg + 1) * 4].unsqueeze(2)
                    rs = rots[:, sg * 4:(sg + 1) * 4].unsqueeze(2)
                    n_r = w0[pn][0][:, sg * 4:(sg + 1) * 4].unsqueeze(2)
                    n_i = w0[pn][1][:, sg * 4:(sg + 1) * 4].unsqueeze(2)
                    T.op('dve', lambda e: e.tensor_tensor(out=q1[:].unsqueeze(2), in0=wl_r, in1=rc, op=ALU.mult), r=['wr' + str(B_), 's1out'], w=['q1' + str(B_)])
                    T.op('dve', lambda e: e.tensor_tensor(out=q2[:].unsqueeze(2), in0=wl_i, in1=rs, op=ALU.mult), r=['wi' + str(B_), 's1out'], w=['q2' + str(B_)])
                    T.op('dve', lambda e: e.tensor_tensor(out=n_r, in0=q1[:].unsqueeze(2), in1=q2[:].unsqueeze(2), op=ALU.subtract),
                         r=['q1' + str(B_), 'q2' + str(B_)], w=[f'w0{pn}0'])
                    T.op('dve', lambda e: e.tensor_tensor(out=q1[:].unsqueeze(2), in0=wl_i, in1=rc, op=ALU.mult), r=['wi' + str(B_), 's1out', f'w0{pn}0'], w=['q1' + str(B_)])
                    T.op('dve', lambda e: e.tensor_tensor(out=q2[:].unsqueeze(2), in0=wl_r, in1=rs, op=ALU.mult), r=['wr' + str(B_), 's1out', f'w0{pn}0'], w=['q2' + str(B_)])
                    T.op('dve', lambda e: e.tensor_tensor(out=n_i, in0=q1[:].unsqueeze(2), in1=q2[:].unsqueeze(2), op=ALU.add),
                         r=['q1' + str(B_), 'q2' + str(B_)], w=[f'w0{pn}1'])

                def s_A(n):
                    i, sg = divmod(n, 4)
                    B_ = n % 2
                    wrb, wib, a1, a2, a3, a4 = [hb4[x][B_] for x in ('wrb', 'wib', 'a1b', 'a2b', 'a3b', 'a4b')]
                    xr, xi = xrb[B_], xib[B_]
                    csb = ctb[:, sg * 512:(sg + 1) * 512]
                    ssb = stb[:, sg * 512:(sg + 1) * 512]
                    T.op('dve', lambda e: e.tensor_tensor(out=a1[:], in0=wrb[:], in1=csb, op=ALU.mult), r=['wrb' + str(B_), 'tbb'], w=['a1' + str(B_)])
                    T.op('dve', lambda e: e.tensor_tensor(out=a2[:], in0=wib[:], in1=ssb, op=ALU.mult), r=['wib' + str(B_), 'tbb'], w=['a2' + str(B_)])
                    T.op('dve', lambda e: e.tensor_tensor(out=xr[:], in0=a1[:], in1=a2[:], op=ALU.subtract), r=['a1' + str(B_), 'a2' + str(B_)], w=['xr' + str(B_)])
                    T.op('dve', lambda e: e.tensor_tensor(out=a3[:], in0=wib[:], in1=csb, op=ALU.mult), r=['wib' + str(B_), 'tbb'], w=['a3' + str(B_)])
                    T.op('dve', lambda e: e.tensor_tensor(out=a4[:], in0=wrb[:], in1=ssb, op=ALU.mult), r=['wrb' + str(B_), 'tbb'], w=['a4' + str(B_)])
                    T.op('dve', lambda e: e.tensor_tensor(out=xi[:], in0=a3[:], in1=a4[:], op=ALU.add), r=['a3' + str(B_), 'a4' + str(B_)], w=['xi' + str(B_)])
                    for jj in range(4):
                        j = sg * 4 + jj
                        sl = slice(jj * 128, (jj + 1) * 128)
                        T.op('pe', lambda e: e.matmul(yTp[:, sg * 128:(sg + 1) * 128], lhsT=Cb[0][:, j * 128:(j + 1) * 128], rhs=xr[:, sl],
                                                      start=(jj == 0), stop=False), r=['Cb0', 'xr' + str(B_)], w=['yTp'])
                        T.op('pe', lambda e: e.matmul(yTp[:, sg * 128:(sg + 1) * 128], lhsT=Cb[1][:, j * 128:(j + 1) * 128], rhs=xi[:, sl],
                                                      start=False, stop=(jj == 3)), r=['Cb1', 'xi' + str(B_)], w=['yTp'])

                def s_tail(i):
                    p = i % 2
                    ub, ysb, ygb, sgb = ubs[p], ysbs[p], ygbs[p], sgbs[p]
                    for sg in range(4):
                        sl = slice(sg * 128, (sg + 1) * 128)
                        T.op('dve', lambda e: e.scalar_tensor_tensor(out=ysb[:, sl], in0=ub[:, sl], scalar=dsk[:, sg:sg + 1], in1=yTp[:, sl],
                                                                     op0=ALU.mult, op1=ALU.add), r=['ub' + str(p), 'dsk', 'yTp'], w=['ysb' + str(p)])
                    T.op('act', lambda e: e.activation(out=ygb[:], in_=ysb[:], func=AF.Gelu_apprx_tanh), r=['ysb' + str(p)], w=['ygb' + str(p)])
                    for co in range(4):
                        for ci in range(4):
                            T.op('pe', lambda e: e.matmul(zp[:, co * 128:(co + 1) * 128], lhsT=Wgl[:, ci, co * 128:(co + 1) * 128],
                                                          rhs=ygb[:, ci * 128:(ci + 1) * 128], start=(ci == 0), stop=(ci == 3)),
                                 r=['ygb' + str(p)] + kGl, w=['zp'])
                    for co in range(4):
                        sl = slice(co * 128, (co + 1) * 128)
                        T.op('act', lambda e: e.activation(out=sgb[:, sl], in_=zp[:, sl], func=AF.Sigmoid, bias=bglu[:, co:co + 1]),
                             r=['zp', 'bglu'], w=['sgb' + str(p)])
                    yo = yso[i % 2]
                    T.op('pool', lambda e: e.tensor_tensor(out=yo[:], in0=ygb[:], in1=sgb[:], op=ALU.mult), r=['ygb' + str(p), 'sgb' + str(p)], w=[f'yso{i % 2}'])
                    T.dma('sp', ys_d[i], yo[:], r=[f'yso{i % 2}'], w=[f'ys_{i}'], slot=f'yso{i % 2}')

                NG = 4 * NTS
                if NTS > 0:
                    s_pre(0)
                    s_T(0)
                for n in range(NG):
                    if n + 1 < NG:
                        if (n + 1) % 4 == 0:
                            s_pre((n + 1) // 4)
                        s_T(n + 1)
                    s_SC(n)
                    if n >= 1:
                        s_A(n - 1)
                        if n % 4 == 0:
                            s_tail(n // 4 - 1)
                if NTS > 0:
                    s_A(NG - 1)
                    s_tail(NTS - 1)

        T.barrier()
        with ExitStack() as e6:
            Wmg = sbt(e6, 'Wmg', [128, 8, 2048], BF16)
            Wps = sbt(e6, 'Wps', [128, 4, D], BF16)
            Wpn = sbt(e6, 'Wpn', [128, 4, D], BF16)
            Wo = sbt(e6, 'Wo', [128, 8, D], BF16)
            kMg = load_w(Wmg, di['w_in'], D, 2048, 'Wmg', col0=O4, gt=g1t)
            kPs = load_w(Wps, di['w_ps'], 512, D, 'Wps')
            kPn = load_w(Wpn, di['w_pn'], 512, D, 'Wpn')
            kWo = load_w(Wo, di['w_out'], D, D, 'Wo')
            Gp = [pst(e6, f'Gp{k}', [128, 512]) for k in range(2)]
            Pp = [pst(e6, f'Pp{k}', [128, 512]) for k in range(2)]
            Op = [pst(e6, f'Op{k}', [128, 512]) for k in range(2)]
            gs = [sbt(e6, f'gs{k}', [128, 512]) for k in range(2)]
            mt = [sbt(e6, f'mt{k}', [128, 512]) for k in range(2)]
            mrg = sbt(e6, 'mrg', [128, D], BF16)
            mT = sbt(e6, 'mT', [128, D], BF16)
            ysl_ = [sbt(e6, f'ysl{k}', [128, 512], BF16) for k in range(2)]
            ynl_ = [sbt(e6, f'ynl{k}', [128, 512], BF16) for k in range(2)]
            x1s = [sbt(e6, f'x1s{k}', [128, D]) for k in range(2)]
            for i in range(NT if '4' in PH else 0):
                xt, hT, s = rms_hT(i, x_d)
                q = i % 2
                T.dma('sp', ysl_[q][:], ys_d[i], r=[f'ys_{i}'], w=[f'ysl{q}'], slot=f'ysl{q}')
                T.dma('sp', ynl_[q][:], yn_d[i], r=[f'yn_{i}'], w=[f'ynl{q}'], slot=f'ynl{q}')
                for hf in range(2):
                    hs = slice(hf * 512, (hf + 1) * 512)
                    for m_ in range(2):
                        for c in range(8):
                            T.op('pe', lambda e: e.matmul(Gp[m_][:], lhsT=hT[:, c * 128:(c + 1) * 128],
                                                          rhs=Wmg[:, c, m_ * 1024 + hf * 512:m_ * 1024 + (hf + 1) * 512],
                                                          start=(c == 0), stop=(c == 7)), r=[f'hT{s}'] + kMg, w=[f'Gp{m_}'])
                        T.op('act', lambda e: e.activation(out=gs[m_][:], in_=Gp[m_][:], func=AF.Sigmoid), r=[f'Gp{m_}'], w=[f'gs{m_}'])
                        yl, Wp, kk, kn_ = ((ysl_[q], Wps, kPs, f'ysl{q}'), (ynl_[q], Wpn, kPn, f'ynl{q}'))[m_]
                        for ci in range(4):
                            T.op('pe', lambda e: e.matmul(Pp[m_][:], lhsT=yl[:, ci * 128:(ci + 1) * 128], rhs=Wp[:, ci, hs],
                                                          start=(ci == 0), stop=(ci == 3)), r=[kn_] + kk, w=[f'Pp{m_}'])
                        T.op('dve', lambda e: e.tensor_tensor(out=mt[m_][:], in0=Pp[m_][:], in1=gs[m_][:], op=ALU.mult),
                             r=[f'Pp{m_}', f'gs{m_}'], w=[f'mt{m_}'])
                    T.op('pool', lambda e: e.tensor_tensor(out=mrg[:, hs], in0=mt[0][:], in1=mt[1][:], op=ALU.add), r=['mt0', 'mt1'], w=['mrg'])
                for c in range(8):
                    T.op('pe', lambda e: e.transpose(tp[:, c * 128:(c + 1) * 128], mrg[:, c * 128:(c + 1) * 128], identb[:]),
                         r=['mrg', 'identb'], w=['tp'])
                T.op('dve', lambda e: e.tensor_copy(out=mT[:], in_=tp[:]), r=['tp'], w=['mT'])
                for hf in range(2):
                    hs = slice(hf * 512, (hf + 1) * 512)
                    for c in range(8):
                        T.op('pe', lambda e: e.matmul(Op[hf][:], lhsT=mT[:, c * 128:(c + 1) * 128], rhs=Wo[:, c, hs],
                                                      start=(c == 0), stop=(c == 7)), r=['mT'] + kWo, w=[f'Op{hf}'])
                    T.op('dve', lambda e: e.tensor_tensor(out=x1s[q][:, hs], in0=Op[hf][:], in1=xt[:, hs], op=ALU.add),
                         r=[f'Op{hf}', f'xt{s}'], w=[f'x1s{q}'])
                T.dma('sp', x1_d[i * 128:(i + 1) * 128, :], x1s[q][:], r=[f'x1s{q}'], w=[f'x1_{i}', f'ys_{i}', f'yn_{i}'], slot=f'x1s{q}')

        T.barrier()
        with ExitStack() as e7:
            Wup = sbt(e7, 'Wup', [128, 8, 4096], BF16)
            Wdn = sbt(e7, 'Wdn', [128, 32, D], BF16)
            kUp = load_w(Wup, di['w_up'], D, 4096, 'Wup', gt=g2t)
            kDn = load_w(Wdn, di['w_down'], 4096, D, 'Wdn')
            Up = [pst(e7, f'Up{k}', [128, 512]) for k in range(2)]
            Dn = [pst(e7, f'Dn{k}', [128, 512]) for k in range(2)]
            rl = [sbt(e7, f'rl{k}', [128, 512]) for k in range(2)]
            acT = [sbt(e7, f'acT{k}', [128, 512], BF16) for k in range(2)]
            os_ = [sbt(e7, f'os{k}', [128, D]) for k in range(2)]
            for i in range(NT if '5' in PH else 0):
                xt, hT, s = rms_hT(i, x1_d, keysrc=[f'x1_{i}'])
                q = i % 2
                def down(fg):
                    u = fg % 2
                    for q4 in range(4):
                        fc = fg * 4 + q4
                        for hf in range(2):
                            T.op('pe', lambda e: e.matmul(Dn[hf][:], lhsT=acT[u][:, q4 * 128:(q4 + 1) * 128],
                                                          rhs=Wdn[:, fc, hf * 512:(hf + 1) * 512], start=(fc == 0), stop=(fc == 31)),
                                 r=[f'acT{u}'] + kDn, w=[f'Dn{hf}'])

                for fg in range(8):
                    u = fg % 2
                    for q4 in range(4):
                        fc = fg * 4 + q4
                        for c in range(8):
                            T.op('pe', lambda e: e.matmul(Up[u][:, q4 * 128:(q4 + 1) * 128], lhsT=Wup[:, c, fc * 128:(fc + 1) * 128],
                                                          rhs=hT[:, c * 128:(c + 1) * 128], start=(c == 0), stop=(c == 7)),
                                 r=[f'hT{s}'] + kUp, w=[f'Up{u}'])
                    T.op('act', lambda e: e.activation(out=rl[u][:], in_=Up[u][:], func=AF.Relu), r=[f'Up{u}'], w=[f'rl{u}'])
                    T.op('dve', lambda e: e.tensor_tensor(out=acT[u][:], in0=rl[u][:], in1=rl[u][:], op=ALU.mult), r=[f'rl{u}'], w=[f'acT{u}'])
                    if fg > 0:
                        down(fg - 1)
                down(7)
                for hf in range(2):
                    hs = slice(hf * 512, (hf + 1) * 512)
                    T.op('dve', lambda e: e.tensor_tensor(out=os_[q][:, hs], in0=Dn[hf][:], in1=xt[:, hs], op=ALU.add),
                         r=[f'Dn{hf}', f'xt{s}'], w=[f'os{q}'])
                T.dma('sp', out_d[i * 128:(i + 1) * 128, :], os_[q][:], r=[f'os{q}'], w=[f'out_{i}', f'x1_{i}'], slot=f'os{q}')
        T.finish()
        print("instructions:", T.nins)
    return nc, hc


_CACHE = {}


def run(inputs, SEQ, n_cores, dbg=False):
    key = (SEQ, dbg)
    if key not in _CACHE:
        _CACHE[key] = build(SEQ, dbg)
    nc, hc = _CACHE[key]
    L = host_layouts(inputs)
    x = np.asarray(inputs['x'], dtype=np.float32)
    B = x.shape[0]
    in_maps = []
    for core in range(n_cores):
        m = {'x': np.ascontiguousarray(x[core % B])}
        for k, v in hc.items():
            m['c_' + k] = v
        m.update(L)
        in_maps.append(m)
    res = run_bass_kernel_spmd(nc, in_maps, core_ids=list(range(n_cores)))
    return res


def kernel(**inputs):
    x = np.asarray(inputs['x'])
    B, SEQ, _ = x.shape
    res = run(inputs, SEQ, B)
    return np.stack([np.asarray(res.results[b]['out'], dtype=np.float32) for b in range(B)], 0)
```

```python
import os
import numpy as np
import ml_dtypes
from contextlib import ExitStack
import concourse.bass as bass
import concourse.mybir as mybir
from concourse.bass_utils import run_bass_kernel_spmd

F32 = mybir.dt.float32
BF16 = mybir.dt.bfloat16
I32 = mybir.dt.int32
AF = mybir.ActivationFunctionType
ALU = mybir.AluOpType
AX = mybir.AxisListType
SAME_ENGINE_SYNC = True
NEG = -30000.0
EPS = 1e-6
D = 1024
IN_W = 3864
O1, O2, O3, O4 = 512, 1024, 1792, 1816
BF = ml_dtypes.bfloat16


class TS:
    def __init__(self, nc, es):
        self.nc = nc
        self.es = es
        self.E = dict(pe=nc.tensor, act=nc.scalar, dve=nc.vector, pool=nc.gpsimd, sp=nc.sync)
        self.sem = {k: es.enter_context(nc.semaphore("s_" + k)) for k in self.E}
        self.cnt = {k: 0 for k in self.E}
        self.waited = {k: {} for k in self.E}
        self.lastw = {}
        self.readers = {}
        self.nins = 0

    def _need(self, e, reads, writes):
        deps = {}
        for k in reads:
            t = self.lastw.get(k)
            if t is not None and deps.get(t[0], 0) < t[1]:
                deps[t[0]] = t[1]
        for k in writes:
            t = self.lastw.get(k)
            if t is not None and deps.get(t[0], 0) < t[1]:
                deps[t[0]] = t[1]
            for s, v in self.readers.get(k, {}).items():
                if deps.get(s, 0) < v:
                    deps[s] = v
        for s, v in deps.items():
            if s == e and (e == 'pe' or not SAME_ENGINE_SYNC):
                continue
            if self.waited[e].get(s, 0) >= v:
                continue
            self.E[e].wait_ge(self.sem[s], v)
            self.waited[e][s] = v

    def _record(self, tok, reads, writes):
        s, v = tok
        for k in writes:
            self.lastw[k] = tok
            self.readers[k] = {}
        for k in reads:
            d = self.readers.setdefault(k, {})
            if d.get(s, 0) < v:
                d[s] = v

    def op(self, e, fn, r=(), w=()):
        self._need(e, r, w)
        ins = fn(self.E[e])
        self.cnt[e] += 1
        ins.then_inc(self.sem[e], 1)
        self._record((e, self.cnt[e]), r, w)
        self.nins += 1

    def dma(self, q, out, in_, r=(), w=(), slot=None):
        self._need(q, r, w)
        name = 'd_' + slot
        if name not in self.sem:
            self.sem[name] = self.es.enter_context(self.nc.semaphore("s_" + name))
            self.cnt[name] = 0
        if self.cnt[name] > 0 and self.waited[q].get(name, 0) < self.cnt[name]:
            self.E[q].wait_ge(self.sem[name], self.cnt[name])
            self.waited[q][name] = self.cnt[name]
        ins = self.E[q].dma_start(out=out, in_=in_)
        self.cnt[name] += 16
        ins.then_inc(self.sem[name], 16)
        self._record((name, self.cnt[name]), r, w)
        self.nins += 1

    def barrier(self):
        for e in self.E:
            for s, h in self.sem.items():
                if self.cnt[s] > 0 and self.waited[e].get(s, 0) < self.cnt[s]:
                    self.E[e].wait_ge(h, self.cnt[s])
                    self.waited[e][s] = self.cnt[s]

    def finish(self):
        for s, h in self.sem.items():
            if self.cnt[s] > 0 and self.waited['sp'].get(s, 0) < self.cnt[s]:
                self.E['sp'].wait_ge(h, self.cnt[s])


def host_consts(SEQ):
    NT = SEQ // 128
    nblk = SEQ // 16 - 1
    NCT = (nblk + 127) // 128
    n_sel = SEQ // 64
    c = {}
    c['identf'] = np.eye(128, dtype=np.float32)
    i4 = np.tile(np.eye(128, dtype=np.float32), (1, 4))
    c['i4'] = i4.astype(BF)
    qi = np.arange(128)[:, None]
    ki = np.arange(128)[None, :]
    c['negtri'] = np.where(ki > qi, NEG, 0.0).astype(BF)
    c['negtric'] = np.where(ki <= qi, NEG, 0.0).astype(BF)
    keys = np.arange(SEQ)
    c['eall'] = (keys[None, :] // 64 == np.arange(128)[:, None]).astype(np.float32).astype(BF)
    A = np.zeros((128, NCT, 128), np.float32)
    for cc in range(NCT * 128):
        for n in ((cc + 1) // 4, cc // 4):
            if 4 * n - 1 <= cc <= 4 * n + 3 and n < n_sel and n != 127:
                A[cc % 128, cc // 128, n] = 1.0
    A[:, :, 127] = 1.0
    c['aaug'] = A.reshape(128, NCT * 128).astype(BF)
    m = np.arange(-127, 145)[None, :]
    f = ((np.arange(128) + 1) // 16)[:, None]
    c['wc'] = np.where(m > f, NEG, 0.0).astype(BF)
    t = np.arange(SEQ)
    cur = t // 64
    n = np.arange(128)[None, :]
    fb = np.zeros((SEQ, 128), np.float32)
    forced = (n == 0) | (n == cur[:, None]) | (n == cur[:, None] - 1)
    fb[forced] = 1e4
    fb[(n > cur[:, None]) | (n >= n_sel)] = -1e30
    c['fb'] = fb
    kpos = np.stack([64 * (t // 64), t % 64, np.ones_like(t), np.ones_like(t)]).astype(np.float32)
    c['kpos'] = kpos.astype(BF)
    ce = 16 * np.arange(NCT * 128) + 31
    c['cpos'] = np.stack([64 * (ce // 64), ce % 64, np.ones_like(ce), np.ones_like(ce)]).astype(np.float32).astype(BF)
    slopes = 2.0 ** -(np.arange(8) + 1.0)
    qpos = np.zeros((4, 8, SEQ), np.float32)
    qpos[0] = slopes[:, None]
    qpos[1] = slopes[:, None]
    qpos[2] = -slopes[:, None] * (64 * (t // 64))[None, :]
    qpos[3] = -slopes[:, None] * (t % 64)[None, :]
    c['qpos'] = qpos.astype(BF)
    c['tp1'] = np.tile(np.arange(1, 129, dtype=np.float32)[None, :], (128, 1))
    c['ones64'] = np.full((64, 64), 1.0 / 64, np.float32)
    return c


def host_layouts(inp):
    L = {}
    f = lambda a: np.ascontiguousarray(np.asarray(a, dtype=np.float32))
    L['g1'] = f(inp['norm1_g'][0].reshape(8, 128).T)
    L['g2'] = f(inp['norm2_g'][0].reshape(8, 128).T)
    L['w_in'] = f(inp['w_in'][0])
    L['w_glu'] = f(inp['ssm_w_glu'][0])
    L['w_ps'] = f(inp['w_proj_ssm'][0])
    L['w_pn'] = f(inp['w_proj_nsa'][0])
    L['w_out'] = f(inp['w_out'][0])
    L['w_up'] = f(inp['w_up'][0])
    L['w_down'] = f(inp['w_down'][0])
    L['gq'] = f(inp['q_norm_g'][0].reshape(64, 1))
    L['gk'] = f(inp['k_norm_g'][0].T)
    for nm, key in (('w1k', 'cmp_wk1'), ('w1v', 'cmp_wv1')):
        w = np.asarray(inp[key][0]).reshape(32, 64, 128).transpose(1, 0, 2)
        L[nm] = f(np.concatenate([w, w], 0).reshape(128, 32 * 128))
    L['w2k'] = f(inp['cmp_wk2'][0])
    L['w2v'] = f(inp['cmp_wv2'][0])
    L['pek'] = f(np.asarray(inp['cmp_pe_k'][0]).T)
    L['pev'] = f(np.asarray(inp['cmp_pe_v'][0]).T)

    def sm(a):
        a = np.asarray(a)
        return f(a.reshape(16, 2, 64).transpose(1, 2, 0).reshape(128, 16))
    L['are'] = sm(inp['ssm_a_re'][0])
    L['aim'] = sm(inp['ssm_a_im'][0])
    L['ldt'] = sm(np.repeat(np.asarray(inp['ssm_log_dt'][0])[:, None], 64, 1))

    def pad_b(b):
        b = np.asarray(b)
        o = np.zeros((128, 16, 128), np.float32)
        for g in range(32):
            j, gl = g // 2, g % 2
            o[gl * 64:(gl + 1) * 64, j, (j % 4) * 32 + gl * 16:(j % 4) * 32 + gl * 16 + 16] = b[g]
        return o.reshape(128, 2048)
    L['bre'] = pad_b(inp['ssm_b_re'][0])
    L['bim'] = pad_b(inp['ssm_b_im'][0])
    L['cre'] = pad_b(np.asarray(inp['ssm_c_re'][0]).transpose(0, 2, 1))
    L['cim'] = pad_b(np.asarray(inp['ssm_c_im'][0]).transpose(0, 2, 1))
    L['dsk'] = f(inp['ssm_d'][0].reshape(4, 128).T)
    L['bglu'] = f(inp['ssm_b_glu'][0].reshape(4, 128).T)
    return L


def build(SEQ, dbg=False):
    PH = os.environ.get('PHASES', '123S45')
    NT = SEQ // 128
    nblk = SEQ // 16 - 1
    NCT = (nblk + 127) // 128
    nc = bass.Bass("TRN2", target_bir_lowering=False)
    hc = host_consts(SEQ)
    di = {}

    def din(name, shape, dt=F32):
        di[name] = nc.dram_tensor(name, list(shape), dt, kind="ExternalInput").ap()
        return di[name]

    x_d = din('x', [SEQ, D])
    for k, v in hc.items():
        din('c_' + k, v.shape, BF16 if v.dtype == BF else F32)
    shapes = dict(g1=[128, 8], g2=[128, 8], w_in=[D, IN_W], w_glu=[512, 512], w_ps=[512, D], w_pn=[512, D],
                  w_out=[D, D], w_up=[D, 4096], w_down=[4096, D], gq=[64, 1], gk=[64, 3], w1k=[128, 4096],
                  w1v=[128, 4096], w2k=[128, 64], w2v=[128, 64], pek=[64, 32], pev=[64, 32], are=[128, 16],
                  aim=[128, 16], ldt=[128, 16], bre=[128, 2048], bim=[128, 2048], cre=[128, 2048],
                  cim=[128, 2048], dsk=[128, 4], bglu=[128, 4])
    for k, s in shapes.items():
        din(k, s)
    out_d = nc.dram_tensor('out', [SEQ, D], F32, kind="ExternalOutput").ap()
    if dbg:
        yn_t = nc.dram_tensor('yn_scr', [NT, 128, 512], BF16, kind="ExternalOutput").ap()
        ys_t = nc.dram_tensor('ys_scr', [NT, 128, 512], BF16, kind="ExternalOutput").ap()
        x1_d = nc.dram_tensor('x1_scr', [SEQ, D], F32, kind="ExternalOutput").ap()
        yn_d = [yn_t[i] for i in range(NT)]
        ys_d = [ys_t[i] for i in range(NT)]
    else:
        out_bf = out_d.bitcast(BF16)
        yn_d = [out_bf[i * 128:(i + 1) * 128, 0:512] for i in range(NT)]
        ys_d = [out_bf[i * 128:(i + 1) * 128, 512:1024] for i in range(NT)]
        x1_d = out_d

    with ExitStack() as es:
        T = TS(nc, es)

        def sbt(st, n, s, d=F32):
            return st.enter_context(nc.sbuf_tensor('sb_' + n, list(s), d))

        def pst(st, n, s, d=F32):
            return st.enter_context(nc.psum_tensor('ps_' + n, list(s), d))

        identf = sbt(es, 'identf', [128, 128])
        identb = sbt(es, 'identb', [128, 128], BF16)
        i4 = sbt(es, 'i4', [128, 512], BF16)
        negtri = sbt(es, 'negtri', [128, 128], BF16)
        negtric = sbt(es, 'negtric', [128, 128], BF16)
        epsb = sbt(es, 'epsb', [128, 1])
        g1t = sbt(es, 'g1t', [128, 8])
        g2t = sbt(es, 'g2t', [128, 8])
        xts = [sbt(es, f'xt{s}', [128, D]) for s in range(2)]
        hbs = [sbt(es, f'hb{s}', [128, D], BF16) for s in range(2)]
        hTs = [sbt(es, f'hT{s}', [128, D], BF16) for s in range(2)]
        junk = sbt(es, 'junk', [128, D], BF16)
        ssqs = [sbt(es, f'ssq{s}', [128, 1]) for s in range(2)]
        rstds = [sbt(es, f'rstd{s}', [128, 1]) for s in range(2)]
        stg = [sbt(es, f'stg{s}', [128, 1024]) for s in range(2)]
        tp = pst(es, 'tp', [128, 1024], BF16)
        wstate = {'slot': 0}

        T.dma('sp', identf[:], di['c_identf'], w=['identf'], slot='c0')
        T.dma('sp', i4[:], di['c_i4'], w=['i4'], slot='c1')
        T.dma('sp', negtri[:], di['c_negtri'], w=['negtri'], slot='c2')
        T.dma('sp', negtric[:], di['c_negtric'], w=['negtric'], slot='c3')
        T.dma('sp', g1t[:], di['g1'], w=['g1t'], slot='c4')
        T.dma('sp', g2t[:], di['g2'], w=['g2t'], slot='c5')
        T.op('dve', lambda e: e.memset(epsb[:], EPS), w=['epsb'])
        T.op('dve', lambda e: e.tensor_copy(out=identb[:], in_=identf[:]), r=['identf'], w=['identb'])

        def load_cast(dst_ap, src_ap, n, key, scale=None, parts=128):
            s = wstate['slot']
            wstate['slot'] ^= 1
            T.dma('sp', stg[s][0:parts, 0:n], src_ap, w=[f'stg{s}'], slot=f'stg{s}')
            if scale is None:
                T.op('act', lambda e: e.activation(out=dst_ap, in_=stg[s][0:parts, 0:n], func=AF.Copy), r=[f'stg{s}'], w=[key])
            else:
                T.op('act', lambda e: e.activation(out=dst_ap, in_=stg[s][0:parts, 0:n], func=AF.Copy, scale=scale),
                     r=[f'stg{s}', 'g1t', 'g2t'], w=[key])

        def load_w(dst, src, rows, cols, name, col0=0, gt=None):
            keys = []
            for c in range(rows // 128):
                for a in range(0, cols, 1024):
                    n = min(1024, cols - a)
                    k = f'{name}_{c}_{a}'
                    load_cast(dst[:, c, a:a + n], src[c * 128:(c + 1) * 128, col0 + a:col0 + a + n], n, k,
                              scale=None if gt is None else gt[:, c:c + 1])
                    keys.append(k)
            return keys

        def rms_hT(i, src_d, keysrc=()):
            s = i % 2
            xt, hb, hT = xts[s], hbs[s], hTs[s]
            T.dma('sp', xt[:], src_d[i * 128:(i + 1) * 128, :], r=list(keysrc), w=[f'xt{s}'], slot=f'xt{s}')
            T.op('act', lambda e: e.activation(out=junk[:], in_=xt[:], func=AF.Square, accum_out=ssqs[s][:]),
                 r=[f'xt{s}'], w=['junk', f'ssq{s}'])
            T.op('act', lambda e: e.activation(out=rstds[s][:], in_=ssqs[s][:], func=AF.Sqrt, bias=epsb[:], scale=1.0 / D),
                 r=[f'ssq{s}', 'epsb'], w=[f'rstd{s}'])
            T.op('dve', lambda e: e.reciprocal(out=rstds[s][:], in_=rstds[s][:]), r=[f'rstd{s}'], w=[f'rstd{s}'])
            T.op('act', lambda e: e.activation(out=hb[:], in_=xt[:], func=AF.Copy, scale=rstds[s][:]),
                 r=[f'xt{s}', f'rstd{s}'], w=[f'hb{s}'])
            for c in range(8):
                T.op('pe', lambda e: e.transpose(tp[:, c * 128:(c + 1) * 128], hb[:, c * 128:(c + 1) * 128], identb[:]),
                     r=[f'hb{s}', 'identb'], w=['tp'])
            T.op('dve', lambda e: e.tensor_copy(out=hT[:], in_=tp[:]), r=['tp'], w=[f'hT{s}'])
            return xt, hT, s

        with ExitStack() as ea:
            KselT = [sbt(ea, f'KselT{g}', [68, SEQ], BF16) for g in range(2)]
            KwinT = [sbt(ea, f'KwinT{g}', [68, SEQ], BF16) for g in range(2)]
            Vsel = sbt(ea, 'Vsel', [128, NT * 130], BF16)
            Vwin = sbt(ea, 'Vwin', [128, NT * 130], BF16)
            KcT = [sbt(ea, f'KcT{g}', [68, NCT * 128], BF16) for g in range(2)]
            Vc = sbt(ea, 'Vc', [128, NCT * 130], BF16)
            gqt = sbt(ea, 'gqt', [64, 1])
            gq8 = sbt(ea, 'gq8', [64, 1])
            gkt = sbt(ea, 'gkt', [64, 3])
            T.dma('sp', gqt[:], di['gq'], w=['gqt'], slot='c0')
            T.dma('sp', gkt[:], di['gk'], w=['gkt'], slot='c1')
            T.op('dve', lambda e: e.tensor_scalar(out=gq8[:], in0=gqt[:], scalar1=0.125, scalar2=None, op0=ALU.mult), r=['gqt'], w=['gq8'])
            for g in range(2):
                T.dma('sp', KselT[g][64:68, :], di['c_kpos'], w=[f'KselTp{g}'], slot='c2')
                T.dma('sp', KwinT[g][64:68, :], di['c_kpos'], w=[f'KwinTp{g}'], slot='c3')
                T.dma('sp', KcT[g][64:68, :], di['c_cpos'], w=[f'KcTp{g}'], slot='c4')
            T.op('pool', lambda e: e.memset(Vsel[:], 1.0), w=['Vsel_init'])
            T.op('pool', lambda e: e.memset(Vwin[:], 1.0), w=['Vwin_init'])
            T.op('pool', lambda e: e.memset(Vc[:], 1.0), w=['Vc_init'])

            with ExitStack() as e1:
                Wkv = sbt(e1, 'Wkv', [128, 8, 768], BF16)
                rawk = sbt(e1, 'rawk', [128, SEQ], BF16)
                rawv = sbt(e1, 'rawv', [128, SEQ], BF16)
                kvp = pst(e1, 'kvp', [128, 512])
                cmp_ = pst(e1, 'cmp_', [128, 512])
                tp2 = pst(e1, 'tp2', [128, 1024], BF16)
                sq = sbt(e1, 'sq', [128, 256])
                ssk = sbt(e1, 'ssk', [128, 4])
                rsk = sbt(e1, 'rsk', [128, 4])
                kn = sbt(e1, 'kn', [128, 256], BF16)
                kW = load_w(Wkv, di['w_in'], D, 768, 'Wkv', col0=O2, gt=g1t)
                for i in range(NT if '1' in PH else 0):
                    xt, hT, s = rms_hT(i, x_d)
                    cs = slice(i * 128, (i + 1) * 128)
                    for c in range(8):
                        T.op('pe', lambda e: e.matmul(kvp[:], lhsT=hT[:, c * 128:(c + 1) * 128], rhs=Wkv[:, c, 256:768],
                                                      start=(c == 0), stop=(c == 7)), r=[f'hT{s}'] + kW, w=['kvp'])
                    for half in range(2):
                        for c in range(8):
                            T.op('pe', lambda e: e.matmul(cmp_[:, half * 128:(half + 1) * 128],
                                                          lhsT=Wkv[:, c, half * 128:(half + 1) * 128],
                                                          rhs=hT[:, c * 128:(c + 1) * 128], start=(c == 0), stop=(c == 7)),
                                 r=[f'hT{s}'] + kW, w=['cmp_'])
                    T.op('act', lambda e: e.activation(out=sq[:, 0:128], in_=kvp[:, 0:128], func=AF.Square), r=['kvp'], w=['sq'])
                    T.op('act', lambda e: e.activation(out=sq[:, 128:256], in_=kvp[:, 256:384], func=AF.Square), r=['kvp'], w=['sq'])
                    T.op('dve', lambda e: e.tensor_reduce(out=ssk[:], in_=sq[:].rearrange("p (h d) -> p h d", h=4), axis=AX.X, op=ALU.add),
                         r=['sq'], w=['ssk'])
                    T.op('act', lambda e: e.activation(out=rsk[:], in_=ssk[:], func=AF.Sqrt, bias=epsb[:], scale=1.0 / 64),
                         r=['ssk', 'epsb'], w=['rsk'])
                    T.op('dve', lambda e: e.reciprocal(out=rsk[:], in_=rsk[:]), r=['rsk'], w=['rsk'])
                    for b in range(2):
                        T.op('dve', lambda e: e.tensor_tensor(
                            out=kn[:, b * 128:(b + 1) * 128].rearrange("p (h d) -> p h d", h=2),
                            in0=kvp[:, b * 256:b * 256 + 128].rearrange("p (h d) -> p h d", h=2),
                            in1=rsk[:, 2 * b:2 * b + 2].unsqueeze(2).broadcast_to([128, 2, 64]), op=ALU.mult),
                            r=['kvp', 'rsk'], w=['kn'])
                    for j in range(4):
                        T.op('pe', lambda e: e.transpose(tp2[0:64, j * 128:(j + 1) * 128], kn[:, j * 64:(j + 1) * 64], identb[:]),
                             r=['kn', 'identb'], w=['tp2'])
                    for j in range(4):
                        br, g = (1, j) if j < 2 else (2, j - 2)
                        dst = (KselT if br == 1 else KwinT)[g]
                        T.op('act', lambda e: e.activation(out=dst[0:64, cs], in_=tp2[0:64, j * 128:(j + 1) * 128], func=AF.Copy,
                                                           scale=gkt[:, br:br + 1]), r=['tp2', 'gkt'],
                             w=[f'K{br}_{g}_{i}'])
                    for b, Vt, nm in ((0, Vsel, 'Vs'), (1, Vwin, 'Vw')):
                        T.op('dve', lambda e: e.tensor_copy(
                            out=Vt[:, i * 130:(i + 1) * 130].rearrange("p (g d) -> p g d", g=2)[:, :, 0:64],
                            in_=kvp[:, b * 256 + 128:b * 256 + 256].rearrange("p (g d) -> p g d", g=2)),
                            r=['kvp', 'Vsel_init', 'Vwin_init'], w=[f'{nm}_{i}'])
                    T.op('act', lambda e: e.activation(out=rawk[:, cs], in_=cmp_[:, 0:128], func=AF.Copy), r=['cmp_'], w=['rawk'])
                    T.op('act', lambda e: e.activation(out=rawv[:, cs], in_=cmp_[:, 128:256], func=AF.Copy), r=['cmp_'], w=['rawv'])

                W1 = {'k': sbt(e1, 'W1k', [128, 4096], BF16), 'v': sbt(e1, 'W1v', [128, 4096], BF16)}
                W2 = {'k': sbt(e1, 'W2k', [128, 64], BF16), 'v': sbt(e1, 'W2v', [128, 64], BF16)}
                pe_ = {'k': sbt(e1, 'pek', [64, 32], BF16), 'v': sbt(e1, 'pev', [64, 32], BF16)}
                ones64 = sbt(e1, 'ones64', [64, 64])
                T.dma('sp', ones64[:], di['c_ones64'], w=['ones64'], slot='c0')
                for kd in 'kv':
                    for a in range(4):
                        load_cast(W1[kd][:, a * 1024:(a + 1) * 1024], di['w1' + kd][:, a * 1024:(a + 1) * 1024], 1024, 'W1' + kd)
                    load_cast(W2[kd][:], di['w2' + kd], 64, 'W2' + kd)
                    load_cast(pe_[kd][:], di['pe' + kd], 32, 'pe' + kd, parts=64)
                b1p = pst(e1, 'b1p', [128, 512])
                b1s = sbt(e1, 'b1s', [128, 1])
                hid = sbt(e1, 'hid', [128, NCT * 128], BF16)
                sqc = sbt(e1, 'sqc', [64, 512])
                rsc = sbt(e1, 'rsc', [64, 512])
                T.op('pool', lambda e: e.memset(hid[:], 0.0), w=['hid'])
                for g in range(2):
                    T.op('pool', lambda e: e.memset(KcT[g][0:64, :], 0.0), w=[f'KcT{g}'])
                for kd in ('kv' if '2' in PH else ''):
                    raw = rawk if kd == 'k' else rawv
                    for r_ in range(32):
                        T.op('pe', lambda e: e.matmul(b1p[:, 0:1], lhsT=W1[kd][0:64, r_ * 128:(r_ + 1) * 128],
                                                      rhs=pe_[kd][0:64, r_:r_ + 1], start=(r_ == 0), stop=(r_ == 31)),
                             r=['W1' + kd, 'pe' + kd], w=['b1p'])
                    T.op('dve', lambda e: e.tensor_copy(out=b1s[:], in_=b1p[:, 0:1]), r=['b1p'], w=['b1s'])
                    for g in range(2):
                        ps_ = slice(g * 64, (g + 1) * 64)
                        for r_ in range(32):
                            T.op('pe', lambda e: e.matmul(kvp[:, 0:nblk], lhsT=W1[kd][ps_, r_ * 128:(r_ + 1) * 128],
                                                          rhs=raw[ps_, r_:r_ + 16 * (nblk - 1) + 1:16],
                                                          start=(r_ == 0), stop=(r_ == 31)),
                                 r=['W1' + kd, 'rawk', 'rawv'], w=['kvp'])
                        T.op('act', lambda e: e.activation(out=hid[:, 0:nblk], in_=kvp[:, 0:nblk], func=AF.Gelu_apprx_tanh, bias=b1s[:]),
                             r=['kvp', 'b1s'], w=['hid'])
                        if kd == 'k':
                            T.op('pe', lambda e: e.matmul(cmp_[0:64, 0:nblk], lhsT=W2['k'][:, :], rhs=hid[:, 0:nblk], start=True, stop=True),
                                 r=['W2k', 'hid'], w=['cmp_'])
                            T.op('act', lambda e: e.activation(out=sqc[:, 0:nblk], in_=cmp_[0:64, 0:nblk], func=AF.Square), r=['cmp_'], w=['sqc'])
                            T.op('pe', lambda e: e.matmul(b1p[0:64, 0:nblk], lhsT=ones64[:, :], rhs=sqc[:, 0:nblk], start=True, stop=True),
                                 r=['ones64', 'sqc'], w=['b1p'])
                            T.op('act', lambda e: e.activation(out=rsc[:, 0:nblk], in_=b1p[0:64, 0:nblk], func=AF.Sqrt, bias=epsb[0:64, :]),
                                 r=['b1p', 'epsb'], w=['rsc'])
                            T.op('dve', lambda e: e.reciprocal(out=rsc[:, 0:nblk], in_=rsc[:, 0:nblk]), r=['rsc'], w=['rsc'])
                            T.op('dve', lambda e: e.scalar_tensor_tensor(out=KcT[g][0:64, 0:nblk], in0=cmp_[0:64, 0:nblk], scalar=gkt[:, 0:1],
                                                                         in1=rsc[:, 0:nblk], op0=ALU.mult, op1=ALU.mult),
                                 r=['cmp_', 'gkt', 'rsc'], w=[f'KcT{g}'])
                        else:
                            for ct in range(NCT):
                                T.op('pe', lambda e: e.matmul(cmp_[:, 0:64], lhsT=hid[:, ct * 128:(ct + 1) * 128], rhs=W2['v'][:, :],
                                                              start=True, stop=True), r=['W2v', 'hid'], w=['cmp_'])
                                T.op('dve', lambda e: e.tensor_copy(out=Vc[:, (ct * 2 + g) * 65:(ct * 2 + g) * 65 + 64], in_=cmp_[:, 0:64]),
                                     r=['cmp_', 'Vc_init'], w=[f'Vc{g}'])

            if dbg and not os.environ.get('NODUMP'):
                for g in range(2):
                    dk = nc.dram_tensor(f'dbg_kc{g}', [68, NCT * 128], BF16, kind="ExternalOutput").ap()
                    T.dma('sp', dk, KcT[g][:], r=[f'KcT{g}', f'KcTp{g}'], slot=f'dbgk{g}')
                dv = nc.dram_tensor('dbg_vc', [128, NCT * 130], BF16, kind="ExternalOutput").ap()
                T.dma('sp', dv, Vc[:], r=['Vc0', 'Vc1', 'Vc_init'], slot='dbgv')
            T.barrier()
            with ExitStack() as e3:
                Wq = sbt(e3, 'Wq', [128, 8, 536], BF16)
                eall = sbt(e3, 'eall', [128, SEQ], BF16)
                aaug = sbt(e3, 'aaug', [128, NCT * 128], BF16)
                wc = sbt(e3, 'wc', [128, 272], BF16)
                T.dma('sp', eall[:], di['c_eall'], w=['eall'], slot='c0')
                T.dma('sp', aaug[:], di['c_aaug'], w=['aaug'], slot='c1')
                T.dma('sp', wc[:], di['c_wc'], w=['wc'], slot='c2')
                kQ = load_w(Wq, di['w_in'], D, 512, 'Wq', col0=O1, gt=g1t)
                kQ += load_w(Wq[:, :, 512:536], di['w_in'], D, 24, 'Wg', col0=O3, gt=g1t)
                bankQ = pst(e3, 'bankQ', [128, 512])
                bankG = pst(e3, 'bankG', [128, 512])
                Sb = [pst(e3, f'S{k}', [128, 512]) for k in range(3)]
                Ob = [pst(e3, f'O{k}', [128, 512]) for k in range(2)]
                IMP = bankQ
                Pb = [sbt(e3, f'P{k}', [128, 512], BF16) for k in range(4)]
                QTs = [sbt(e3, f'QT{k}', [68, 1024], BF16) for k in range(2)]
                Pc = [sbt(e3, f'Pc{k}', [128, 512], BF16) for k in range(NCT)]
                sqq = sbt(e3, 'sqq', [128, 512])
                ss8 = sbt(e3, 'ss8', [128, 8])
                rq = sbt(e3, 'rq', [128, 8])
                qn = sbt(e3, 'qn', [128, 512], BF16)
                gsig = sbt(e3, 'gsig', [128, 24])
                fbs = [sbt(e3, f'fb{k}', [128, 128]) for k in range(2)]
                den4 = sbt(e3, 'den4', [128, 4])
                impa = sbt(e3, 'impa', [128, 128])
                m1 = sbt(e3, 'm1', [128, 8])
                m2 = sbt(e3, 'm2', [128, 8])
                tmpm = sbt(e3, 'tmpm', [128, 128])
                negms = [sbt(e3, f'negm{k}', [128, 128], BF16) for k in range(2)]
                nmT4 = [sbt(e3, f'nmT4{k}', [128, 512], BF16) for k in range(2)]
                osbs = [sbt(e3, f'osb{k}', [65, 512]) for k in range(4)]
                dd = sbt(e3, 'dd', [128, 4])
                rr = sbt(e3, 'rr', [128, 4])
                ynsa = sbt(e3, 'ynsa', [128, 512])
                ynb = sbt(e3, 'ynb', [128, 512], BF16)
                ynT = [sbt(e3, f'ynT{k}', [128, 512], BF16) for k in range(2)]
                cnt = {'s': 0, 'o': 0, 'c': 0}

                pend = []

                deferred = []

                def defer(n, fn):
                    deferred.append([n, fn])

                def tick():
                    for d in deferred:
                        d[0] -= 1
                    while deferred and deferred[0][0] <= 0:
                        deferred.pop(0)[1]()

                def emit_S(u):
                    k = cnt['s'] % 3
                    kp = cnt['s'] % 4
                    cnt['s'] += 1
                    u['kp'] = kp
                    S, P = Sb[k], Pb[kp]
                    masks = u['masks']
                    T.op('pe', lambda e: e.matmul(S[:], lhsT=u['KT'][0:68, u['lo']:u['lo'] + 128],
                                                  rhs=u['QT'][0:68, u['g'] * 512:(u['g'] + 1) * 512],
                                                  start=True, stop=(len(masks) == 0)), r=u['kkeys'] + [u['qk']], w=[f'S{k}'])
                    for mi, (ml, mr, mk) in enumerate(masks):
                        T.op('pe', lambda e: e.matmul(S[:], lhsT=ml, rhs=mr, start=False, stop=(mi == len(masks) - 1)),
                             r=mk, w=[f'S{k}'])
                    T.op('act', lambda e: e.activation(out=P[:], in_=S[:], func=AF.Exp), r=[f'S{k}'], w=[f'P{kp}'])

                def emit_PV(u):
                    kp = u['kp']
                    P = Pb[kp]
                    O = Ob[u['O']]
                    T.op('pe', lambda e: e.matmul(O[0:65, :], lhsT=u['Vt'][:, u['voff']:u['voff'] + 65], rhs=P[:],
                                                  start=u['first'], stop=u['last']), r=[f'P{kp}'] + u['vkeys'], w=[f"O{u['O']}"])
                    if u.get('imp_ct') is not None:
                        ic = u['imp_ct']
                        T.op('pool', lambda e: e.tensor_copy(out=Pc[ic][:], in_=P[:]), r=[f'P{kp}'], w=[f'Pc{ic}'])
                    if u.get('post') is not None:
                        u['post']()

                P3SKIP = os.environ.get('P3SKIP', '')

                def push(u):
                    if 'u' in P3SKIP:
                        return
                    emit_S(u)
                    if len(pend) >= 2:
                        emit_PV(pend.pop(0))
                    pend.append(u)
                    tick()

                def flush():
                    while pend:
                        emit_PV(pend.pop(0))
                    while deferred:
                        deferred.pop(0)[1]()

                DBG_BR = int(os.environ.get('DBG_BR', '-1'))

                def combine(oi, g, br):
                    if DBG_BR >= 0 and br != DBG_BR:
                        return
                    ko = cnt['c'] % 4
                    cnt['c'] += 1
                    osb = osbs[ko]
                    Oacc = Ob[oi]
                    T.op('act', lambda e: e.activation(out=osb[:], in_=Oacc[0:65, :], func=AF.Copy), r=[f'O{oi}'], w=[f'osb{ko}'])
                    defer(2, lambda: combine2(ko, g, br))

                def combine2(ko, g, br):
                    osb = osbs[ko]
                    for hg in range(4):
                        T.op('pe', lambda e: e.transpose(bankG[:, hg * 65:(hg + 1) * 65], osb[0:65, hg * 128:(hg + 1) * 128], identf[0:65, 0:65]),
                             r=[f'osb{ko}', 'identf'], w=['bankG'])
                    ot = bankG[:, 0:260].rearrange("p (h d) -> p h d", h=4)
                    T.op('dve', lambda e: e.tensor_scalar(out=dd[:].unsqueeze(2), in0=ot[:, :, 64:65], scalar1=1e-30, scalar2=None, op0=ALU.max),
                         r=['bankG'], w=['dd'])
                    T.op('dve', lambda e: e.reciprocal(out=dd[:], in_=dd[:]), r=['dd'], w=['dd'])
                    if DBG_BR >= 0:
                        T.op('dve', lambda e: e.tensor_copy(out=rr[:], in_=dd[:]), r=['dd'], w=['rr'])
                    else:
                        T.op('dve', lambda e: e.tensor_tensor(out=rr[:], in0=dd[:], in1=gsig[:, br * 8 + g * 4:br * 8 + g * 4 + 4], op=ALU.mult),
                             r=['dd', 'gsig'], w=['rr'])
                    for hg in range(4):
                        hh = g * 4 + hg
                        ysl = ynsa[:, hh * 64:(hh + 1) * 64]
                        if br == 0 or DBG_BR >= 0:
                            T.op('dve', lambda e: e.tensor_scalar(out=ysl, in0=bankG[:, hg * 65:hg * 65 + 64], scalar1=rr[:, hg:hg + 1],
                                                                  scalar2=None, op0=ALU.mult), r=['bankG', 'rr'], w=['ynsa'])
                        else:
                            T.op('dve', lambda e: e.scalar_tensor_tensor(out=ysl, in0=bankG[:, hg * 65:hg * 65 + 64], scalar=rr[:, hg:hg + 1],
                                                                         in1=ysl, op0=ALU.mult, op1=ALU.add), r=['bankG', 'rr', 'ynsa'], w=['ynsa'])

                for i in range(NT if '3' in PH else 0):
                    xt, hT, s = rms_hT(i, x_d)
                    QT = QTs[i % 2]
                    qk = f'QT{i % 2}'
                    fb = fbs[i % 2]
                    for c in range(8):
                        T.op('pe', lambda e: e.matmul(bankQ[:], lhsT=hT[:, c * 128:(c + 1) * 128], rhs=Wq[:, c, 0:512],
                                                      start=(c == 0), stop=(c == 7)), r=[f'hT{s}'] + kQ, w=['bankQ'])
                    for c in range(8):
                        T.op('pe', lambda e: e.matmul(bankG[:, 0:24], lhsT=hT[:, c * 128:(c + 1) * 128], rhs=Wq[:, c, 512:536],
                                                      start=(c == 0), stop=(c == 7)), r=[f'hT{s}'] + kQ, w=['bankG'])
                    T.op('act', lambda e: e.activation(out=sqq[:], in_=bankQ[:], func=AF.Square), r=['bankQ'], w=['sqq'])
                    T.op('dve', lambda e: e.tensor_reduce(out=ss8[:], in_=sqq[:].rearrange("p (h d) -> p h d", h=8), axis=AX.X, op=ALU.add),
                         r=['sqq'], w=['ss8'])
                    T.op('act', lambda e: e.activation(out=rq[:], in_=ss8[:], func=AF.Sqrt, bias=epsb[:], scale=1.0 / 64), r=['ss8', 'epsb'], w=['rq'])
                    T.op('dve', lambda e: e.reciprocal(out=rq[:], in_=rq[:]), r=['rq'], w=['rq'])
                    T.op('dve', lambda e: e.tensor_tensor(out=qn[:].rearrange("p (h d) -> p h d", h=8),
                                                          in0=bankQ[:].rearrange("p (h d) -> p h d", h=8),
                                                          in1=rq[:].unsqueeze(2).broadcast_to([128, 8, 64]), op=ALU.mult),
                         r=['bankQ', 'rq'], w=['qn'])
                    for h in range(8):
                        T.op('pe', lambda e: e.transpose(tp[0:64, h * 128:(h + 1) * 128], qn[:, h * 64:(h + 1) * 64], identb[:]),
                             r=['qn', 'identb'], w=['tp'])
                    T.op('act', lambda e: e.activation(out=QT[0:64, :], in_=tp[0:64, :], func=AF.Copy, scale=gq8[:]), r=['tp', 'gq8'], w=[qk])
                    T.dma('sp', QT[64:68, :].rearrange("p (h q) -> p h q", h=8), di['c_qpos'][:, :, i * 128:(i + 1) * 128],
                          w=[qk + 'p'], slot=qk + 'p')
                    qk2 = [qk, qk + 'p']
                    T.op('act', lambda e: e.activation(out=gsig[:], in_=bankG[:, 0:24], func=AF.Sigmoid), r=['bankG'], w=['gsig'])
                    T.dma('sp', fb[:], di['c_fb'][i * 128:(i + 1) * 128, :], w=[f'fb{i % 2}'], slot=f'fb{i % 2}')
                    if dbg and not os.environ.get('NODUMP'):
                        if i == 0:
                            dgs = nc.dram_tensor('dbg_gsig', [NT * 128, 24], F32, kind="ExternalOutput").ap()
                        T.dma('sp', dgs[i * 128:(i + 1) * 128, :], gsig[:], r=['gsig'], slot='dbggs')
                    n_ct = min(NCT, (8 * i + 6) // 128 + 1)

                    def imp_a(g, n_ct=n_ct, fb=fb, i=i):
                        for h in range(4):
                            for ct in range(n_ct):
                                T.op('pe', lambda e: e.matmul(IMP[:, h * 128:(h + 1) * 128], lhsT=Pc[ct][:, h * 128:(h + 1) * 128],
                                                              rhs=aaug[:, ct * 128:(ct + 1) * 128], start=(ct == 0), stop=(ct == n_ct - 1)),
                                     r=[f'Pc{ct}', 'aaug'], w=['bankQ'])
                        iv = IMP[:].rearrange("p (h n) -> p h n", h=4)
                        T.op('dve', lambda e: e.tensor_scalar(out=den4[:].unsqueeze(2), in0=iv[:, :, 127:128], scalar1=1e-30, scalar2=None, op0=ALU.max),
                             r=['bankQ'], w=['den4'])
                        T.op('dve', lambda e: e.reciprocal(out=den4[:], in_=den4[:]), r=['den4'], w=['den4'])
                        T.op('dve', lambda e: e.tensor_scalar(out=impa[:], in0=IMP[:, 0:128], scalar1=den4[:, 0:1], scalar2=None, op0=ALU.mult),
                             r=['bankQ', 'den4'], w=['impa'])
                        for h in range(1, 4):
                            T.op('dve', lambda e: e.scalar_tensor_tensor(out=impa[:], in0=IMP[:, h * 128:(h + 1) * 128], scalar=den4[:, h:h + 1],
                                                                         in1=impa[:], op0=ALU.mult, op1=ALU.add), r=['bankQ', 'den4', 'impa'], w=['impa'])
                        T.op('dve', lambda e: e.tensor_tensor(out=impa[:], in0=impa[:], in1=fb[:], op=ALU.add), r=['impa', f'fb{i % 2}'], w=['impa'])
                        T.op('dve', lambda e: e.max(out=m1[:], in_=impa[:]), r=['impa'], w=['m1'])
                        T.op('dve', lambda e: e.match_replace(out=tmpm[:], in_to_replace=m1[:], in_values=impa[:], imm_value=-3e38),
                             r=['impa', 'm1'], w=['tmpm'])
                        T.op('dve', lambda e: e.max(out=m2[:], in_=tmpm[:]), r=['tmpm'], w=['m2'])
                        T.op('dve', lambda e: e.tensor_scalar(out=negms[g][:], in0=impa[:], scalar1=m2[:, 7:8], scalar2=NEG, op0=ALU.is_lt, op1=ALU.mult),
                             r=['impa', 'm2'], w=[f'negm{g}'])
                        defer(3, lambda: imp_b(g))

                    def imp_b(g):
                        T.op('pe', lambda e: e.transpose(tp[:, 0:128], negms[g][:], identb[:]), r=[f'negm{g}', 'identb'], w=['tp'])
                        T.op('dve', lambda e: e.tensor_copy(out=nmT4[g][:].rearrange("p (h q) -> p h q", h=4),
                                                            in_=tp[:, 0:128].unsqueeze(1).broadcast_to([128, 4, 128])), r=['tp'], w=[f'nmT4{g}'])

                    def post_cmp(oi, g):
                        combine(oi, g, 0)
                        defer(2, lambda: imp_a(g))

                    obank = {}
                    for g in range(2):
                        oi = cnt['o'] % 2
                        cnt['o'] += 1
                        for ct in range(n_ct):
                            off = 128 * ct - 8 * i + 2
                            masks = []
                            if off > -127:
                                masks.append((wc[:, 127 + off:127 + off + 128], i4[:], ['wc', 'i4']))
                            push(dict(KT=KcT[g], QT=QT, g=g, qk=qk, lo=ct * 128, masks=masks, Vt=Vc, voff=(ct * 2 + g) * 65, O=oi,
                                      first=(ct == 0), last=(ct == n_ct - 1), kkeys=[f'KcT{g}', f'KcTp{g}', qk + 'p'],
                                      vkeys=[f'Vc{g}', 'Vc_init'], imp_ct=ct,
                                      post=(lambda oi=oi, g=g: post_cmp(oi, g)) if ct == n_ct - 1 else None))
                        if g == 0:
                            flush()
                    for g in range(2):
                        oi = cnt['o'] % 2
                        cnt['o'] += 1
                        k0 = max(0, i - 4)
                        for kt in range(k0, i + 1):
                            masks = []
                            if kt == i:
                                masks.append((negtri[:], i4[:], ['negtri', 'i4']))
                            if kt == i - 4:
                                masks.append((negtric[:], i4[:], ['negtric', 'i4']))
                            push(dict(KT=KwinT[g], QT=QT, g=g, qk=qk, lo=kt * 128, masks=masks, Vt=Vwin, voff=kt * 130 + g * 65, O=oi,
                                      first=(kt == k0), last=(kt == i), kkeys=[f'K2_{g}_{kt}', f'KwinTp{g}', qk + 'p'], vkeys=[f'Vw_{kt}'],
                                      post=(lambda oi=oi, g=g: combine(oi, g, 2)) if kt == i else None))
                    flush()
                    for g in range(2):
                        oi = cnt['o'] % 2
                        cnt['o'] += 1
                        for kt in range(i + 1):
                            masks = [(eall[:, kt * 128:(kt + 1) * 128], nmT4[g][:], ['eall', f'nmT4{g}'])]
                            if kt == i:
                                masks.append((negtri[:], i4[:], ['negtri', 'i4']))
                            push(dict(KT=KselT[g], QT=QT, g=g, qk=qk, lo=kt * 128, masks=masks, Vt=Vsel, voff=kt * 130 + g * 65, O=oi,
                                      first=(kt == 0), last=(kt == i), kkeys=[f'K1_{g}_{kt}', f'KselTp{g}', qk + 'p'], vkeys=[f'Vs_{kt}'],
                                      post=(lambda oi=oi, g=g: combine(oi, g, 1)) if kt == i else None))
                    flush()
                    yT = ynT[i % 2]
                    T.op('act', lambda e: e.activation(out=ynb[:], in_=ynsa[:], func=AF.Copy), r=['ynsa'], w=['ynb'])
                    for c in range(4):
                        T.op('pe', lambda e: e.transpose(tp[:, c * 128:(c + 1) * 128], ynb[:, c * 128:(c + 1) * 128], identb[:]),
                             r=['ynb', 'identb'], w=['tp'])
                    T.op('dve', lambda e: e.tensor_copy(out=yT[:], in_=tp[:, 0:512]), r=['tp'], w=[f'ynT{i % 2}'])
                    T.dma('sp', yn_d[i], yT[:], r=[f'ynT{i % 2}'], w=[f'yn_{i}'], slot=f'ynT{i % 2}')

        T.barrier()
        with ExitStack() as e4:
            ctab = sbt(e4, 'ctab', [128, 2048])
            stab = sbt(e4, 'stab', [128, 2048])
            BbT = [sbt(e4, f'BbT{k}', [128, 2048], BF16) for k in range(2)]
            Cb = [sbt(e4, f'Cb{k}', [128, 2048], BF16) for k in range(2)]
            mag = sbt(e4, 'mag', [128, 16])
            rotc = sbt(e4, 'rotc', [128, 16])
            rots = sbt(e4, 'rots', [128, 16])
            dsk = sbt(e4, 'dsk', [128, 4])
            bglu = sbt(e4, 'bglu', [128, 4])
            T.dma('sp', dsk[:], di['dsk'], w=['dsk'], slot='c0')
            T.dma('sp', bglu[:], di['bglu'], w=['bglu'], slot='c1')
            ctb = sbt(e4, 'ctb', [128, 2048], BF16)
            stb = sbt(e4, 'stb', [128, 2048], BF16)
            Wu = sbt(e4, 'Wu', [128, 8, 512], BF16)
            Wgl = sbt(e4, 'Wgl', [128, 4, 512], BF16)
            kU = load_w(Wu, di['w_in'], D, 512, 'Wu', col0=0, gt=g1t)
            kGl = load_w(Wgl, di['w_glu'], 512, 512, 'Wgl')

            def sincos(st, phi, n, s_out, c_out, nm):
                u = sbt(st, nm + 'u', [128, n])
                ki = sbt(st, nm + 'ki', [128, n], I32)
                kf = sbt(st, nm + 'kf', [128, n])
                fx = sbt(st, nm + 'fx', [128, n])
                r_ = sbt(st, nm + 'r', [128, n])
                for shift, dst in ((0.0, s_out), (np.pi / 2, c_out)):
                    T.op('dve', lambda e: e.tensor_scalar(out=u[:], in0=phi, scalar1=shift, scalar2=1.0 / (2 * np.pi), op0=ALU.add, op1=ALU.mult),
                         r=[nm + 'phi'], w=[nm + 'u'])
                    T.op('dve', lambda e: e.tensor_copy(out=ki[:], in_=u[:]), r=[nm + 'u'], w=[nm + 'ki'])
                    T.op('dve', lambda e: e.tensor_copy(out=kf[:], in_=ki[:]), r=[nm + 'ki'], w=[nm + 'kf'])
                    T.op('dve', lambda e: e.tensor_scalar(out=r_[:], in0=phi, scalar1=shift, scalar2=None, op0=ALU.add), r=[nm + 'phi'], w=[nm + 'r'])
                    T.op('dve', lambda e: e.scalar_tensor_tensor(out=r_[:], in0=kf[:], scalar=-2 * np.pi, in1=r_[:], op0=ALU.mult, op1=ALU.add),
                         r=[nm + 'kf', nm + 'r'], w=[nm + 'r'])
                    T.op('dve', lambda e: e.tensor_scalar(out=fx[:], in0=r_[:], scalar1=np.pi, scalar2=-2 * np.pi, op0=ALU.is_gt, op1=ALU.mult),
                         r=[nm + 'r'], w=[nm + 'fx'])
                    T.op('dve', lambda e: e.tensor_tensor(out=r_[:], in0=r_[:], in1=fx[:], op=ALU.add), r=[nm + 'r', nm + 'fx'], w=[nm + 'r'])
                    T.op('dve', lambda e: e.tensor_scalar(out=fx[:], in0=r_[:], scalar1=-np.pi, scalar2=2 * np.pi, op0=ALU.is_lt, op1=ALU.mult),
                         r=[nm + 'r'], w=[nm + 'fx'])
                    T.op('dve', lambda e: e.tensor_tensor(out=r_[:], in0=r_[:], in1=fx[:], op=ALU.add), r=[nm + 'r', nm + 'fx'], w=[nm + 'r'])
                    T.op('dve', lambda e: e.tensor_scalar(out=r_[:], in0=r_[:], scalar1=3.14159, scalar2=-3.14159, op0=ALU.min, op1=ALU.max),
                         r=[nm + 'r'], w=[nm + 'r'])
                    T.op('act', lambda e: e.activation(out=dst, in_=r_[:], func=AF.Sin), r=[nm + 'r'], w=[nm + 'out'])

            with ExitStack() as e0:
                sm = lambda n, s=[128, 16]: sbt(e0, n, s)
                are, aim, ldt = sm('are'), sm('aim'), sm('ldt')
                T.dma('sp', are[:], di['are'], w=['are'], slot='c2')
                T.dma('sp', aim[:], di['aim'], w=['aim'], slot='c3')
                T.dma('sp', ldt[:], di['ldt'], w=['ldt'], slot='c4')
                dt_, adr, adi, cs_, sn_ = sm('dt_'), sm('adr'), sm('adi'), sm('cs_'), sm('sn_')
                abr, abi, nr, den, fr, fi, t0, t1 = sm('abr'), sm('abi'), sm('nr'), sm('den'), sm('fr'), sm('fi'), sm('t0'), sm('t1')
                a128 = sm('a128')
                V = lambda fn, r, w: T.op('dve', fn, r=r, w=w)
                T.op('act', lambda e: e.activation(out=dt_[:], in_=ldt[:], func=AF.Exp), r=['ldt'], w=['dt_'])
                V(lambda e: e.tensor_tensor(out=adr[:], in0=are[:], in1=dt_[:], op=ALU.mult), ['are', 'dt_'], ['adr'])
                V(lambda e: e.tensor_tensor(out=adi[:], in0=aim[:], in1=dt_[:], op=ALU.mult), ['aim', 'dt_'], ['s0phi'])
                T.op('act', lambda e: e.activation(out=mag[:], in_=adr[:], func=AF.Exp), r=['adr'], w=['mag'])
                sincos(e0, adi[:], 16, sn_[:], cs_[:], 's0')
                V(lambda e: e.tensor_tensor(out=abr[:], in0=mag[:], in1=cs_[:], op=ALU.mult), ['mag', 's0out'], ['abr'])
                V(lambda e: e.tensor_tensor(out=abi[:], in0=mag[:], in1=sn_[:], op=ALU.mult), ['mag', 's0out'], ['abi'])
                V(lambda e: e.tensor_scalar(out=nr[:], in0=abr[:], scalar1=-1.0, scalar2=None, op0=ALU.add), ['abr'], ['nr'])
                V(lambda e: e.tensor_tensor(out=den[:], in0=are[:], in1=are[:], op=ALU.mult), ['are'], ['den'])
                V(lambda e: e.tensor_tensor(out=t0[:], in0=aim[:], in1=aim[:], op=ALU.mult), ['aim'], ['t0'])
                V(lambda e: e.tensor_tensor(out=den[:], in0=den[:], in1=t0[:], op=ALU.add), ['den', 't0'], ['den'])
                V(lambda e: e.reciprocal(out=den[:], in_=den[:]), ['den'], ['den'])
                V(lambda e: e.tensor_tensor(out=t0[:], in0=nr[:], in1=are[:], op=ALU.mult), ['nr', 'are'], ['t0'])
                V(lambda e: e.tensor_tensor(out=t1[:], in0=abi[:], in1=aim[:], op=ALU.mult), ['abi', 'aim'], ['t1'])
                V(lambda e: e.tensor_tensor(out=t0[:], in0=t0[:], in1=t1[:], op=ALU.add), ['t0', 't1'], ['t0'])
                V(lambda e: e.tensor_tensor(out=fr[:], in0=t0[:], in1=den[:], op=ALU.mult), ['t0', 'den'], ['fr'])
                V(lambda e: e.tensor_tensor(out=t0[:], in0=abi[:], in1=are[:], op=ALU.mult), ['abi', 'are'], ['t0'])
                V(lambda e: e.tensor_tensor(out=t1[:], in0=nr[:], in1=aim[:], op=ALU.mult), ['nr', 'aim'], ['t1'])
                V(lambda e: e.tensor_tensor(out=t0[:], in0=t0[:], in1=t1[:], op=ALU.subtract), ['t0', 't1'], ['t0'])
                V(lambda e: e.tensor_tensor(out=fi[:], in0=t0[:], in1=den[:], op=ALU.mult), ['t0', 'den'], ['fi'])
                V(lambda e: e.tensor_scalar(out=a128[:], in0=adi[:], scalar1=128.0, scalar2=None, op0=ALU.mult), ['s0phi'], ['s1phi'])
                sincos(e0, a128[:], 16, rots[:], rotc[:], 's1')
                tp1 = sbt(e0, 'tp1', [128, 128])
                T.dma('sp', tp1[:], di['c_tp1'], w=['tp1'], slot='c0')
                phi = sbt(e0, 'phi', [128, 2048])
                V(lambda e: e.tensor_tensor(out=phi[:].rearrange("p (j t) -> p j t", j=16),
                                            in0=adi[:].unsqueeze(2).broadcast_to([128, 16, 128]),
                                            in1=tp1[:].unsqueeze(1).broadcast_to([128, 16, 128]), op=ALU.mult), ['s0phi', 'tp1'], ['s2phi'])
                sincos(e0, phi[:], 2048, stab[:], ctab[:], 's2')
                T.op('act', lambda e: e.activation(out=ctb[:], in_=ctab[:], func=AF.Copy), r=['s2out'], w=['tbb'])
                T.op('act', lambda e: e.activation(out=stb[:], in_=stab[:], func=AF.Copy), r=['s2out'], w=['tbb'])
                bre = sbt(e0, 'bre', [128, 2048])
                bim = sbt(e0, 'bim', [128, 2048])
                u1 = sbt(e0, 'u1', [128, 2048])
                u2 = sbt(e0, 'u2', [128, 2048])
                bb = [sbt(e0, f'bb{k}', [128, 2048]) for k in range(2)]
                T.dma('sp', bre[:], di['bre'], w=['bre'], slot='c1')
                T.dma('sp', bim[:], di['bim'], w=['bim'], slot='c2')
                v3 = lambda t_: t_[:].rearrange("p (j c) -> p j c", j=16)
                bc = lambda t_: t_[:].unsqueeze(2).broadcast_to([128, 16, 128])
                V(lambda e: e.tensor_tensor(out=v3(u1), in0=v3(bre), in1=bc(fr), op=ALU.mult), ['bre', 'fr'], ['u1'])
                V(lambda e: e.tensor_tensor(out=v3(u2), in0=v3(bim), in1=bc(fi), op=ALU.mult), ['bim', 'fi'], ['u2'])
                V(lambda e: e.tensor_tensor(out=bb[0][:], in0=u1[:], in1=u2[:], op=ALU.subtract), ['u1', 'u2'], ['bb0'])
                V(lambda e: e.tensor_tensor(out=v3(u1), in0=v3(bim), in1=bc(fr), op=ALU.mult), ['bim', 'fr', 'bb0'], ['u1'])
                V(lambda e: e.tensor_tensor(out=v3(u2), in0=v3(bre), in1=bc(fi), op=ALU.mult), ['bre', 'fi', 'bb0'], ['u2'])
                V(lambda e: e.tensor_tensor(out=bb[1][:], in0=u1[:], in1=u2[:], op=ALU.add), ['u1', 'u2'], ['bb1'])
                trp = pst(e0, 'trp', [128, 512])
                for k in range(2):
                    for jg in range(4):
                        for jj in range(4):
                            j = jg * 4 + jj
                            T.op('pe', lambda e: e.transpose(trp[:, jj * 128:(jj + 1) * 128], bb[k][:, j * 128:(j + 1) * 128], identf[:]),
                                 r=[f'bb{k}', 'identf'], w=['trp'])
                        T.op('dve', lambda e: e.tensor_copy(out=BbT[k][:, jg * 512:(jg + 1) * 512], in_=trp[:]), r=['trp'], w=[f'BbT{k}'])
                T.dma('sp', bre[:], di['cre'], r=['u1', 'u2'], w=['bre'], slot='c1')
                T.dma('sp', bim[:], di['cim'], r=['u1', 'u2'], w=['bim'], slot='c2')
                T.op('act', lambda e: e.activation(out=Cb[0][:], in_=bre[:], func=AF.Copy), r=['bre'], w=['Cb0'])
                T.op('act', lambda e: e.activation(out=Cb[1][:], in_=bim[:], func=AF.Copy, scale=-1.0), r=['bim'], w=['Cb1'])

            T.barrier()
            with ExitStack() as e5:
                bankU = pst(e5, 'bankU', [128, 512])
                bu = [[pst(e5, f'bu{a}{b}', [128, 512]) for b in range(2)] for a in range(2)]
                yTp = pst(e5, 'yTp', [128, 512])
                zp = pst(e5, 'zp', [128, 512])
                ubs = [sbt(e5, f'ub{b}', [128, 512], BF16) for b in range(2)]
                nm12 = ('t1_', 't2_', 't3_', 't4_', 'vr', 'vi', 'wr', 'wi')
                f4b = {n: [sbt(e5, f'{n}{b}', [128, 512]) for b in range(2)] for n in nm12}
                hb4 = {n: [sbt(e5, f'{n}{b}', [128, 512], BF16) for b in range(2)] for n in ('wrb', 'wib', 'a1b', 'a2b', 'a3b', 'a4b')}
                xrb = [sbt(e5, f'xr{b}', [128, 512], BF16) for b in range(2)]
                xib = [sbt(e5, f'xi{b}', [128, 512], BF16) for b in range(2)]
                w0 = [[sbt(e5, f'w0{p}{k}', [128, 16]) for k in range(2)] for p in range(2)]
                q1b = [sbt(e5, f'q1{b}', [128, 4]) for b in range(2)]
                q2b = [sbt(e5, f'q2{b}', [128, 4]) for b in range(2)]
                ysbs = [sbt(e5, f'ysb{b}', [128, 512]) for b in range(2)]
                ygbs = [sbt(e5, f'ygb{b}', [128, 512], BF16) for b in range(2)]
                sgbs = [sbt(e5, f'sgb{b}', [128, 512]) for b in range(2)]
                yso = [sbt(e5, f'yso{k}', [128, 512], BF16) for k in range(2)]
                for k in range(2):
                    T.op('dve', lambda e: e.memset(w0[0][k][:], 0.0), w=[f'w00{k}'])
                NTS = NT if 'S' in PH else 0

                def s_pre(i):
                    xt, hT, s = rms_hT(i, x_d)
                    p = i % 2
                    ub = ubs[p]
                    for sg in range(4):
                        for c in range(8):
                            T.op('pe', lambda e: e.matmul(bankU[:, sg * 128:(sg + 1) * 128], lhsT=Wu[:, c, sg * 128:(sg + 1) * 128],
                                                          rhs=hT[:, c * 128:(c + 1) * 128], start=(c == 0), stop=(c == 7)),
                                 r=[f'hT{s}'] + kU, w=['bankU'])
                    T.op('act', lambda e: e.activation(out=ub[:], in_=bankU[:], func=AF.Copy), r=['bankU'], w=['ub' + str(p)])

                def s_T(n):
                    i, sg = divmod(n, 4)
                    p, B_ = i % 2, n % 2
                    ub = ubs[p]
                    t1_, t2_, t3_, t4_, vr, vi, wr, wi = [f4b[x][B_] for x in nm12]
                    bur, bui = bu[B_]
                    kr, ki_ = f'bu{B_}0', f'bu{B_}1'
                    cs = ctab[:, sg * 512:(sg + 1) * 512]
                    ss = stab[:, sg * 512:(sg + 1) * 512]
                    for jj in range(4):
                        j = sg * 4 + jj
                        for k, (bt, kk) in enumerate(((bur, kr), (bui, ki_))):
                            T.op('pe', lambda e: e.matmul(bt[:, jj * 128:(jj + 1) * 128], lhsT=BbT[k][:, j * 128:(j + 1) * 128],
                                                          rhs=ub[:, sg * 128:(sg + 1) * 128], start=True, stop=True),
                                 r=['ub' + str(p), f'BbT{k}'], w=[kk])
                    T.op('dve', lambda e: e.tensor_tensor(out=t1_[:], in0=bur[:], in1=cs, op=ALU.mult), r=[kr, 's2out'], w=['t1_' + str(B_)])
                    T.op('dve', lambda e: e.tensor_tensor(out=t2_[:], in0=bui[:], in1=ss, op=ALU.mult), r=[ki_, 's2out'], w=['t2_' + str(B_)])
                    T.op('pool', lambda e: e.tensor_tensor(out=vr[:], in0=t1_[:], in1=t2_[:], op=ALU.add), r=['t1_' + str(B_), 't2_' + str(B_)], w=['vr' + str(B_)])
                    T.op('dve', lambda e: e.tensor_tensor(out=t3_[:], in0=bui[:], in1=cs, op=ALU.mult), r=[ki_, 's2out'], w=['t3_' + str(B_)])
                    T.op('dve', lambda e: e.tensor_tensor(out=t4_[:], in0=bur[:], in1=ss, op=ALU.mult), r=[kr, 's2out'], w=['t4_' + str(B_)])
                    T.op('pool', lambda e: e.tensor_tensor(out=vi[:], in0=t3_[:], in1=t4_[:], op=ALU.subtract), r=['t3_' + str(B_), 't4_' + str(B_)], w=['vi' + str(B_)])

                def s_SC(n):
                    i, sg = divmod(n, 4)
                    p, pn, B_ = i % 2, (i + 1) % 2, n % 2
                    t1_, t2_, t3_, t4_, vr, vi, wr, wi = [f4b[x][B_] for x in nm12]
                    wrb, wib = hb4['wrb'][B_], hb4['wib'][B_]
                    q1, q2 = q1b[B_], q2b[B_]
                    for jj in range(4):
                        j = sg * 4 + jj
                        sl = slice(jj * 128, (jj + 1) * 128)
                        for k, (vv, ww, nm) in enumerate(((vr, wr, 'wr' + str(B_)), (vi, wi, 'wi' + str(B_)))):
                            T.op('dve', lambda e: e.tensor_tensor_scan(out=ww[:, sl], data0=mag[:, j:j + 1].broadcast_to([128, 128]),
                                                                       data1=vv[:, sl], initial=w0[p][k][:, j:j + 1],
                                                                       op0=ALU.mult, op1=ALU.add),
                                 r=['mag', 'vr' + str(B_) if k == 0 else 'vi' + str(B_), f'w0{p}{k}'], w=[nm])
                    T.op('act', lambda e: e.activation(out=wrb[:], in_=wr[:], func=AF.Copy), r=['wr' + str(B_)], w=['wrb' + str(B_)])
                    T.op('act', lambda e: e.activation(out=wib[:], in_=wi[:], func=AF.Copy), r=['wi' + str(B_)], w=['wib' + str(B_)])
                    wl_r = wr[:].rearrange("p (j t) -> p j t", j=4)[:, :, 127:128]
                    wl_i = wi[:].rearrange("p (j t) -> p j t", j=4)[:, :, 127:128]
                    rc = rotc[:, sg * 4:(sg + 1) * 4].unsqueeze(2)
                    rs = rots[:, sg * 4:(sg + 1) * 4].unsqueeze(2)
                    n_r = w0[pn][0][:, sg * 4:(sg + 1) * 4].unsqueeze(2)
                    n_i = w0[pn][1][:, sg * 4:(sg + 1) * 4].unsqueeze(2)
                    T.op('dve', lambda e: e.tensor_tensor(out=q1[:].unsqueeze(2), in0=wl_r, in1=rc, op=ALU.mult), r=['wr' + str(B_), 's1out'], w=['q1' + str(B_)])
                    T.op('dve', lambda e: e.tensor_tensor(out=q2[:].unsqueeze(2), in0=wl_i, in1=rs, op=ALU.mult), r=['wi' + str(B_), 's1out'], w=['q2' + str(B_)])
                    T.op('dve', lambda e: e.tensor_tensor(out=n_r, in0=q1[:].unsqueeze(2), in1=q2[:].unsqueeze(2), op=ALU.subtract),
                         r=['q1' + str(B_), 'q2' + str(B_)], w=[f'w0{pn}0'])
                    T.op('dve', lambda e: e.tensor_tensor(out=q1[:].unsqueeze(2), in0=wl_i, in1=rc, op=ALU.mult), r=['wi' + str(B_), 's1out', f'w0{pn}0'], w=['q1' + str(B_)])
                    T.op('dve', lambda e: e.tensor_tensor(out=q2[:].unsqueeze(2), in0=wl_r, in1=rs, op=ALU.mult), r=['wr' + str(B_), 's1out', f'w0{pn}0'], w=['q2' + str(B_)])
                    T.op('dve', lambda e: e.tensor_tensor(out=n_i, in0=q1[:].unsqueeze(2), in1=q2[:].unsqueeze(2), op=ALU.add),
                         r=['q1' + str(B_), 'q2' + str(B_)], w=[f'w0{pn}1'])

                def s_A(n):
                    i, sg = divmod(n, 4)
                    B_ = n % 2
                    wrb, wib, a1, a2, a3, a4 = [hb4[x][B_] for x in ('wrb', 'wib', 'a1b', 'a2b', 'a3b', 'a4b')]
                    xr, xi = xrb[B_], xib[B_]
                    csb = ctb[:, sg * 512:(sg + 1) * 512]
                    ssb = stb[:, sg * 512:(sg + 1) * 512]
                    T.op('dve', lambda e: e.tensor_tensor(out=a1[:], in0=wrb[:], in1=csb, op=ALU.mult), r=['wrb' + str(B_), 'tbb'], w=['a1' + str(B_)])
                    T.op('dve', lambda e: e.tensor_tensor(out=a2[:], in0=wib[:], in1=ssb, op=ALU.mult), r=['wib' + str(B_), 'tbb'], w=['a2' + str(B_)])
                    T.op('dve', lambda e: e.tensor_tensor(out=xr[:], in0=a1[:], in1=a2[:], op=ALU.subtract), r=['a1' + str(B_), 'a2' + str(B_)], w=['xr' + str(B_)])
                    T.op('dve', lambda e: e.tensor_tensor(out=a3[:], in0=wib[:], in1=csb, op=ALU.mult), r=['wib' + str(B_), 'tbb'], w=['a3' + str(B_)])
                    T.op('dve', lambda e: e.tensor_tensor(out=a4[:], in0=wrb[:], in1=ssb, op=ALU.mult), r=['wrb' + str(B_), 'tbb'], w=['a4' + str(B_)])
                    T.op('dve', lambda e: e.tensor_tensor(out=xi[:], in0=a3[:], in1=a4[:], op=ALU.add), r=['a3' + str(B_), 'a4' + str(B_)], w=['xi' + str(B_)])
                    for jj in range(4):
                        j = sg * 4 + jj
                        sl = slice(jj * 128, (jj + 1) * 128)
                        T.op('pe', lambda e: e.matmul(yTp[:, sg * 128:(sg + 1) * 128], lhsT=Cb[0][:, j * 128:(j + 1) * 128], rhs=xr[:, sl],
                                                      start=(jj == 0), stop=False), r=['Cb0', 'xr' + str(B_)], w=['yTp'])
                        T.op('pe', lambda e: e.matmul(yTp[:, sg * 128:(sg + 1) * 128], lhsT=Cb[1][:, j * 128:(j + 1) * 128], rhs=xi[:, sl],
                                                      start=False, stop=(jj == 3)), r=['Cb1', 'xi' + str(B_)], w=['yTp'])

                def s_tail(i):
                    p = i % 2
                    ub, ysb, ygb, sgb = ubs[p], ysbs[p], ygbs[p], sgbs[p]
                    for sg in range(4):
                        sl = slice(sg * 128, (sg + 1) * 128)
                        T.op('dve', lambda e: e.scalar_tensor_tensor(out=ysb[:, sl], in0=ub[:, sl], scalar=dsk[:, sg:sg + 1], in1=yTp[:, sl],
                                                                     op0=ALU.mult, op1=ALU.add), r=['ub' + str(p), 'dsk', 'yTp'], w=['ysb' + str(p)])
                    T.op('act', lambda e: e.activation(out=ygb[:], in_=ysb[:], func=AF.Gelu_apprx_tanh), r=['ysb' + str(p)], w=['ygb' + str(p)])
                    for co in range(4):
                        for ci in range(4):
                            T.op('pe', lambda e: e.matmul(zp[:, co * 128:(co + 1) * 128], lhsT=Wgl[:, ci, co * 128:(co + 1) * 128],
                                                          rhs=ygb[:, ci * 128:(ci + 1) * 128], start=(ci == 0), stop=(ci == 3)),
                                 r=['ygb' + str(p)] + kGl, w=['zp'])
                    for co in range(4):
                        sl = slice(co * 128, (co + 1) * 128)
                        T.op('act', lambda e: e.activation(out=sgb[:, sl], in_=zp[:, sl], func=AF.Sigmoid, bias=bglu[:, co:co + 1]),
                             r=['zp', 'bglu'], w=['sgb' + str(p)])
                    yo = yso[i % 2]
                    T.op('pool', lambda e: e.tensor_tensor(out=yo[:], in0=ygb[:], in1=sgb[:], op=ALU.mult), r=['ygb' + str(p), 'sgb' + str(p)], w=[f'yso{i % 2}'])
                    T.dma('sp', ys_d[i], yo[:], r=[f'yso{i % 2}'], w=[f'ys_{i}'], slot=f'yso{i % 2}')

                NG = 4 * NTS
                if NTS > 0:
                    s_pre(0)
                    s_T(0)
                for n in range(NG):
                    if n + 1 < NG:
                        if (n + 1) % 4 == 0:
                            s_pre((n + 1) // 4)
                        s_T(n + 1)
                    s_SC(n)
                    if n >= 1:
                        s_A(n - 1)
                        if n % 4 == 0:
                            s_tail(n // 4 - 1)
                if NTS > 0:
                    s_A(NG - 1)
                    s_tail(NTS - 1)

        T.barrier()
        with ExitStack() as e6:
            Wmg = sbt(e6, 'Wmg', [128, 8, 2048], BF16)
            Wps = sbt(e6, 'Wps', [128, 4, D], BF16)
            Wpn = sbt(e6, 'Wpn', [128, 4, D], BF16)
            Wo = sbt(e6, 'Wo', [128, 8, D], BF16)
            kMg = load_w(Wmg, di['w_in'], D, 2048, 'Wmg', col0=O4, gt=g1t)
            kPs = load_w(Wps, di['w_ps'], 512, D, 'Wps')
            kPn = load_w(Wpn, di['w_pn'], 512, D, 'Wpn')
            kWo = load_w(Wo, di['w_out'], D, D, 'Wo')
            Gp = [pst(e6, f'Gp{k}', [128, 512]) for k in range(2)]
            Pp = [pst(e6, f'Pp{k}', [128, 512]) for k in range(2)]
            Op = [pst(e6, f'Op{k}', [128, 512]) for k in range(2)]
            gs = [sbt(e6, f'gs{k}', [128, 512]) for k in range(2)]
            mt = [sbt(e6, f'mt{k}', [128, 512]) for k in range(2)]
            mrgs = [sbt(e6, f'mrg{k}', [128, D], BF16) for k in range(2)]
            mT = sbt(e6, 'mT', [128, D], BF16)
            ysl_ = [sbt(e6, f'ysl{k}', [128, 512], BF16) for k in range(2)]
            ynl_ = [sbt(e6, f'ynl{k}', [128, 512], BF16) for k in range(2)]
            x1s = [sbt(e6, f'x1s{k}', [128, D]) for k in range(2)]
            NT4 = NT if '4' in PH else 0
            st4 = {}

            def front4(i):
                xt, hT, s = rms_hT(i, x_d)
                st4[i] = (xt, s)
                q = i % 2
                T.dma('sp', ysl_[q][:], ys_d[i], r=[f'ys_{i}'], w=[f'ysl{q}'], slot=f'ysl{q}')
                T.dma('sp', ynl_[q][:], yn_d[i], r=[f'yn_{i}'], w=[f'ynl{q}'], slot=f'ynl{q}')
                for hf in range(2):
                    hs = slice(hf * 512, (hf + 1) * 512)
                    for m_ in range(2):
                        for c in range(8):
                            T.op('pe', lambda e: e.matmul(Gp[m_][:], lhsT=hT[:, c * 128:(c + 1) * 128],
                                                          rhs=Wmg[:, c, m_ * 1024 + hf * 512:m_ * 1024 + (hf + 1) * 512],
                                                          start=(c == 0), stop=(c == 7)), r=[f'hT{s}'] + kMg, w=[f'Gp{m_}'])
                        T.op('act', lambda e: e.activation(out=gs[m_][:], in_=Gp[m_][:], func=AF.Sigmoid), r=[f'Gp{m_}'], w=[f'gs{m_}'])
                        yl, Wp, kk, kn_ = ((ysl_[q], Wps, kPs, f'ysl{q}'), (ynl_[q], Wpn, kPn, f'ynl{q}'))[m_]
                        for ci in range(4):
                            T.op('pe', lambda e: e.matmul(Pp[m_][:], lhsT=yl[:, ci * 128:(ci + 1) * 128], rhs=Wp[:, ci, hs],
                                                          start=(ci == 0), stop=(ci == 3)), r=[kn_] + kk, w=[f'Pp{m_}'])
                        T.op('dve', lambda e: e.tensor_tensor(out=mt[m_][:], in0=Pp[m_][:], in1=gs[m_][:], op=ALU.mult),
                             r=[f'Pp{m_}', f'gs{m_}'], w=[f'mt{m_}'])
                    T.op('pool', lambda e: e.tensor_tensor(out=mrgs[q][:, hs], in0=mt[0][:], in1=mt[1][:], op=ALU.add), r=['mt0', 'mt1'], w=[f'mrg{q}'])

            def back4(i):
                xt, s = st4.pop(i)
                q = i % 2
                mrg = mrgs[q]
                for c in range(8):
                    T.op('pe', lambda e: e.transpose(tp[:, c * 128:(c + 1) * 128], mrg[:, c * 128:(c + 1) * 128], identb[:]),
                         r=[f'mrg{q}', 'identb'], w=['tp'])
                T.op('dve', lambda e: e.tensor_copy(out=mT[:], in_=tp[:]), r=['tp'], w=['mT'])
                for hf in range(2):
                    hs = slice(hf * 512, (hf + 1) * 512)
                    for c in range(8):
                        T.op('pe', lambda e: e.matmul(Op[hf][:], lhsT=mT[:, c * 128:(c + 1) * 128], rhs=Wo[:, c, hs],
                                                      start=(c == 0), stop=(c == 7)), r=['mT'] + kWo, w=[f'Op{hf}'])
                    T.op('dve', lambda e: e.tensor_tensor(out=x1s[q][:, hs], in0=Op[hf][:], in1=xt[:, hs], op=ALU.add),
                         r=[f'Op{hf}', f'xt{s}'], w=[f'x1s{q}'])
                T.dma('sp', x1_d[i * 128:(i + 1) * 128, :], x1s[q][:], r=[f'x1s{q}'], w=[f'x1_{i}', f'ys_{i}', f'yn_{i}'], slot=f'x1s{q}')

            if NT4 > 0:
                front4(0)
            for i in range(NT4):
                if i + 1 < NT4:
                    front4(i + 1)
                back4(i)

        T.barrier()
        with ExitStack() as e7:
            Wup = sbt(e7, 'Wup', [128, 8, 4096], BF16)
            Wdn = sbt(e7, 'Wdn', [128, 32, D], BF16)
            kUp = load_w(Wup, di['w_up'], D, 4096, 'Wup', gt=g2t)
            kDn = load_w(Wdn, di['w_down'], 4096, D, 'Wdn')
            Up = [pst(e7, f'Up{k}', [128, 512]) for k in range(2)]
            Dn = [pst(e7, f'Dn{k}', [128, 512]) for k in range(2)]
            rl = [sbt(e7, f'rl{k}', [128, 512]) for k in range(2)]
            acT = [sbt(e7, f'acT{k}', [128, 512], BF16) for k in range(2)]
            os_ = [sbt(e7, f'os{k}', [128, D]) for k in range(2)]
            NT5 = NT if '5' in PH else 0
            nxt5 = rms_hT(0, x1_d, keysrc=['x1_0']) if NT5 > 0 else None
            for i in range(NT5):
                xt, hT, s = nxt5
                q = i % 2
                def down(fg):
                    u = fg % 2
                    for q4 in range(4):
                        fc = fg * 4 + q4
                        for hf in range(2):
                            T.op('pe', lambda e: e.matmul(Dn[hf][:], lhsT=acT[u][:, q4 * 128:(q4 + 1) * 128],
                                                          rhs=Wdn[:, fc, hf * 512:(hf + 1) * 512], start=(fc == 0), stop=(fc == 31)),
                                 r=[f'acT{u}'] + kDn, w=[f'Dn{hf}'])

                for fg in range(8):
                    u = fg % 2
                    for q4 in range(4):
                        fc = fg * 4 + q4
                        for c in range(8):
                            T.op('pe', lambda e: e.matmul(Up[u][:, q4 * 128:(q4 + 1) * 128], lhsT=Wup[:, c, fc * 128:(fc + 1) * 128],
                                                          rhs=hT[:, c * 128:(c + 1) * 128], start=(c == 0), stop=(c == 7)),
                                 r=[f'hT{s}'] + kUp, w=[f'Up{u}'])
                    T.op('act', lambda e: e.activation(out=rl[u][:], in_=Up[u][:], func=AF.Relu), r=[f'Up{u}'], w=[f'rl{u}'])
                    T.op('dve', lambda e: e.tensor_tensor(out=acT[u][:], in0=rl[u][:], in1=rl[u][:], op=ALU.mult), r=[f'rl{u}'], w=[f'acT{u}'])
                    if fg > 0:
                        down(fg - 1)
                    if fg == 4 and i + 1 < NT5:
                        nxt5 = rms_hT(i + 1, x1_d, keysrc=[f'x1_{i + 1}'])
                down(7)
                for hf in range(2):
                    hs = slice(hf * 512, (hf + 1) * 512)
                    T.op('dve', lambda e: e.tensor_tensor(out=os_[q][:, hs], in0=Dn[hf][:], in1=xt[:, hs], op=ALU.add),
                         r=[f'Dn{hf}', f'xt{s}'], w=[f'os{q}'])
                T.dma('sp', out_d[i * 128:(i + 1) * 128, :], os_[q][:], r=[f'os{q}'], w=[f'out_{i}', f'x1_{i}'], slot=f'os{q}')
        T.finish()
        print("instructions:", T.nins)
    return nc, hc


_CACHE = {}


def run(inputs, SEQ, n_cores, dbg=False):
    key = (SEQ, dbg)
    if key not in _CACHE:
        _CACHE[key] = build(SEQ, dbg)
    nc, hc = _CACHE[key]
    L = host_layouts(inputs)
    x = np.asarray(inputs['x'], dtype=np.float32)
    B = x.shape[0]
    in_maps = []
    for core in range(n_cores):
        m = {'x': np.ascontiguousarray(x[core % B])}
        for k, v in hc.items():
            m['c_' + k] = v
        m.update(L)
        in_maps.append(m)
    res = run_bass_kernel_spmd(nc, in_maps, core_ids=list(range(n_cores)))
    return res


def kernel(**inputs):
    x = np.asarray(inputs['x'])
    B, SEQ, _ = x.shape
    res = run(inputs, SEQ, B)
    return np.stack([np.asarray(res.results[b]['out'], dtype=np.float32) for b in range(B)], 0)
```
